# Optimizing a Trainium2 kernel written in Bass

```python
import jax, jax.numpy as jnp
from jax import lax
import numpy as np

D_MODEL = 1024
BATCH = 4
SEQ = 8192
DEPTH = 4

N_EVEN = (DEPTH + 1) // 2
N_ODD = DEPTH // 2

MLA_HEADS = 8
MLA_Q_RANK = 384
MLA_KV_RANK = 256
MLA_NOPE = 64
MLA_ROPE = 32
MLA_V = 64
MLA_QK = MLA_NOPE + MLA_ROPE
Q_BLOCK = 128

RET_HEADS = 8
RET_DK = 64
RET_DV = 64
RET_CHUNK = 128
RET_DECAY_BASE = 5.0

GLA_HEADS = 4
GLA_DK = 128
GLA_DV = 256
GLA_GATE_RANK = 16
GLA_TAU = 16.0
GLA_CHUNK = 64

D_FF = 2816
CONV_W = 3

ROPE_THETA = 10000.0
EPS = 1e-6

EVEN_SPLIT = (MLA_Q_RANK, MLA_KV_RANK, MLA_ROPE,
              RET_HEADS * RET_DK, RET_HEADS * RET_DK, RET_HEADS * RET_DV, RET_HEADS * RET_DV)
EVEN_IN = MLA_Q_RANK + MLA_KV_RANK + MLA_ROPE + 2 * RET_HEADS * RET_DK + 2 * RET_HEADS * RET_DV
EVEN_OUT = MLA_HEADS * MLA_V + RET_HEADS * RET_DV
ODD_SPLIT = (GLA_HEADS * GLA_DK, GLA_HEADS * GLA_DK, GLA_HEADS * GLA_DV, GLA_HEADS * GLA_DV,
             GLA_GATE_RANK, GLA_GATE_RANK)
ODD_IN = 2 * GLA_HEADS * GLA_DK + 2 * GLA_HEADS * GLA_DV + 2 * GLA_GATE_RANK
ODD_OUT = GLA_HEADS * GLA_DV

kernel_name = "hybrid_mla_retention_gla_convffn_encoder"


def _split(p, sizes):
    outs, s = [], 0
    for n in sizes:
        outs.append(p[..., s:s + n])
        s += n
    return outs


def _rmsnorm(x, g):
    xf = x.astype(jnp.float32)
    y = xf * lax.rsqrt(jnp.mean(xf * xf, axis=-1, keepdims=True) + EPS)
    return (y * g.astype(jnp.float32)).astype(x.dtype)


def _rope(x, positions):
    half = x.shape[-1] // 2
    inv = ROPE_THETA ** (-jnp.arange(half, dtype=jnp.float32) / half)
    ang = positions.astype(jnp.float32)[:, :, None] * inv
    cos = jnp.cos(ang)[:, :, None, :]
    sin = jnp.sin(ang)[:, :, None, :]
    x1 = x[..., :half].astype(jnp.float32)
    x2 = x[..., half:].astype(jnp.float32)
    return jnp.concatenate([x1 * cos - x2 * sin, x2 * cos + x1 * sin], axis=-1).astype(x.dtype)


def _mla(cq, ckv, k_rope, positions, q_norm, kv_norm, w_uq, w_ukv, q_head_norm, k_head_norm):
    B, S, _ = cq.shape
    H = MLA_HEADS
    q = (_rmsnorm(cq, q_norm) @ w_uq).reshape(B, S, H, MLA_QK)
    kv = (_rmsnorm(ckv, kv_norm) @ w_ukv).reshape(B, S, H, MLA_NOPE + MLA_V)
    k_nope, v = kv[..., :MLA_NOPE], kv[..., MLA_NOPE:]
    k = jnp.concatenate([k_nope, jnp.broadcast_to(k_rope[:, :, None, :], (B, S, H, MLA_ROPE))], axis=-1)
    q = _rmsnorm(q, q_head_norm)
    k = _rmsnorm(k, k_head_norm)
    q = jnp.concatenate([q[..., :MLA_NOPE], _rope(q[..., MLA_NOPE:], positions)], axis=-1)
    k = jnp.concatenate([k[..., :MLA_NOPE], _rope(k[..., MLA_NOPE:], positions)], axis=-1)
    scale = MLA_QK ** -0.5
    nb = S // Q_BLOCK
    qb = q.reshape(B, nb, Q_BLOCK, H, MLA_QK).transpose(1, 0, 2, 3, 4)

    def block(qi):
        s = jnp.einsum('bqhd,bkhd->bhqk', qi, k).astype(jnp.float32) * scale
        p = jax.nn.softmax(s, axis=-1).astype(v.dtype)
        return jnp.einsum('bhqk,bkhe->bqhe', p, v)

    o = lax.map(block, qb)
    return o.transpose(1, 0, 2, 3, 4).reshape(B, S, H * MLA_V)


def _retention_dir(q, k, v, log_gamma, include_diag):
    B, S, H, dk = q.shape
    dv = v.shape[-1]
    C = RET_CHUNK
    n = S // C
    q = q.reshape(B, n, C, H, dk)
    k = k.reshape(B, n, C, H, dk)
    v = v.reshape(B, n, C, H, dv)
    lg = log_gamma.astype(jnp.float32)
    pos = jnp.arange(C, dtype=jnp.float32)
    rel = pos[:, None] - pos[None, :]
    mask = (rel >= 0) if include_diag else (rel > 0)
    decay = jnp.where(mask[None], jnp.exp(lg[:, None, None] * jnp.maximum(rel, 0.0)[None]), 0.0)
    scores = jnp.einsum('bnihd,bnjhd->bnhij', q, k) * decay
    intra = jnp.einsum('bnhij,bnjhe->bnihe', scores, v)
    zeta = jnp.exp(lg[:, None] * (C - 1 - pos)[None])
    chunk_state = jnp.einsum('bnjhd,bnjhe,hj->bnhde', k, v, zeta)
    chunk_decay = jnp.exp(lg * C)[None, :, None, None]

    def step(R, s):
        return chunk_decay * R + s, R

    R0 = jnp.zeros((B, H, dk, dv), chunk_state.dtype)
    _, R_prev = lax.scan(step, R0, jnp.moveaxis(chunk_state, 1, 0))
    R_prev = jnp.moveaxis(R_prev, 0, 1)
    xi = jnp.exp(lg[:, None] * (pos + 1.0)[None]).T
    cross = jnp.einsum('bnihd,bnhde->bnihe', q, R_prev) * xi[None, None, :, :, None]
    return (intra + cross).reshape(B, S, H, dv)


def _retention(rq, rk, rv, rg, positions, theta_fwd, theta_bwd, out_norm):
    B, S, _ = rq.shape
    H = RET_HEADS
    q = _rope(rq.reshape(B, S, H, RET_DK), positions)
    k = _rope(rk.reshape(B, S, H, RET_DK), positions) * (RET_DK ** -0.5)
    v = rv.reshape(B, S, H, RET_DV)
    lg_f = jnp.log1p(-jnp.exp2(-theta_fwd.astype(jnp.float32)))
    lg_b = jnp.log1p(-jnp.exp2(-theta_bwd.astype(jnp.float32)))
    o_f = _retention_dir(q, k, v, lg_f, True)
    o_b = jnp.flip(_retention_dir(jnp.flip(q, 1), jnp.flip(k, 1), jnp.flip(v, 1), lg_b, False), 1)
    o = _rmsnorm(o_f + o_b, out_norm)
    return jax.nn.silu(rg) * o.reshape(B, S, H * RET_DV)


def _gla_dir(q, k, v, log_a, include_diag):
    B, S, H, dk = q.shape
    dv = v.shape[-1]
    C = GLA_CHUNK
    n = S // C
    q = q.reshape(B, n, C, H, dk)
    k = k.reshape(B, n, C, H, dk)
    v = v.reshape(B, n, C, H, dv)
    b = jnp.cumsum(log_a.astype(jnp.float32).reshape(B, n, C, H, dk), axis=2)
    b_mid = b[:, :, C // 2:C // 2 + 1]
    b_last = b[:, :, -1]
    qc = q * jnp.exp(b - b_mid)
    kc = k * jnp.exp(b_mid - b)
    A = jnp.einsum('bnihd,bnjhd->bnhij', qc, kc)
    pos = jnp.arange(C)
    mask = (pos[:, None] >= pos[None, :]) if include_diag else (pos[:, None] > pos[None, :])
    A = jnp.where(mask, A, 0.0)
    intra = jnp.einsum('bnhij,bnjhe->bnihe', A, v)
    k_dec = k * jnp.exp(b_last[:, :, None] - b)
    chunk_state = jnp.einsum('bnjhd,bnjhe->bnhde', k_dec, v)
    chunk_decay = jnp.exp(b_last)

    def step(Sm, xs):
        dcy, st = xs
        return dcy[..., None] * Sm + st, Sm

    S0 = jnp.zeros((B, H, dk, dv), chunk_state.dtype)
    _, S_prev = lax.scan(step, S0, (jnp.moveaxis(chunk_decay, 1, 0), jnp.moveaxis(chunk_state, 1, 0)))
    S_prev = jnp.moveaxis(S_prev, 0, 1)
    inter = jnp.einsum('bnihd,bnhde->bnihe', q * jnp.exp(b), S_prev)
    return (intra + inter).reshape(B, S, H, dv)


def _gla(gq, gk, gv, gr, ga_f, ga_b, w_gate_fwd, b_gate_fwd, w_gate_bwd, b_gate_bwd, out_norm):
    B, S, _ = gq.shape
    H = GLA_HEADS
    q = gq.reshape(B, S, H, GLA_DK) * (GLA_DK ** -0.5)
    k = gk.reshape(B, S, H, GLA_DK)
    v = gv.reshape(B, S, H, GLA_DV)
    la_f = (jax.nn.log_sigmoid((ga_f @ w_gate_fwd + b_gate_fwd).astype(jnp.float32)) / GLA_TAU).reshape(B, S, H, GLA_DK)
    la_b = (jax.nn.log_sigmoid((ga_b @ w_gate_bwd + b_gate_bwd).astype(jnp.float32)) / GLA_TAU).reshape(B, S, H, GLA_DK)
    o_f = _gla_dir(q, k, v, la_f, True)
    o_b = jnp.flip(_gla_dir(jnp.flip(q, 1), jnp.flip(k, 1), jnp.flip(v, 1), jnp.flip(la_b, 1), False), 1)
    o = _rmsnorm(o_f + o_b, out_norm)
    return jax.nn.silu(gr) * o.reshape(B, S, H * GLA_DV)


def _conv_ffn(x, norm_g, w_up, conv_w, conv_b, w_down):
    h = _rmsnorm(x, norm_g)
    up = h @ w_up
    gate, val = up[..., :D_FF], up[..., D_FF:]
    gate = lax.conv_general_dilated(gate, conv_w[:, None, :].astype(gate.dtype), window_strides=(1,),
                                    padding='SAME', dimension_numbers=('NWC', 'WIO', 'NWC'),
                                    feature_group_count=D_FF) + conv_b
    return (jax.nn.silu(gate) * val) @ w_down


def setup_inputs(seed: int = 0) -> dict:
    key = jax.random.key(seed)
    ks = iter(jax.random.split(key, 40))

    def dense(shape, fan_in):
        return jax.random.normal(next(ks), shape, jnp.float32) * (fan_in ** -0.5)

    def gain(shape):
        return 1.0 + 0.02 * jax.random.normal(next(ks), shape, jnp.float32)

    def small(shape, s):
        return s * jax.random.normal(next(ks), shape, jnp.float32)

    x = jax.random.normal(next(ks), (BATCH, SEQ, D_MODEL), jnp.float32)
    start = jax.random.randint(next(ks), (BATCH, 1), 0, 4096)
    positions = (start + jnp.arange(SEQ)[None, :]).astype(jnp.int32)
    ret_base = RET_DECAY_BASE + jnp.arange(RET_HEADS, dtype=jnp.float32)
    return {
        "x": x,
        "positions": positions,
        "mix_norm_even": gain((N_EVEN, D_MODEL)),
        "w_in_even": dense((N_EVEN, D_MODEL, EVEN_IN), D_MODEL),
        "mla_q_norm": gain((N_EVEN, MLA_Q_RANK)),
        "mla_kv_norm": gain((N_EVEN, MLA_KV_RANK)),
        "mla_w_uq": dense((N_EVEN, MLA_Q_RANK, MLA_HEADS * MLA_QK), MLA_Q_RANK),
        "mla_w_ukv": dense((N_EVEN, MLA_KV_RANK, MLA_HEADS * (MLA_NOPE + MLA_V)), MLA_KV_RANK),
        "mla_q_head_norm": gain((N_EVEN, MLA_QK)),
        "mla_k_head_norm": gain((N_EVEN, MLA_QK)),
        "ret_theta_fwd": ret_base + small((N_EVEN, RET_HEADS), 0.1),
        "ret_theta_bwd": ret_base + small((N_EVEN, RET_HEADS), 0.1),
        "ret_out_norm": gain((N_EVEN, RET_HEADS, RET_DV)),
        "w_out_even": dense((N_EVEN, EVEN_OUT, D_MODEL), EVEN_OUT),
        "mix_norm_odd": gain((N_ODD, D_MODEL)),
        "w_in_odd": dense((N_ODD, D_MODEL, ODD_IN), D_MODEL),
        "gla_w_gate_fwd": dense((N_ODD, GLA_GATE_RANK, GLA_HEADS * GLA_DK), GLA_GATE_RANK),
        "gla_b_gate_fwd": small((N_ODD, GLA_HEADS * GLA_DK), 0.1),
        "gla_w_gate_bwd": dense((N_ODD, GLA_GATE_RANK, GLA_HEADS * GLA_DK), GLA_GATE_RANK),
        "gla_b_gate_bwd": small((N_ODD, GLA_HEADS * GLA_DK), 0.1),
        "gla_out_norm": gain((N_ODD, GLA_HEADS, GLA_DV)),
        "w_out_odd": dense((N_ODD, ODD_OUT, D_MODEL), ODD_OUT),
        "ffn_norm": gain((DEPTH, D_MODEL)),
        "ffn_w_up": dense((DEPTH, D_MODEL, 2 * D_FF), D_MODEL),
        "ffn_conv_w": dense((DEPTH, CONV_W, D_FF), CONV_W),
        "ffn_conv_b": small((DEPTH, D_FF), 0.02),
        "ffn_w_down": dense((DEPTH, D_FF, D_MODEL), D_FF),
    }


def reference(x, positions, mix_norm_even, w_in_even, mla_q_norm, mla_kv_norm, mla_w_uq, mla_w_ukv,
              mla_q_head_norm, mla_k_head_norm, ret_theta_fwd, ret_theta_bwd, ret_out_norm, w_out_even,
              mix_norm_odd, w_in_odd, gla_w_gate_fwd, gla_b_gate_fwd, gla_w_gate_bwd, gla_b_gate_bwd,
              gla_out_norm, w_out_odd, ffn_norm, ffn_w_up, ffn_conv_w, ffn_conv_b, ffn_w_down):
    for layer in range(DEPTH):
        i = layer // 2
        if layer % 2 == 0:
            h = _rmsnorm(x, mix_norm_even[i])
            cq, ckv, k_rope, rq, rk, rv, rg = _split(h @ w_in_even[i], EVEN_SPLIT)
            a = _mla(cq, ckv, k_rope, positions, mla_q_norm[i], mla_kv_norm[i], mla_w_uq[i], mla_w_ukv[i],
                     mla_q_head_norm[i], mla_k_head_norm[i])
            r = _retention(rq, rk, rv, rg, positions, ret_theta_fwd[i], ret_theta_bwd[i], ret_out_norm[i])
            x = x + jnp.concatenate([a, r], axis=-1) @ w_out_even[i]
        else:
            h = _rmsnorm(x, mix_norm_odd[i])
            gq, gk, gv, gr, ga_f, ga_b = _split(h @ w_in_odd[i], ODD_SPLIT)
            g = _gla(gq, gk, gv, gr, ga_f, ga_b, gla_w_gate_fwd[i], gla_b_gate_fwd[i],
                     gla_w_gate_bwd[i], gla_b_gate_bwd[i], gla_out_norm[i])
            x = x + g @ w_out_odd[i]
        x = x + _conv_ffn(x, ffn_norm[layer], ffn_w_up[layer], ffn_conv_w[layer], ffn_conv_b[layer],
                          ffn_w_down[layer])
    return x
```

```python
import sys
import numpy as np
from contextlib import ExitStack
import concourse.bass as bass
import concourse.mybir as mybir
from concourse.bass_utils import run_bass_kernel_spmd

F32 = mybir.dt.float32
BF16 = mybir.dt.bfloat16
I32 = mybir.dt.int32
AF = mybir.ActivationFunctionType
ALU = mybir.AluOpType

D = 1024
DFF = 2816
NGC = DFF // 128
EPS = 1e-6


class Buf:
    __slots__ = ("name", "w", "r")

    def __init__(self, name=""):
        self.name = name
        self.w = []
        self.r = []


class Tile:
    def __init__(self, ap, name=""):
        self.ap = ap
        self.b = Buf(name)

    def __getitem__(self, k):
        return self.ap[k]


class _Rec:
    def __init__(self):
        self.call = None

    def __getattr__(self, name):
        def f(*a, **kw):
            self.call = (name, a, kw)
            return self
        return f


class FW:
    ENGS = ("pe", "act", "dve", "pool", "sp")
    NDMA = 36
    NSW = 8
    NEPOCH = 1

    def __init__(self, nc, stack):
        self.nc = nc
        self.streams = {e: [] for e in self.ENGS}
        self.sem = {}
        for ep in range(self.NEPOCH):
            for e in self.ENGS:
                self.sem[(e, ep)] = stack.enter_context(nc.semaphore(f"s_{e}_{ep}"))
        self.dsem = [stack.enter_context(nc.semaphore(f"s_dma_{i}")) for i in range(self.NDMA)]
        self.dval = [0] * self.NDMA
        self.ccsem = stack.enter_context(nc.semaphore("s_cc"))
        self.ccval = 0
        self.dnext = 0
        self.dnext_sw = 0
        self.epoch = 0
        self.cnt = {e: 0 for e in self.ENGS}
        self.seen = {}
        self.n_ops = {e: 0 for e in self.ENGS}

    def _wait(self, eng, ev):
        semk, val = ev
        if semk[0] == "E":
            if semk[2] != self.epoch:
                return
            if semk[1] == eng and eng == "pe":
                return
        key = (eng, semk)
        if self.seen.get(key, 0) >= val:
            return
        self.seen[key] = val
        if semk[0] == "E":
            s = self.sem[(semk[1], semk[2])]
        elif semk[0] == "C":
            s = self.ccsem
        else:
            s = self.dsem[semk[1]]
        self.streams[eng].append(lambda e, s=s, val=val: e.wait_ge(s, val))

    @staticmethod
    def _bl(ts):
        return [t.b if isinstance(t, Tile) else t for t in ts]

    def _deps(self, reads, writes):
        deps = []
        for b in reads:
            deps.extend(b.w)
        for b in writes:
            deps.extend(b.w)
            deps.extend(b.r)
        return deps

    def _commit(self, ev, reads, writes):
        for b in reads:
            b.r.append(ev)
            if len(b.r) > 48:
                b.r = b.r[-48:]
        for b in writes:
            b.w = [ev]
            b.r = []

    def op(self, eng, fn, reads=(), writes=()):
        reads = self._bl(reads)
        writes = self._bl(writes)
        for ev in self._deps(reads, writes):
            self._wait(eng, ev)
        self.cnt[eng] += 1
        ev = (("E", eng, self.epoch), self.cnt[eng])
        s = self.sem[(eng, self.epoch)]
        ln = sys._getframe(1).f_lineno
        rec = _Rec()
        fn(rec)
        call = rec.call
        self.streams[eng].append(lambda e, call=call, s=s, ln=ln: getattr(e, call[0])(*call[1], **call[2]).then_inc(s, 1).annotate(f"L{ln}"))
        self.n_ops[eng] += 1
        self._commit(ev, reads, writes)
        return ev

    def dma(self, q, out, in_, reads=(), writes=(), tag="", **kw):
        reads = self._bl(reads)
        writes = self._bl(writes)
        if q == "pool":
            i = self.NDMA - self.NSW + self.dnext_sw
            self.dnext_sw = (self.dnext_sw + 1) % self.NSW
        else:
            i = self.dnext
            self.dnext = (self.dnext + 1) % (self.NDMA - self.NSW)
        if self.dval[i] > 0:
            self._wait(q, (("D", i), self.dval[i]))
        for ev in self._deps(reads, writes):
            self._wait(q, ev)
        self.dval[i] += 16
        ev = (("D", i), self.dval[i])
        s = self.dsem[i]
        self.streams[q].append(
            lambda e, s=s, out=out, in_=in_, kw=kw, ln=sys._getframe(1).f_lineno, tag=tag: e.dma_start(
                out=out, in_=in_, **kw).then_inc(s, 16).annotate(f"L{ln}{tag}"))
        self.n_ops[q] += 1
        self._commit(ev, reads, writes)
        return ev

    def collective(self, kind, op, groups, in_ap, out_ap, reads=(), writes=()):
        reads = self._bl(reads)
        writes = self._bl(writes)
        eng = "pool"
        for ev in self._deps(reads, writes):
            self._wait(eng, ev)
        self.ccval += 1
        ev = (("C",), self.ccval)
        s = self.ccsem
        self.streams[eng].append(
            lambda e, s=s: e.collective_compute(kind, op, replica_groups=groups, ins=[in_ap],
                                                outs=[out_ap]).then_inc(s, 1))
        self._commit(ev, reads, writes)
        return ev

    def barrier(self):
        for i in range(self.NDMA):
            if self.dval[i] > 0:
                self._wait("pool", (("D", i), self.dval[i]))
        for e in self.ENGS:
            if e != "pool" and self.cnt[e] > 0:
                self._wait("pool", (("E", e, self.epoch), self.cnt[e]))
        if self.ccval > 0:
            self._wait("pool", (("C",), self.ccval))
        self.cnt["pool"] += 1
        s = self.sem[("pool", self.epoch)]
        self.streams["pool"].append(lambda e, s=s: e.nop().then_inc(s, 1))
        for e in self.ENGS:
            if e != "pool":
                self._wait(e, (("E", "pool", self.epoch), self.cnt["pool"]))
        if self.epoch + 1 < self.NEPOCH:
            self.epoch += 1
            self.cnt = {e: 0 for e in self.ENGS}

    def finish(self, block):
        st = self.streams

        @block.tensor
        def _(e):
            for f in st["pe"]:
                f(e)

        @block.scalar
        def _(e):
            for f in st["act"]:
                f(e)

        @block.vector
        def _(e):
            for f in st["dve"]:
                f(e)

        @block.gpsimd
        def _(e):
            for f in st["pool"]:
                f(e)

        @block.sync
        def _(e):
            for f in st["sp"]:
                f(e)


class SBAlloc:
    WORDS = 50176

    def __init__(self, nc, stack):
        self.t = stack.enter_context(nc.sbuf_tensor("sbig", [128, self.WORDS], F32))
        self.off = 0
        self.n = 0

    def mark(self):
        return self.off

    def release(self, m):
        self.off = m

    def alloc(self, free_shape, dtype=F32, parts=128, name=None):
        n = int(np.prod(free_shape))
        esz = 4 if dtype in (F32, I32) else 2
        words = (n * esz + 3) // 4
        words = (words + 7) // 8 * 8
        assert self.off + words <= self.WORDS, f"SBUF overflow: {self.off}+{words}"
        ap = self.t[0:parts, self.off:self.off + words]
        self.off += words
        if dtype != F32:
            ap = ap.bitcast(dtype)
        ap = ap[:, 0:n]
        if len(free_shape) == 2:
            ap = ap.rearrange("p (a b) -> p a b", a=free_shape[0])
        elif len(free_shape) == 3:
            ap = ap.rearrange("p (a b c) -> p a b c", a=free_shape[0], b=free_shape[1])
        self.n += 1
        return Tile(ap, name or f"sb{self.n}")


def _make_consts():
    j = np.arange(128)[:, None]
    i = np.arange(128)[None, :]
    same64 = (j // 64) == (i // 64)
    c = {}
    g = -1.0 / 16.0
    c["m_le"] = np.where(same64 & (j <= i), g, 0.0)
    c["m_ge"] = np.where(same64 & (j >= i), g, 0.0)
    c["m_gt"] = np.where(same64 & (j > i), g, 0.0)
    c["m_lt"] = np.where(same64 & (j < i), g, 0.0)
    c["mf64"] = np.where(same64 & (j <= i), 1.0, 0.0)
    c["mb64"] = np.where(same64 & (j > i), 1.0, 0.0)
    c["relf"] = np.maximum(i - j, 0).astype(np.float64)
    c["relb"] = np.maximum(j - i, 0).astype(np.float64)
    c["mf128"] = (j <= i).astype(np.float64)
    c["mb128"] = (j > i).astype(np.float64)
    c["ident"] = (j == i).astype(np.float64)
    c["iota1"] = (i + 1.0) + 0 * j
    c["iotar"] = (128.0 - i) + 0 * j
    c["c128"] = np.full((128, 128), 128.0)
    c["ones96"] = ((j < 96) + 0 * i).astype(np.float64)
    c["ones64b"] = ((j // 64) == (i // 64)).astype(np.float64)
    sel = np.zeros((128, 128))
    for m in range(96):
        sel[m, m] = 1.0
    for m in range(32):
        sel[96 + m, 64 + m] = 1.0
    c["sel"] = sel
    sel65 = np.zeros((128, 128))
    sel65[64, :64] = 1.0
    c["sel65"] = sel65
    rm = np.zeros((128, 128))
    for m in range(32):
        rm[m, 64 + m] = 1.0
    for m in range(16):
        rm[m + 16, 96 + m] = -1.0
        rm[m, 96 + 16 + m] = 1.0
    c["ropemat"] = rm
    tabc = np.zeros((128, 128))
    p = np.arange(128)
    inv32 = 10000.0 ** (-(np.arange(32, dtype=np.float32) / np.float32(32))).astype(np.float32)
    inv16 = 10000.0 ** (-(np.arange(16, dtype=np.float32) / np.float32(16))).astype(np.float32)
    tabc[:, 0] = inv32[p % 32] / (2 * np.pi)
    tabc[64:, 1] = inv16[p[64:] % 16] / (2 * np.pi)
    tabc[:, 2] = 0.25
    tabc[:, 3] = 0.0
    tabc[:96, 4] = 0.25
    tabc[:, 5] = p
    tabc[:, 6] = 127.0 - p
    c["tabc"] = tabc
    names = list(c)
    arr = np.concatenate([c[n].astype(np.float32) for n in names], axis=1)
    return names, np.ascontiguousarray(arr)


CONST_NAMES, CONST_ARR = _make_consts()


class Ctx:
    pass


def build_program(T, groups, n_layers=4, dbg=None):
    dbg = dbg or {}
    nc = bass.Bass("TRN2", target_bir_lowering=False)
    S2 = 2 * T
    K = Ctx()
    K.nc, K.T, K.S2, K.dbg = nc, T, S2, dbg

    def din(name, shape, dt=F32):
        return nc.dram_tensor(name, list(shape), dt, kind="ExternalInput").ap()

    def dscr(name, shape, dt=F32):
        kind = "ExternalOutput" if name in dbg.get("dump", ()) else "Internal"
        return Tile(nc.dram_tensor(name, list(shape), dt, kind=kind).ap(), name)

    K.xT_in = din("xT", [D, T])
    K.flags = din("flags", [128, 4])
    K.w = {}
    for nm, shp in (("ffn_norm", [4, D]), ("ffn_w_up", [4, D, 2 * DFF]), ("ffn_conv_w", [4, 3, DFF]),
                    ("ffn_conv_b", [4, DFF]), ("ffn_w_down", [4, DFF, D]),
                    ("w_out_even", [2, D, D]), ("w_out_odd", [2, D, D])):
        K.w[nm] = din(nm, shp)
    for nm, shp in (("mix_norm_odd", [2, D]), ("w_in_odd", [2, D, 3104]), ("gla_w_gate_fwd", [2, 16, 512]),
                    ("gla_b_gate_fwd", [2, 512]), ("gla_w_gate_bwd", [2, 16, 512]), ("gla_b_gate_bwd", [2, 512]),
                    ("gla_out_norm", [2, 4, 256])):
        K.w[nm] = din(nm, shp)
    for nm, shp in (("mix_norm_even", [2, D]), ("w_in_even", [2, D, 2720]), ("mla_q_norm", [2, 384]),
                    ("mla_kv_norm", [2, 256]), ("mla_w_uq", [2, 384, 768]), ("mla_w_ukv", [2, 256, 1024]),
                    ("mla_q_head_norm", [2, 96]), ("mla_k_head_norm", [2, 96]), ("ret_theta_fwd", [2, 8]),
                    ("ret_theta_bwd", [2, 8]), ("ret_out_norm", [2, 8, 64])):
        K.w[nm] = din(nm, shp)
    K.pos_loc = din("pos_loc", [1, T], I32)
    K.pos_all = din("pos_all", [1, S2], I32)
    K.consts_in = din("consts", list(CONST_ARR.shape))
    if "dbg_at" in dbg:
        K.dbg_at = din("dbg_at", [D, T])
    K.yT = nc.dram_tensor("yT", [D, T], F32, kind="ExternalOutput").ap()

    K.xT = dscr("xT_s", [D, T])
    K.AT = dscr("AT", [D, T], BF16)
    K.H2T = dscr("H2T", [D, T + 2], BF16)
    K.HBX = dscr("HBX", [128, 16])
    K.HBG = dscr("HBG", [256, 16])
    NC64 = T // 64
    for nm in ("QCF", "KCF", "QBF", "QCB", "KCB", "QBB"):
        setattr(K, nm, dscr(nm, [512, T], BF16))
    K.KDF = dscr("KDF", [T, 512], BF16)
    K.KDB = dscr("KDB", [T, 512], BF16)
    K.VT = dscr("VT", [T, 1024], BF16)
    K.SGT = dscr("SGT", [D, T], BF16)
    K.RPB = dscr("RPB", [NC64, 128, 1024], BF16)
    K.GST = dscr("GST", [128, 2048])
    K.GSG = dscr("GSG", [256, 2048])
    K.TRC = dscr("TRC", [128, T])
    K.TRS = dscr("TRS", [128, T])
    K.TQ = dscr("TQ", [128, T])
    K.TK = dscr("TK", [128, S2])
    K.CQN = dscr("CQN", [384, T], BF16)
    TL = min(512, T)
    K.LAT = [dscr(f"LAT{j}", [288, TL]) for j in range(T // TL)]
    K.LATG = [dscr(f"LATG{j}", [576, TL]) for j in range(T // TL)]
    K.RQ = dscr("RQ", [512, T], BF16)
    K.RK = dscr("RK", [512, T], BF16)
    K.RV = dscr("RV", [T, 512], BF16)
    K.QT = dscr("QT", [8, 96, T], BF16)
    K.KT = dscr("KT", [8, 96, S2], BF16)
    K.VX = dscr("VX", [S2, 520], BF16)
    K.RST = dscr("RST", [64, 1024])
    K.RSG = dscr("RSG", [128, 1024])

    with ExitStack() as st:
        fw = FW(nc, st)
        sb = SBAlloc(nc, st)
        K.fw, K.sb, K.groups = fw, sb, groups
        psum = st.enter_context(nc.psum_tensor("psum", [128, 4096], F32))
        K.ps = [Tile(psum[:, i * 512:(i + 1) * 512], f"ps{i}") for i in range(8)]
        K.psum_full = psum
        K.ps_rr = 0

        K.ones_bf = sb.alloc([128], BF16, name="ones_bf")
        fw.op("dve", lambda e: e.memset(K.ones_bf[:], 1.0), writes=[K.ones_bf])
        K.flg = sb.alloc([4], F32, name="flg")
        fw.dma("sp", K.flg[:], K.flags[:, :], writes=[K.flg])
        K.cst = {}
        for ci, nm in enumerate(CONST_NAMES):
            t = sb.alloc([128], F32, name="c_" + nm)
            fw.dma("sp", t[:], K.consts_in[:, ci * 128:(ci + 1) * 128], writes=[t])
            K.cst[nm] = t
        K.ones_f = sb.alloc([128], F32, name="ones_f")
        fw.op("dve", lambda e: e.memset(K.ones_f[:], 1.0), writes=[K.ones_f])

        if "only_odd" not in dbg:
            make_tables(K)
        m0 = sb.mark()
        for c in range(8):
            t = sb.alloc([T], F32)
            fw.dma("sp", t[:], K.xT_in[c * 128:(c + 1) * 128, :], writes=[t])
            fw.dma("sp", K.xT[c * 128:(c + 1) * 128, :], t[:], reads=[t])
            if c % 2 == 1:
                pass
        fw.barrier()
        sb.release(m0)

        for L in range(n_layers):
            i = L // 2
            if "dbg_at" in dbg:
                m0 = sb.mark()
                for c in range(8):
                    t = sb.alloc([T], BF16)
                    fw.dma("pool", t[:], K.dbg_at[c * 128:(c + 1) * 128, :], writes=[t])
                    fw.dma("sp", K.AT[c * 128:(c + 1) * 128, :], t[:], reads=[t])
                fw.barrier()
                sb.release(m0)
            elif L % 2 == 1 or "only_odd" in dbg:
                gla_layer(K, i)
            else:
                even_layer(K, i)
            wout = K.w["w_out_even"][i] if (L % 2 == 0 and "only_odd" not in dbg) else K.w["w_out_odd"][i]
            phase_out(K, L, wout)
            phase_ffn(K, L, last=(L == n_layers - 1))

        fw.barrier()
        with nc.Block() as block:
            fw.finish(block)
    K.n_ops = fw.n_ops
    return nc, K


def next_ps(K):
    p = K.ps[K.ps_rr % 6]
    K.ps_rr += 1
    return p


def load_w_bf16(K, dst, src, kc_n, ncols, c0=0):
    v = src.rearrange("(c p) n -> p c n", p=128)
    step = max(1, 4096 // ncols)
    for k0 in range(0, kc_n, step):
        k1 = min(kc_n, k0 + step)
        K.fw.dma("pool", dst[:, k0:k1, c0:c0 + ncols], v[:, k0:k1, :], writes=[dst])


def load_col(K, dst, src_vec, nchunk):
    K.fw.dma("sp", dst[:, 0:nchunk], src_vec.rearrange("(c p) -> p c", p=128), writes=[dst],
             allow_slow_non_contiguous=True)


def rms_rstd(K, sq, nchunk, width, dim, out_rstd, ones=None, P=128):
    fw = K.fw
    ones = ones or K.ones_bf
    ps = next_ps(K)
    for c in range(nchunk):
        fw.op("pe", lambda e, c=c: e.matmul(ps[0:P, 0:width], ones[0:P, 0:P], sq[0:P, c, 0:width],
                                            start=(c == 0), stop=(c == nchunk - 1)),
              reads=[ones, sq], writes=[ps])
    fw.op("act", lambda e: e.activation(out_rstd[0:P, 0:width], ps[0:P, 0:width], AF.Sqrt, bias=EPS,
                                        scale=1.0 / dim), reads=[ps], writes=[out_rstd])
    fw.op("dve", lambda e: e.reciprocal(out_rstd[0:P, 0:width], out_rstd[0:P, 0:width]),
          reads=[out_rstd], writes=[out_rstd])


def phase_out(K, L, wout):
    fw, sb, T = K.fw, K.sb, K.T
    m0 = sb.mark()
    TT = 512 if T >= 512 else T
    w = sb.alloc([8, D], BF16, name="wout")
    load_w_bf16(K, w, wout, 8, D)
    g2 = sb.alloc([8], F32, name="g2")
    load_col(K, g2, K.w["ffn_norm"][L], 8)
    hb = sb.alloc([8, 2], F32, name="hb")
    ATv = K.AT.ap.rearrange("(c p) t -> p c t", p=128)
    XTv = K.xT.ap.rearrange("(c p) t -> p c t", p=128)
    H2v = K.H2T.ap.rearrange("(c p) t -> p c t", p=128)
    nt = T // TT
    at_t = [sb.alloc([8, TT], BF16, name=f"at{j}") for j in range(2)]
    x_t = [sb.alloc([8, TT], F32, name=f"x{j}") for j in range(2)]
    sq_t = sb.alloc([8, TT], BF16, name="sq")
    h2_t = [sb.alloc([8, TT], BF16, name=f"h2{j}") for j in range(2)]
    rstd = sb.alloc([TT], F32, name="rstd")
    def load_out(ti):
        fw.dma("sp", at_t[ti % 2][:, :, :], ATv[:, :, ti * TT:(ti + 1) * TT], writes=[at_t[ti % 2]])
        fw.dma("sp", x_t[ti % 2][:, :, :], XTv[:, :, ti * TT:(ti + 1) * TT], writes=[x_t[ti % 2]])

    load_out(0)
    for ti in range(nt):
        t0 = ti * TT
        at, xt, h2 = at_t[ti % 2], x_t[ti % 2], h2_t[ti % 2]
        if ti + 1 < nt:
            load_out(ti + 1)
        for oc in range(8):
            ps = next_ps(K)
            for kc in range(8):
                fw.op("pe", lambda e, kc=kc, oc=oc, ps=ps, at=at: e.matmul(
                    ps[:, 0:TT], w[:, kc, oc * 128:(oc + 1) * 128], at[:, kc, :],
                    start=(kc == 0), stop=(kc == 7)), reads=[w, at], writes=[ps])
            fw.op("dve", lambda e, oc=oc, ps=ps, xt=xt: e.tensor_tensor(
                xt[:, oc, :], xt[:, oc, :], ps[:, 0:TT], ALU.add), reads=[ps, xt], writes=[xt])
        fw.dma("sp", XTv[:, :, t0:t0 + TT], xt[:, :, :], reads=[xt])
        for oc in range(8):
            fw.op("act", lambda e, oc=oc, xt=xt: e.activation(sq_t[:, oc, :], xt[:, oc, :], AF.Square),
                  reads=[xt], writes=[sq_t])
        rms_rstd(K, sq_t, 8, TT, D, rstd)
        for oc in range(8):
            fw.op("dve", lambda e, oc=oc, xt=xt, h2=h2: e.scalar_tensor_tensor(
                h2[:, oc, :], xt[:, oc, :], g2[:, oc:oc + 1], rstd[:, :], ALU.mult, ALU.mult),
                reads=[xt, g2, rstd], writes=[h2])
        fw.dma("sp", H2v[:, :, 1 + t0:1 + t0 + TT], h2[:, :, :], reads=[h2])
        if ti == 0:
            fw.op("dve", lambda e, h2=h2: e.tensor_copy(hb[:, :, 0:1], h2[:, :, 0:1]), reads=[h2], writes=[hb])
        if ti == nt - 1:
            fw.op("dve", lambda e, h2=h2: e.tensor_copy(hb[:, :, 1:2], h2[:, :, TT - 1:TT]), reads=[h2], writes=[hb])
    fw.dma("sp", K.HBX[:, :], hb[:, :, :].rearrange("p a b -> p (a b)"), reads=[hb], writes=[K.HBX])
    fw.collective("AllGather", ALU.bypass, K.groups, K.HBX.ap, K.HBG.ap, reads=[K.HBX], writes=[K.HBG])
    g0 = sb.alloc([8, 2], F32, name="g0")
    g1 = sb.alloc([8, 2], F32, name="g1")
    fw.dma("sp", g0[:, :, :].rearrange("p a b -> p (a b)"), K.HBG[0:128, :], reads=[K.HBG], writes=[g0])
    fw.dma("sp", g1[:, :, :].rearrange("p a b -> p (a b)"), K.HBG[128:256, :], reads=[K.HBG], writes=[g1])
    hl = sb.alloc([8, 1], BF16, name="hl")
    hr = sb.alloc([8, 1], BF16, name="hr")
    fw.op("dve", lambda e: e.tensor_scalar(hl[:, :, :], g0[:, :, 1:2], K.flg[:, 0:1], None, ALU.mult),
          reads=[g0, K.flg], writes=[hl])
    fw.op("dve", lambda e: e.tensor_scalar(hr[:, :, :], g1[:, :, 0:1], K.flg[:, 1:2], None, ALU.mult),
          reads=[g1, K.flg], writes=[hr])
    fw.dma("sp", H2v[:, :, 0:1], hl[:, :, :], reads=[hl], allow_slow_non_contiguous=True)
    fw.dma("sp", H2v[:, :, T + 1:T + 2], hr[:, :, :], reads=[hr], allow_slow_non_contiguous=True)
    fw.barrier()
    sb.release(m0)


def phase_ffn(K, L, last):
    fw, sb, T = K.fw, K.sb, K.T
    m0 = sb.mark()
    TF = 256
    wup = sb.alloc([8, 2 * DFF], BF16, name="wup")
    load_w_bf16(K, wup, K.w["ffn_w_up"][L], 8, 2 * DFF)
    wdn = sb.alloc([NGC, D], BF16, name="wdn")
    load_w_bf16(K, wdn, K.w["ffn_w_down"][L], NGC, D)
    cw = sb.alloc([3, NGC], F32, name="cw")
    for k in range(3):
        fw.dma("sp", cw[:, k, :], K.w["ffn_conv_w"][L, k].rearrange("(c p) -> p c", p=128), writes=[cw],
               allow_slow_non_contiguous=True)
    cb = sb.alloc([NGC], F32, name="cb")
    load_col(K, cb, K.w["ffn_conv_b"][L], NGC)
    XTv = K.xT.ap.rearrange("(c p) t -> p c t", p=128)
    YTv = K.yT.rearrange("(c p) t -> p c t", p=128)
    H2v = K.H2T.ap.rearrange("(c p) t -> p c t", p=128)
    h_t = [sb.alloc([8, TF + 2], BF16, name=f"h{j}") for j in range(2)]
    x_t = [sb.alloc([8, TF], F32, name=f"xf{j}") for j in range(2)]
    act = sb.alloc([NGC, TF], BF16, name="act")
    cv = [sb.alloc([TF], F32, name=f"cv{j}") for j in range(2)]
    def load_ffn(ti):
        fw.dma("sp", h_t[ti % 2][:, :, :], H2v[:, :, ti * TF:ti * TF + TF + 2], writes=[h_t[ti % 2]])
        fw.dma("sp", x_t[ti % 2][:, :, :], XTv[:, :, ti * TF:(ti + 1) * TF], writes=[x_t[ti % 2]])

    load_ffn(0)
    for ti in range(T // TF):
        t0 = ti * TF
        h, xt = h_t[ti % 2], x_t[ti % 2]
        if ti + 1 < T // TF:
            load_ffn(ti + 1)
        for gc in range(NGC):
            psg = next_ps(K)
            psv = next_ps(K)
            c = cv[gc % 2]
            for kc in range(8):
                fw.op("pe", lambda e, kc=kc, gc=gc, psg=psg, h=h: e.matmul(
                    psg[:, 0:TF + 2], wup[:, kc, gc * 128:(gc + 1) * 128], h[:, kc, :],
                    start=(kc == 0), stop=(kc == 7)), reads=[wup, h], writes=[psg])
            for kc in range(8):
                fw.op("pe", lambda e, kc=kc, gc=gc, psv=psv, h=h: e.matmul(
                    psv[:, 0:TF], wup[:, kc, DFF + gc * 128:DFF + (gc + 1) * 128], h[:, kc, 1:TF + 1],
                    start=(kc == 0), stop=(kc == 7)), reads=[wup, h], writes=[psv])
            fw.op("act", lambda e, gc=gc, psg=psg, c=c: e.activation(
                c[:, :], psg[:, 1:TF + 1], AF.Identity, bias=cb[:, gc:gc + 1], scale=cw[:, 1, gc:gc + 1]),
                reads=[psg, cb, cw], writes=[c])
            fw.op("dve", lambda e, gc=gc, psg=psg, c=c: e.scalar_tensor_tensor(
                c[:, :], psg[:, 0:TF], cw[:, 0, gc:gc + 1], c[:, :], ALU.mult, ALU.add),
                reads=[psg, cw, c], writes=[c])
            fw.op("dve", lambda e, gc=gc, psg=psg, c=c: e.scalar_tensor_tensor(
                c[:, :], psg[:, 2:TF + 2], cw[:, 2, gc:gc + 1], c[:, :], ALU.mult, ALU.add),
                reads=[psg, cw, c], writes=[c])
            fw.op("act", lambda e, c=c: e.activation(c[:, :], c[:, :], AF.Silu), reads=[c], writes=[c])
            fw.op("dve", lambda e, gc=gc, psv=psv, c=c: e.tensor_tensor(
                act[:, gc, :], c[:, :], psv[:, 0:TF], ALU.mult), reads=[c, psv], writes=[act])
        for oc in range(8):
            ps = next_ps(K)
            for gc in range(NGC):
                fw.op("pe", lambda e, gc=gc, oc=oc, ps=ps: e.matmul(
                    ps[:, 0:TF], wdn[:, gc, oc * 128:(oc + 1) * 128], act[:, gc, :],
                    start=(gc == 0), stop=(gc == NGC - 1)), reads=[wdn, act], writes=[ps])
            fw.op("dve", lambda e, oc=oc, ps=ps, xt=xt: e.tensor_tensor(
                xt[:, oc, :], xt[:, oc, :], ps[:, 0:TF], ALU.add), reads=[ps, xt], writes=[xt])
        if last:
            fw.dma("sp", YTv[:, :, t0:t0 + TF], xt[:, :, :], reads=[xt])
        else:
            fw.dma("sp", XTv[:, :, t0:t0 + TF], xt[:, :, :], reads=[xt])
    fw.barrier()
    sb.release(m0)


def norm_tile(K, xt, h, sq, rstd, gcol, TT):
    fw = K.fw
    for oc in range(8):
        fw.op("act", lambda e, oc=oc: e.activation(sq[:, oc, 0:TT], xt[:, oc, 0:TT], AF.Square),
              reads=[xt], writes=[sq])
    rms_rstd(K, sq, 8, TT, D, rstd)
    for oc in range(8):
        fw.op("dve", lambda e, oc=oc: e.scalar_tensor_tensor(
            h[:, oc, 0:TT], xt[:, oc, 0:TT], gcol[:, oc:oc + 1], rstd[:, 0:TT], ALU.mult, ALU.mult),
            reads=[xt, gcol, rstd], writes=[h])


def proj_fm(K, w, c0, ncol, h, TT, t0=0):
    ps = next_ps(K)
    for kc in range(8):
        K.fw.op("pe", lambda e, kc=kc: e.matmul(ps[0:ncol, 0:TT], w[:, kc, c0:c0 + ncol], h[:, kc, t0:t0 + TT],
                                                start=(kc == 0), stop=(kc == 7)), reads=[w, h], writes=[ps])
    return ps


def proj_tm(K, w, c0, ncol, h, t0, ps=None):
    ps = ps or next_ps(K)
    for kc in range(8):
        K.fw.op("pe", lambda e, kc=kc: e.matmul(ps[:, 0:ncol], h[:, kc, t0:t0 + 128], w[:, kc, c0:c0 + ncol],
                                                start=(kc == 0), stop=(kc == 7)), reads=[w, h], writes=[ps])
    return ps


def gla_layer(K, i):
    sb, T = K.sb, K.T
    mL = sb.mark()
    NC = T // 64
    decf = sb.alloc([4, NC], F32, name="decf")
    decb = sb.alloc([4, NC], F32, name="decb")
    gla_o1(K, i, decf, decb)
    gla_o23(K, i, decf, decb)
    sb.release(mL)


def gla_o1(K, i, decf, decb):
    fw, sb, T = K.fw, K.sb, K.T
    m0 = sb.mark()
    TT = min(512, T)
    NS = TT // 128
    w = sb.alloc([8, 3104], BF16, name="w_in_odd")
    load_w_bf16(K, w, K.w["w_in_odd"][i], 8, 3104)
    g1 = sb.alloc([8], F32, name="g1")
    load_col(K, g1, K.w["mix_norm_odd"][i], 8)
    gn = sb.alloc([8], F32, name="gn")
    load_col(K, gn, K.w["gla_out_norm"][i].rearrange("h e -> (h e)"), 8)
    wg = {}
    gae = {}
    for d, wn, bn in (("f", "gla_w_gate_fwd", "gla_b_gate_fwd"), ("b", "gla_w_gate_bwd", "gla_b_gate_bwd")):
        t = sb.alloc([512], F32, parts=32, name="wg" + d)
        fw.dma("sp", t[0:16, :], K.w[wn][i], writes=[t])
        fw.dma("sp", t[16:17, :], K.w[bn][i:i + 1, :], writes=[t])
        wg[d] = t
        ga = sb.alloc([TT], F32, parts=32, name="gae" + d)
        fw.op("dve", lambda e, ga=ga: e.memset(ga[:, :], 1.0), writes=[ga])
        gae[d] = ga
    XTv = K.xT.ap.rearrange("(c p) t -> p c t", p=128)
    x_t = [sb.alloc([8, TT], F32, name=f"gx{j}") for j in range(2)]
    h = sb.alloc([8, TT], BF16, name="gh")
    sq = sb.alloc([8, TT], BF16, name="gsq")
    rstd = sb.alloc([TT], F32, name="grstd")
    q = sb.alloc([4, TT], F32, name="gq")
    k = sb.alloc([4, TT], F32, name="gk")
    sg = sb.alloc([8, TT], BF16, name="gsg")
    tmp = [sb.alloc([512], F32, name=f"gtmp{j}") for j in range(2)]
    outs = {nm: sb.alloc([4, TT], BF16, name="o" + nm) for nm in ("QCF", "KCF", "QBF", "QCB", "KCB", "QBB")}
    lt = {d: sb.alloc([512], F32, name="l" + d) for d in "fb"}
    vt = sb.alloc([1024], BF16, name="gvt")
    kd = [sb.alloc([512], BF16, name=f"gkd{j}") for j in range(2)]
    Et = [sb.alloc([128], F32, name=f"gE{j}") for j in range(2)]
    rEt = [sb.alloc([128], F32, name=f"grE{j}") for j in range(2)]
    ecnt = 0
    fw.dma("sp", x_t[0][:, :, :], XTv[:, :, 0:TT], writes=[x_t[0]])
    for ti in range(T // TT):
        t0 = ti * TT
        xt = x_t[ti % 2]
        if ti + 1 < T // TT:
            fw.dma("sp", x_t[(ti + 1) % 2][:, :, :], XTv[:, :, t0 + TT:t0 + 2 * TT], writes=[x_t[(ti + 1) % 2]])
        norm_tile(K, xt, h, sq, rstd, g1, TT)
        for c in range(4):
            ps = proj_fm(K, w, c * 128, 128, h, TT)
            fw.op("act", lambda e, c=c, ps=ps: e.mul(q[:, c, :], ps[:, 0:TT], 128 ** -0.5), reads=[ps], writes=[q])
        for c in range(4):
            ps = proj_fm(K, w, 512 + c * 128, 128, h, TT)
            fw.op("dve", lambda e, c=c, ps=ps: e.tensor_copy(k[:, c, :], ps[:, 0:TT]), reads=[ps], writes=[k])
        for c in range(8):
            ps = proj_fm(K, w, 2048 + c * 128, 128, h, TT)
            tm = tmp[c % 2]
            fw.op("act", lambda e, ps=ps, tm=tm: e.activation(tm[:, 0:TT], ps[:, 0:TT], AF.Silu), reads=[ps], writes=[tm])
            fw.op("dve", lambda e, c=c, tm=tm: e.tensor_scalar(sg[:, c, :], tm[:, 0:TT], gn[:, c:c + 1], None, ALU.mult),
                  reads=[tm, gn], writes=[sg])
        fw.dma("sp", K.SGT.ap.rearrange("(c p) t -> p c t", p=128)[:, :, t0:t0 + TT], sg[:, :, :], reads=[sg])
        for d, c0 in (("f", 3072), ("b", 3088)):
            ps = proj_fm(K, w, c0, 16, h, TT)
            fw.op("dve", lambda e, d=d, ps=ps: e.tensor_copy(gae[d][0:16, :], ps[0:16, 0:TT]), reads=[ps], writes=[gae[d]])
        for s_ in range(NS):
            s0 = s_ * 128
            tok0 = t0 + s0
            for half in range(2):
                ps = proj_tm(K, w, 1024 + half * 512, 512, h, s0)
                if half == 0:
                    fw.op("act", lambda e, ps=ps: e.copy(vt[:, 0:512], ps[:, 0:512]), reads=[ps], writes=[vt])
                else:
                    fw.op("dve", lambda e, ps=ps: e.tensor_copy(vt[:, 512:1024], ps[:, 0:512]), reads=[ps], writes=[vt])
            fw.dma("sp", K.VT[tok0:tok0 + 128, :], vt[:, :], reads=[vt])
            psk = proj_tm(K, w, 512, 512, h, s0, ps=K.ps[6])
            for di, d in enumerate("fb"):
                l = lt[d]
                psx = next_ps(K)
                fw.op("pe", lambda e, d=d, psx=psx: e.matmul(psx[:, 0:512], gae[d][0:17, s0:s0 + 128], wg[d][0:17, :],
                                                             start=True, stop=True), reads=[gae[d], wg[d]], writes=[psx])
                fw.op("act", lambda e, psx=psx, l=l: e.activation(l[:, :], psx[:, 0:512], AF.Exp, scale=-1.0),
                      reads=[psx], writes=[l])
                fw.op("act", lambda e, l=l: e.activation(l[:, :], l[:, :], AF.Ln, bias=1.0), reads=[l], writes=[l])
                mstrict = K.cst["m_gt"] if d == "f" else K.cst["m_lt"]
                pse = next_ps(K)
                fw.op("pe", lambda e, pse=pse, l=l, m=mstrict: e.matmul(pse[:, 0:512], m[:, :], l[:, :], start=True, stop=True),
                      reads=[mstrict, l], writes=[pse])
                tm = tmp[di]
                fw.op("act", lambda e, pse=pse, tm=tm: e.activation(tm[:, :], pse[:, 0:512], AF.Exp), reads=[pse], writes=[tm])
                kdt = kd[di]
                fw.op("dve", lambda e, tm=tm, kdt=kdt: e.tensor_tensor(kdt[:, :], tm[:, :], psk[:, 0:512], ALU.mult),
                      reads=[tm, psk], writes=[kdt])
                fw.dma("sp", (K.KDF if d == "f" else K.KDB)[tok0:tok0 + 128, :], kdt[:, :], reads=[kdt])
                mincl = K.cst["m_le"] if d == "f" else K.cst["m_ge"]
                mid = 32 if d == "f" else 31
                last = 63 if d == "f" else 0
                qc, kc_, qb = (outs["QCF"], outs["KCF"], outs["QBF"]) if d == "f" else (outs["QCB"], outs["KCB"], outs["QBB"])
                dec = decf if d == "f" else decb
                for hh in range(4):
                    E, rE = Et[ecnt % 2], rEt[ecnt % 2]
                    ecnt += 1
                    psb = next_ps(K)
                    fw.op("pe", lambda e, psb=psb, l=l, hh=hh, m=mincl: e.matmul(
                        psb[:, 0:128], l[:, hh * 128:(hh + 1) * 128], m[:, :], start=True, stop=True),
                        reads=[l, mincl], writes=[psb])
                    fw.op("act", lambda e, psb=psb, E=E: e.activation(E[:, :], psb[:, 0:128], AF.Exp), reads=[psb], writes=[E])
                    fw.op("dve", lambda e, E=E, rE=rE: e.reciprocal(rE[:, :], E[:, :]), reads=[E], writes=[rE])
                    for ch in range(2):
                        cs = slice(ch * 64, ch * 64 + 64)
                        ts_ = slice(s0 + ch * 64, s0 + ch * 64 + 64)
                        mc = ch * 64 + mid
                        fw.op("dve", lambda e, E=E, rE=rE, cs=cs, ts_=ts_, mc=mc, hh=hh, qc=qc: e.scalar_tensor_tensor(
                            qc[:, hh, ts_], E[:, cs], rE[:, mc:mc + 1], q[:, hh, ts_], ALU.mult, ALU.mult),
                            reads=[E, rE, q], writes=[qc])
                        fw.op("dve", lambda e, E=E, rE=rE, cs=cs, ts_=ts_, mc=mc, hh=hh, kc_=kc_: e.scalar_tensor_tensor(
                            kc_[:, hh, ts_], rE[:, cs], E[:, mc:mc + 1], k[:, hh, ts_], ALU.mult, ALU.mult),
                            reads=[E, rE, k], writes=[kc_])
                        cidx = (tok0 // 64) + ch
                        lc = ch * 64 + last
                        fw.op("dve", lambda e, E=E, lc=lc, hh=hh, cidx=cidx, dec=dec: e.tensor_copy(
                            dec[:, hh, cidx:cidx + 1], E[:, lc:lc + 1]), reads=[E], writes=[dec])
                    fw.op("dve", lambda e, E=E, hh=hh, qb=qb: e.tensor_tensor(
                        qb[:, hh, s0:s0 + 128], E[:, :], q[:, hh, s0:s0 + 128], ALU.mult), reads=[E, q], writes=[qb])
        for nm, tl in outs.items():
            fw.dma("sp", getattr(K, nm).ap.rearrange("(h p) t -> p h t", p=128)[:, :, t0:t0 + TT], tl[:, :, :], reads=[tl], tag=nm)
    fw.barrier()
    sb.release(m0)


def gla_o23(K, i, decf, decb):
    fw, sb, T = K.fw, K.sb, K.T
    m0 = sb.mark()
    NG = T // 128
    Sf = sb.alloc([1024], F32, name="Sf")
    Sb = sb.alloc([1024], F32, name="Sb")
    P = sb.alloc([4], F32, name="P")
    fw.op("dve", lambda e: e.memset(Sf[:, :], 0.0), writes=[Sf])
    fw.op("dve", lambda e: e.memset(Sb[:, :], 0.0), writes=[Sb])
    fw.op("dve", lambda e: e.memset(P[:, :], 1.0), writes=[P])
    kdf_t = [sb.alloc([512], BF16, name=f"kdf{j}") for j in range(2)]
    kdb_t = [sb.alloc([512], BF16, name=f"kdb{j}") for j in range(2)]
    v_t = [sb.alloc([1024], BF16, name=f"v{j}") for j in range(2)]

    def states(kdt, vtl, ch):
        p0 = ch * 64
        pa, pb = next_ps(K), next_ps(K)
        for hh in range(4):
            ps = pa if hh < 2 else pb
            col = (hh % 2) * 256
            fw.op("pe", lambda e, ps=ps, col=col, hh=hh: e.matmul(
                ps[:, col:col + 256], kdt[p0:p0 + 64, hh * 128:(hh + 1) * 128], vtl[p0:p0 + 64, hh * 256:(hh + 1) * 256],
                start=True, stop=True), reads=[kdt, vtl], writes=[ps])
        return pa, pb

    def psl(pa, pb, hh):
        ps = pa if hh < 2 else pb
        col = (hh % 2) * 256
        return ps, ps[:, col:col + 256]

    def load_p1(g):
        fw.dma("sp", kdf_t[g % 2][:, :], K.KDF[g * 128:(g + 1) * 128, :], writes=[kdf_t[g % 2]])
        fw.dma("sp", kdb_t[g % 2][:, :], K.KDB[g * 128:(g + 1) * 128, :], writes=[kdb_t[g % 2]])
        fw.dma("sp", v_t[g % 2][:, :], K.VT[g * 128:(g + 1) * 128, :], writes=[v_t[g % 2]])

    load_p1(0)
    for g in range(NG):
        kdf, kdb, v = kdf_t[g % 2], kdb_t[g % 2], v_t[g % 2]
        if g + 1 < NG:
            load_p1(g + 1)
        for ch in range(2):
            c = 2 * g + ch
            pa, pb = states(kdf, v, ch)
            for hh in range(4):
                ps, pv = psl(pa, pb, hh)
                fw.op("dve", lambda e, hh=hh, pv=pv, c=c: e.scalar_tensor_tensor(
                    Sf[:, hh * 256:(hh + 1) * 256], Sf[:, hh * 256:(hh + 1) * 256], decf[:, hh, c:c + 1], pv,
                    ALU.mult, ALU.add), reads=[Sf, decf, ps], writes=[Sf])
            pa, pb = states(kdb, v, ch)
            for hh in range(4):
                ps, pv = psl(pa, pb, hh)
                fw.op("dve", lambda e, hh=hh, pv=pv: e.scalar_tensor_tensor(
                    Sb[:, hh * 256:(hh + 1) * 256], pv, P[:, hh:hh + 1], Sb[:, hh * 256:(hh + 1) * 256],
                    ALU.mult, ALU.add), reads=[Sb, P, ps], writes=[Sb])
            fw.op("dve", lambda e, c=c: e.tensor_tensor(P[:, :], P[:, :], decb[:, :, c], ALU.mult),
                  reads=[P, decb], writes=[P])
    fw.dma("sp", K.GST[:, 0:1024], Sf[:, :], reads=[Sf], writes=[K.GST])
    fw.dma("sp", K.GST[:, 1024:2048], Sb[:, :], reads=[Sb], writes=[K.GST])
    fw.collective("AllGather", ALU.bypass, K.groups, K.GST.ap, K.GSG.ap, reads=[K.GST], writes=[K.GSG])
    i_f = sb.alloc([1024], F32, name="i_f")
    i_b = sb.alloc([1024], F32, name="i_b")
    fw.dma("sp", i_f[:, :], K.GSG[0:128, 0:1024], reads=[K.GSG], writes=[i_f])
    fw.dma("sp", i_b[:, :], K.GSG[128:256, 1024:2048], reads=[K.GSG], writes=[i_b])
    fw.op("dve", lambda e: e.tensor_scalar(Sf[:, :], i_f[:, :], K.flg[:, 0:1], None, ALU.mult), reads=[i_f, K.flg], writes=[Sf])
    fw.op("dve", lambda e: e.tensor_scalar(Sb[:, :], i_b[:, :], K.flg[:, 1:2], None, ALU.mult), reads=[i_b, K.flg], writes=[Sb])
    rp_t = [sb.alloc([1024], BF16, name=f"rp{j}") for j in range(2)]
    def load_bw(g):
        fw.dma("sp", kdb_t[g % 2][:, :], K.KDB[g * 128:(g + 1) * 128, :], writes=[kdb_t[g % 2]])
        fw.dma("sp", v_t[g % 2][:, :], K.VT[g * 128:(g + 1) * 128, :], writes=[v_t[g % 2]])

    load_bw(NG - 1)
    for g in reversed(range(NG)):
        kdb, v = kdb_t[g % 2], v_t[g % 2]
        if g - 1 >= 0:
            load_bw(g - 1)
        for ch in (1, 0):
            c = 2 * g + ch
            rp = rp_t[c % 2]
            fw.op("act", lambda e, rp=rp: e.copy(rp[:, :], Sb[:, :]), reads=[Sb], writes=[rp])
            fw.dma("sp", K.RPB[c], rp[:, :], reads=[rp])
            pa, pb = states(kdb, v, ch)
            for hh in range(4):
                ps, pv = psl(pa, pb, hh)
                fw.op("dve", lambda e, hh=hh, pv=pv, c=c: e.scalar_tensor_tensor(
                    Sb[:, hh * 256:(hh + 1) * 256], Sb[:, hh * 256:(hh + 1) * 256], decb[:, hh, c:c + 1], pv,
                    ALU.mult, ALU.add), reads=[Sb, decb, ps], writes=[Sb])
    fw.barrier()
    names = ("KCF", "QCF", "QBF", "KCB", "QCB", "QBB")
    in_t = [{nm: sb.alloc([4, 128], BF16, name=f"i{nm}{j}") for nm in names} for j in range(2)]
    rpb_t = [sb.alloc([2, 1024], BF16, name=f"rpb{j}") for j in range(2)]
    sg_t = [sb.alloc([8, 128], BF16, name=f"sg{j}") for j in range(2)]
    rpf = [sb.alloc([1024], BF16, name=f"rpf{j}") for j in range(2)]
    af = sb.alloc([4, 128], BF16, name="af")
    ab = sb.alloc([4, 128], BF16, name="ab")
    sq = sb.alloc([2, 512], BF16, name="osq")
    rstd = sb.alloc([512], F32, name="orstd")
    tmpo = sb.alloc([512], F32, name="otmp")
    at = [sb.alloc([8, 128], BF16, name=f"oat{j}") for j in range(2)]
    mf = K.cst["mf64"].ap.rearrange("p (o n) -> p o n", o=1).to_broadcast([128, 4, 128])
    mb = K.cst["mb64"].ap.rearrange("p (o n) -> p o n", o=1).to_broadcast([128, 4, 128])
    def load_fw(g):
        j = g % 2
        tsl = slice(g * 128, (g + 1) * 128)
        for nm in names:
            fw.dma("sp", in_t[j][nm][:, :, :], getattr(K, nm).ap.rearrange("(h p) t -> p h t", p=128)[:, :, tsl], writes=[in_t[j][nm]])
        fw.dma("sp", kdf_t[j][:, :], K.KDF[tsl, :], writes=[kdf_t[j]])
        fw.dma("sp", v_t[j][:, :], K.VT[tsl, :], writes=[v_t[j]])
        fw.dma("sp", rpb_t[j][:, :, :], K.RPB.ap[2 * g:2 * g + 2].rearrange("c p n -> p c n"), writes=[rpb_t[j]])
        fw.dma("sp", sg_t[j][:, :, :], K.SGT.ap.rearrange("(c p) t -> p c t", p=128)[:, :, tsl], writes=[sg_t[j]])

    load_fw(0)
    for g in range(NG):
        j = g % 2
        it, kdf, v, rpb, sgl = in_t[j], kdf_t[j], v_t[j], rpb_t[j], sg_t[j]
        tsl = slice(g * 128, (g + 1) * 128)
        if g + 1 < NG:
            load_fw(g + 1)
        psF, psB = next_ps(K), next_ps(K)
        for hh in range(4):
            fw.op("pe", lambda e, hh=hh, it=it, psF=psF: e.matmul(psF[:, hh * 128:(hh + 1) * 128], it["KCF"][:, hh, :], it["QCF"][:, hh, :],
                                                           start=True, stop=True), reads=[it["KCF"], it["QCF"]], writes=[psF])
        for hh in range(4):
            fw.op("pe", lambda e, hh=hh, it=it, psB=psB: e.matmul(psB[:, hh * 128:(hh + 1) * 128], it["KCB"][:, hh, :], it["QCB"][:, hh, :],
                                                           start=True, stop=True), reads=[it["KCB"], it["QCB"]], writes=[psB])
        fw.op("dve", lambda e, psF=psF: e.tensor_tensor(af[:, :, :], psF[:, 0:512].rearrange("p (h n) -> p h n", h=4), mf, ALU.mult),
              reads=[psF, K.cst["mf64"]], writes=[af])
        fw.op("dve", lambda e, psB=psB: e.tensor_tensor(ab[:, :, :], psB[:, 0:512].rearrange("p (h n) -> p h n", h=4), mb, ALU.mult),
              reads=[psB, K.cst["mb64"]], writes=[ab])
        for ch in range(2):
            c = 2 * g + ch
            fw.op("act", lambda e, ch=ch: e.copy(rpf[ch][:, :], Sf[:, :]), reads=[Sf], writes=[rpf[ch]])
            pa, pb = states(kdf, v, ch)
            for hh in range(4):
                ps, pv = psl(pa, pb, hh)
                fw.op("dve", lambda e, hh=hh, pv=pv, c=c: e.scalar_tensor_tensor(
                    Sf[:, hh * 256:(hh + 1) * 256], Sf[:, hh * 256:(hh + 1) * 256], decf[:, hh, c:c + 1], pv,
                    ALU.mult, ALU.add), reads=[Sf, decf, ps], writes=[Sf])
        po = []
        for half in range(2):
            p = next_ps(K)
            po.append(p)
            for hh in range(4):
                cols = hh * 128
                ec = hh * 256 + half * 128
                for ch in range(2):
                    cc = cols + ch * 64
                    r0 = ch * 64
                    fw.op("pe", lambda e: e.matmul(
                        p[:, cc:cc + 64], v[r0:r0 + 64, ec:ec + 128], af[r0:r0 + 64, hh, r0:r0 + 64], start=True, stop=False),
                        reads=[v, af], writes=[p])
                    fw.op("pe", lambda e: e.matmul(
                        p[:, cc:cc + 64], v[r0:r0 + 64, ec:ec + 128], ab[r0:r0 + 64, hh, r0:r0 + 64], start=False, stop=False),
                        reads=[v, ab], writes=[p])
                    fw.op("pe", lambda e: e.matmul(
                        p[:, cc:cc + 64], rpf[ch][:, ec:ec + 128], it["QBF"][:, hh, r0:r0 + 64],
                        start=False, stop=False), reads=[rpf[ch], it["QBF"]], writes=[p])
                    fw.op("pe", lambda e: e.matmul(
                        p[:, cc:cc + 64], rpb[:, ch, ec:ec + 128], it["QBB"][:, hh, r0:r0 + 64],
                        start=False, stop=True), reads=[rpb, it["QBB"]], writes=[p])
        for half in range(2):
            fw.op("act", lambda e, half=half, p=po[half]: e.activation(sq[:, half, :], p[:, 0:512], AF.Square),
                  reads=[po[half]], writes=[sq])
        rms_rstd(K, sq, 2, 512, 256, rstd)
        a = at[j]
        for half in range(2):
            fw.op("dve", lambda e, p=po[half]: e.tensor_tensor(tmpo[:, :], p[:, 0:512], rstd[:, :], ALU.mult),
                  reads=[po[half], rstd], writes=[tmpo])
            fw.op("dve", lambda e, half=half, a=a, sgl=sgl: e.tensor_tensor(
                a[:, half::2, :], tmpo[:, :].rearrange("p (h n) -> p h n", h=4), sgl[:, half::2, :], ALU.mult),
                reads=[tmpo, sgl], writes=[a])
        fw.dma("sp", K.AT.ap.rearrange("(c p) t -> p c t", p=128)[:, :, tsl], a[:, :, :], reads=[a])
    fw.barrier()
    sb.release(m0)


def make_tables(K):
    fw, sb, T, S2 = K.fw, K.sb, K.T, K.S2
    m0 = sb.mark()
    tabc = K.cst["tabc"]
    CH = min(1024, T)
    pi = sb.alloc([CH], I32, name="tpi")
    pf = sb.alloc([CH], F32, name="tpf")
    u = sb.alloc([CH], F32, name="tu")
    kf = sb.alloc([CH], F32, name="tkf")
    TWO_PI = float(2 * np.pi * (1 - 1e-6))
    for dst, pos, n, cc, pc in ((K.TRC, K.pos_loc, T, 0, 2), (K.TRS, K.pos_loc, T, 0, 3),
                                (K.TQ, K.pos_loc, T, 1, 4), (K.TK, K.pos_all, S2, 1, 4)):
        for c0 in range(0, n, CH):
            fw.dma("sp", pi[:, :], pos[:, c0:c0 + CH].partition_broadcast(128), writes=[pi])
            fw.op("dve", lambda e: e.tensor_copy(pf[:, :], pi[:, :]), reads=[pi], writes=[pf])
            fw.op("dve", lambda e: e.tensor_scalar(u[:, :], pf[:, :], tabc[:, cc:cc + 1], tabc[:, pc:pc + 1], ALU.mult, ALU.add),
                  reads=[pf, tabc], writes=[u])
            fw.op("dve", lambda e: e.tensor_copy(pi[:, :], u[:, :]), reads=[u], writes=[pi])
            fw.op("dve", lambda e: e.tensor_copy(kf[:, :], pi[:, :]), reads=[pi], writes=[kf])
            fw.op("dve", lambda e: e.tensor_tensor(u[:, :], u[:, :], kf[:, :], ALU.subtract), reads=[u, kf], writes=[u])
            fw.op("dve", lambda e: e.tensor_scalar(kf[:, :], u[:, :], 0.5, None, ALU.is_gt), reads=[u], writes=[kf])
            fw.op("dve", lambda e: e.tensor_tensor(u[:, :], u[:, :], kf[:, :], ALU.subtract), reads=[u, kf], writes=[u])
            fw.op("act", lambda e: e.activation(u[:, :], u[:, :], AF.Sin, scale=TWO_PI), reads=[u], writes=[u])
            fw.dma("sp", dst[:, c0:c0 + CH], u[:, :], reads=[u])
    fw.barrier()
    sb.release(m0)


def even_layer(K, i):
    sb = K.sb
    mL = sb.mark()
    c = Ctx()
    K.ev = c
    fw = K.fw
    c.ones96 = sb.alloc([128], BF16, name="ones96")
    fw.op("dve", lambda e: e.tensor_copy(c.ones96[:, :], K.cst["ones96"][:, :]), reads=[K.cst["ones96"]], writes=[c.ones96])
    c.sel = sb.alloc([128], BF16, name="selb")
    fw.op("dve", lambda e: e.tensor_copy(c.sel[:, :], K.cst["sel"][:, :]), reads=[K.cst["sel"]], writes=[c.sel])
    c.mtmp = [(sb.alloc([1, 512], BF16, name=f"msq{j}"), sb.alloc([512], F32, name=f"mrstd{j}"),
               sb.alloc([512], F32, name=f"my{j}"), sb.alloc([512], BF16, name=f"my2{j}")) for j in range(2)]
    c.mcnt = 0
    skip = K.dbg.get("skip", ())
    even_e1(K, i)
    if "e2" not in skip:
        even_e2(K, i)
    if "e3" not in skip:
        even_e3(K, i)
    if "e4" not in skip:
        even_e4(K, i)
    sb.release(mL)


def head_gain_col(K, vec96, scale, name):
    fw, sb = K.fw, K.sb
    t = sb.alloc([1], F32, name=name)
    v = vec96.rearrange("(p o) -> p o", o=1)
    fw.dma("sp", t[0:96, :], v[0:96, :], writes=[t])
    fw.dma("sp", t[96:112, :], v[80:96, :], writes=[t])
    fw.dma("sp", t[112:128, :], v[64:80, :], writes=[t])
    if scale != 1.0:
        fw.op("dve", lambda e: e.tensor_scalar(t[:, :], t[:, :], float(scale), None, ALU.mult), reads=[t], writes=[t])
    return t


def mla_post(K, ps, gcol, tab, TT, out):
    fw, c = K.fw, K.ev
    msq, mrstd, my, my2 = c.mtmp[c.mcnt % 2]
    c.mcnt += 1
    fw.op("act", lambda e: e.activation(msq[:, 0, 0:TT], ps[:, 0:TT], AF.Square), reads=[ps], writes=[msq])
    rms_rstd(K, msq, 1, TT, 96, mrstd, ones=c.ones96)
    fw.op("dve", lambda e: e.scalar_tensor_tensor(my[:, 0:TT], ps[:, 0:TT], gcol[:, 0:1], tab[:, 0:TT], ALU.mult, ALU.mult),
          reads=[ps, gcol, tab], writes=[my])
    fw.op("dve", lambda e: e.tensor_tensor(my2[:, 0:TT], my[:, 0:TT], mrstd[:, 0:TT], ALU.mult),
          reads=[my, mrstd], writes=[my2])
    p2 = next_ps(K)
    fw.op("pe", lambda e: e.matmul(p2[0:96, 0:TT], c.sel[:, 0:96], my2[:, 0:TT], start=True, stop=True),
          reads=[c.sel, my2], writes=[p2])
    fw.op("act", lambda e: e.copy(out[0:96, 0:TT], p2[0:96, 0:TT]), reads=[p2], writes=[out])


def norm_chunks(K, src, dst, sq, rstd, gcol, nch, dim, TT):
    fw = K.fw
    for c_ in range(nch):
        fw.op("act", lambda e: e.activation(sq[:, c_, 0:TT], src[:, c_, 0:TT], AF.Square), reads=[src], writes=[sq])
    rms_rstd(K, sq, nch, TT, dim, rstd)
    for c_ in range(nch):
        fw.op("dve", lambda e: e.scalar_tensor_tensor(dst[:, c_, 0:TT], src[:, c_, 0:TT], gcol[:, c_:c_ + 1], rstd[:, 0:TT],
                                                      ALU.mult, ALU.mult), reads=[src, gcol, rstd], writes=[dst])


def even_e1(K, i):
    fw, sb, T = K.fw, K.sb, K.T
    m0 = sb.mark()
    TT = min(512, T)
    NS = TT // 128
    w = sb.alloc([8, 2720], BF16, name="w_in_even")
    load_w_bf16(K, w, K.w["w_in_even"][i], 8, 2720)
    wrot = sb.alloc([8, 1024], BF16, name="wrot")
    for kc in range(8):
        src = w[:, kc, 672:1696].rearrange("p (h two d) -> p h two d", two=2, d=32)
        dst = wrot[:, kc, :].rearrange("p (h two d) -> p h two d", two=2, d=32)
        fw.op("act", lambda e: e.mul(dst[:, :, 0, :], src[:, :, 1, :], -1.0), reads=[w], writes=[wrot])
        fw.op("dve", lambda e: e.tensor_copy(dst[:, :, 1, :], src[:, :, 0, :]), reads=[w], writes=[wrot])
    wq = sb.alloc([3, 768], BF16, name="wq")
    load_w_bf16(K, wq, K.w["mla_w_uq"][i], 3, 768)
    wqx = sb.alloc([3, 1024], BF16, name="wqx")
    for kc in range(3):
        src = wq[:, kc, :].rearrange("p (h d) -> p h d", d=96)
        dst = wqx[:, kc, :].rearrange("p (h d) -> p h d", d=128)
        fw.op("dve", lambda e: e.tensor_copy(dst[:, :, 0:96], src[:, :, 0:96]), reads=[wq], writes=[wqx])
        fw.op("act", lambda e: e.mul(dst[:, :, 96:112], src[:, :, 80:96], -1.0), reads=[wq], writes=[wqx])
        fw.op("dve", lambda e: e.tensor_copy(dst[:, :, 112:128], src[:, :, 64:80]), reads=[wq], writes=[wqx])
    g1 = sb.alloc([8], F32, name="eg1")
    load_col(K, g1, K.w["mix_norm_even"][i], 8)
    gq3 = sb.alloc([3], F32, name="gq3")
    load_col(K, gq3, K.w["mla_q_norm"][i], 3)
    gkv2 = sb.alloc([2], F32, name="gkv2")
    load_col(K, gkv2, K.w["mla_kv_norm"][i], 2)
    gr4 = sb.alloc([4], F32, name="gr4")
    load_col(K, gr4, K.w["ret_out_norm"][i].rearrange("h e -> (h e)"), 4)
    Gq = head_gain_col(K, K.w["mla_q_head_norm"][i], 96 ** -0.5, "Gq")
    XTv = K.xT.ap.rearrange("(c p) t -> p c t", p=128)
    x_t = [sb.alloc([8, TT], F32, name=f"ex{j}") for j in range(2)]
    h = sb.alloc([8, TT], BF16, name="eh")
    sq = sb.alloc([8, TT], BF16, name="esq")
    rstd = sb.alloc([TT], F32, name="erstd")
    rstd2 = sb.alloc([TT], F32, name="erstd2")
    cq = sb.alloc([3, TT], F32, name="cq")
    cqn = sb.alloc([3, TT], BF16, name="cqn")
    ckn = sb.alloc([2, TT], F32, name="ckn")
    kr = sb.alloc([TT], F32, parts=32, name="kr")
    tq = sb.alloc([TT], F32, name="tq")
    tcs = sb.alloc([TT], F32, name="tcos")
    tsn = sb.alloc([TT], F32, name="tsin")
    qo = [sb.alloc([TT], BF16, name=f"qo{j}") for j in range(2)]
    t1 = sb.alloc([TT], F32, name="t1")
    t2 = sb.alloc([TT], F32, name="t2")
    rq = sb.alloc([4, TT], BF16, name="rq")
    rk = sb.alloc([4, TT], BF16, name="rk")
    rv = [sb.alloc([512], BF16, name=f"rv{j}") for j in range(2)]
    sg = sb.alloc([4, TT], BF16, name="esg")
    tq_t = [tq, sb.alloc([TT], F32, name="tq1")]
    tcs_t = [tcs, sb.alloc([TT], F32, name="tcos1")]
    tsn_t = [tsn, sb.alloc([TT], F32, name="tsin1")]

    def load_e1(ti):
        t0 = ti * TT
        fw.dma("sp", x_t[ti % 2][:, :, :], XTv[:, :, t0:t0 + TT], writes=[x_t[ti % 2]])
        fw.dma("sp", tq_t[ti % 2][:, :], K.TQ[:, t0:t0 + TT], writes=[tq_t[ti % 2]])
        fw.dma("sp", tcs_t[ti % 2][:, :], K.TRC[:, t0:t0 + TT], writes=[tcs_t[ti % 2]])
        fw.dma("sp", tsn_t[ti % 2][:, :], K.TRS[:, t0:t0 + TT], writes=[tsn_t[ti % 2]])

    load_e1(0)
    for ti in range(T // TT):
        t0 = ti * TT
        xt, tq, tcs, tsn = x_t[ti % 2], tq_t[ti % 2], tcs_t[ti % 2], tsn_t[ti % 2]
        if ti + 1 < T // TT:
            load_e1(ti + 1)
        norm_tile(K, xt, h, sq, rstd, g1, TT)
        for c_ in range(3):
            ps = proj_fm(K, w, c_ * 128, 128, h, TT)
            fw.op("act", lambda e: e.copy(cq[:, c_, :], ps[:, 0:TT]), reads=[ps], writes=[cq])
        norm_chunks(K, cq, cqn, sq, rstd2, gq3, 3, 384, TT)
        fw.dma("sp", K.CQN.ap.rearrange("(c p) t -> p c t", p=128)[:, :, t0:t0 + TT], cqn[:, :, :], reads=[cqn])
        for c_ in range(2):
            ps = proj_fm(K, w, 384 + c_ * 128, 128, h, TT)
            fw.op("act", lambda e: e.copy(cq[:, c_, :], ps[:, 0:TT]), reads=[ps], writes=[cq])
        norm_chunks(K, cq, ckn, sq, rstd2, gkv2, 2, 256, TT)
        fw.dma("sp", K.LAT[ti].ap[0:256, :].rearrange("(c p) t -> p c t", p=128), ckn[:, :, :], reads=[ckn])
        ps = proj_fm(K, w, 640, 32, h, TT)
        fw.op("act", lambda e: e.copy(kr[:, :], ps[0:32, 0:TT]), reads=[ps], writes=[kr])
        fw.dma("sp", K.LAT[ti][256:288, :], kr[:, :], reads=[kr])
        for hh in range(8):
            ps = next_ps(K)
            for kc in range(3):
                fw.op("pe", lambda e: e.matmul(ps[:, 0:TT], wqx[:, kc, hh * 128:(hh + 1) * 128], cqn[:, kc, :],
                                               start=(kc == 0), stop=(kc == 2)), reads=[wqx, cqn], writes=[ps])
            o = qo[hh % 2]
            mla_post(K, ps, Gq, tq, TT, o)
            fw.dma("sp", K.QT.ap[hh, :, t0:t0 + TT], o[0:96, :], reads=[o])
        for which, c0, r0, dst, scl in (("q", 672, 0, rq, 1.0), ("k", 1184, 512, rk, 0.125)):
            for c_ in range(4):
                pa = proj_fm(K, w, c0 + c_ * 128, 128, h, TT)
                pr = proj_fm(K, wrot, r0 + c_ * 128, 128, h, TT)
                fw.op("dve", lambda e: e.scalar_tensor_tensor(t1[:, :], pa[:, 0:TT], float(scl), tcs[:, :], ALU.mult, ALU.mult),
                      reads=[pa, tcs], writes=[t1])
                fw.op("dve", lambda e: e.scalar_tensor_tensor(t2[:, :], pr[:, 0:TT], float(scl), tsn[:, :], ALU.mult, ALU.mult),
                      reads=[pr, tsn], writes=[t2])
                fw.op("dve", lambda e: e.tensor_tensor(dst[:, c_, :], t1[:, :], t2[:, :], ALU.add), reads=[t1, t2], writes=[dst])
        fw.dma("sp", K.RQ.ap.rearrange("(c p) t -> p c t", p=128)[:, :, t0:t0 + TT], rq[:, :, :], reads=[rq])
        fw.dma("sp", K.RK.ap.rearrange("(c p) t -> p c t", p=128)[:, :, t0:t0 + TT], rk[:, :, :], reads=[rk])
        for s_ in range(NS):
            ps = proj_tm(K, w, 1696, 512, h, s_ * 128)
            r = rv[s_ % 2]
            fw.op("act", lambda e: e.copy(r[:, :], ps[:, 0:512]), reads=[ps], writes=[r])
            fw.dma("sp", K.RV[t0 + s_ * 128:t0 + (s_ + 1) * 128, :], r[:, :], reads=[r])
        for c_ in range(4):
            ps = proj_fm(K, w, 2208 + c_ * 128, 128, h, TT)
            fw.op("act", lambda e: e.activation(t1[:, :], ps[:, 0:TT], AF.Silu), reads=[ps], writes=[t1])
            fw.op("dve", lambda e: e.tensor_scalar(sg[:, c_, :], t1[:, :], gr4[:, c_:c_ + 1], None, ALU.mult),
                  reads=[t1, gr4], writes=[sg])
        fw.dma("sp", K.SGT.ap[0:512, :].rearrange("(c p) t -> p c t", p=128)[:, :, t0:t0 + TT], sg[:, :, :], reads=[sg])
    fw.barrier()
    for j in range(len(K.LAT)):
        fw.collective("AllGather", ALU.bypass, K.groups, K.LAT[j].ap, K.LATG[j].ap)
    fw.barrier()
    sb.release(m0)


def even_e2(K, i):
    fw, sb, T, S2 = K.fw, K.sb, K.T, K.S2
    c = K.ev
    m0 = sb.mark()
    TT = min(512, T)
    wkv = sb.alloc([2, 1024], BF16, name="wkv")
    load_w_bf16(K, wkv, K.w["mla_w_ukv"][i], 2, 1024)
    wkx = sb.alloc([2, 1024], BF16, name="wkx")
    wv = sb.alloc([2, 512], BF16, name="wv")
    fw.op("dve", lambda e: e.memset(wkx[:, :, :], 0.0), writes=[wkx])
    for kc in range(2):
        src = wkv[:, kc, :].rearrange("p (h d) -> p h d", d=128)
        fw.op("dve", lambda e: e.tensor_copy(wkx[:, kc, :].rearrange("p (h d) -> p h d", d=128)[:, :, 0:64], src[:, :, 0:64]),
              reads=[wkv], writes=[wkx])
        fw.op("act", lambda e: e.copy(wv[:, kc, :].rearrange("p (h d) -> p h d", d=64), src[:, :, 64:128]),
              reads=[wkv], writes=[wv])
    rmat = sb.alloc([128], BF16, parts=32, name="rmat")
    fw.op("dve", lambda e: e.tensor_copy(rmat[:, :], K.cst["ropemat"][0:32, :]), reads=[K.cst["ropemat"]], writes=[rmat])
    Gk = head_gain_col(K, K.w["mla_k_head_norm"][i], 1.0, "Gk")
    lat = [sb.alloc([2, TT], F32, name=f"lat{j}") for j in range(2)]
    krf = [sb.alloc([TT], F32, parts=32, name=f"krf{j}") for j in range(2)]
    ckb = sb.alloc([2, TT], BF16, name="ckb")
    krb = sb.alloc([TT], BF16, parts=32, name="krb")
    tk = [sb.alloc([TT], F32, name=f"tk{j}") for j in range(2)]
    ko = [sb.alloc([TT], BF16, name=f"ko{j}") for j in range(2)]
    vx = [sb.alloc([8, 65], BF16, name=f"vx{j}") for j in range(2)]
    for j in range(2):
        fw.op("dve", lambda e: e.memset(vx[j][:, :, :], 1.0), writes=[vx[j]])
    def load_e2(kt):
        k0 = kt * TT
        rr = k0 // T
        lj = (k0 - rr * T) // TT
        fw.dma("sp", lat[kt % 2][:, :, :], K.LATG[lj].ap[rr * 288:rr * 288 + 256, :].rearrange("(c p) t -> p c t", p=128), writes=[lat[kt % 2]])
        fw.dma("sp", krf[kt % 2][:, :], K.LATG[lj][rr * 288 + 256:rr * 288 + 288, :], writes=[krf[kt % 2]])
        fw.dma("sp", tk[kt % 2][:, :], K.TK[:, k0:k0 + TT], writes=[tk[kt % 2]])

    load_e2(0)
    for kt in range(S2 // TT):
        k0 = kt * TT
        la, kf_, tkt = lat[kt % 2], krf[kt % 2], tk[kt % 2]
        if kt + 1 < S2 // TT:
            load_e2(kt + 1)
        fw.op("act", lambda e: e.copy(ckb[:, :, :], la[:, :, :]), reads=[la], writes=[ckb])
        fw.op("dve", lambda e: e.tensor_copy(krb[:, :], kf_[:, :]), reads=[kf_], writes=[krb])
        for hh in range(8):
            ps = next_ps(K)
            for kc in range(2):
                fw.op("pe", lambda e: e.matmul(ps[:, 0:TT], wkx[:, kc, hh * 128:(hh + 1) * 128], ckb[:, kc, :],
                                               start=(kc == 0), stop=False), reads=[wkx, ckb], writes=[ps])
            fw.op("pe", lambda e: e.matmul(ps[:, 0:TT], rmat[0:32, :], krb[0:32, :], start=False, stop=True),
                  reads=[rmat, krb], writes=[ps])
            o = ko[hh % 2]
            mla_post(K, ps, Gk, tkt, TT, o)
            fw.dma("sp", K.KT.ap[hh, :, k0:k0 + TT], o[0:96, :], reads=[o])
        for s_ in range(TT // 128):
            ps = next_ps(K)
            for kc in range(2):
                fw.op("pe", lambda e: e.matmul(ps[:, 0:512], ckb[:, kc, s_ * 128:(s_ + 1) * 128], wv[:, kc, :],
                                               start=(kc == 0), stop=(kc == 1)), reads=[ckb, wv], writes=[ps])
            v = vx[s_ % 2]
            fw.op("act", lambda e: e.copy(v[:, :, 0:64], ps[:, 0:512].rearrange("p (h d) -> p h d", d=64)), reads=[ps], writes=[v])
            fw.dma("sp", K.VX[k0 + s_ * 128:k0 + (s_ + 1) * 128, :], v[:, :, :].rearrange("p h d -> p (h d)"), reads=[v])
    fw.barrier()
    sb.release(m0)


EXP_SHIFT = -8.0


def even_e3(K, i):
    fw, sb, T, S2 = K.fw, K.sb, K.T, K.S2
    m0 = sb.mark()
    TQ = min(512, T)
    NKT = S2 // 128
    NP = NKT // 2
    NQT = T // TQ
    kt_t = [sb.alloc([S2], BF16, name=f"akt{j}") for j in range(2)]
    vx_t = [sb.alloc([NKT, 65], BF16, name=f"avx{j}") for j in range(2)]
    q_t = [sb.alloc([T], BF16, name=f"aq{j}") for j in range(2)]
    pt = [sb.alloc([2, TQ], BF16, name=f"ap{j}") for j in range(3)]
    o65 = sb.alloc([512], F32, name="o65")
    rden = sb.alloc([512], F32, name="rden")
    ao = [sb.alloc([512], BF16, name=f"ao{j}") for j in range(2)]
    shift = sb.alloc([1], F32, name="shift")
    fw.op("dve", lambda e: e.memset(shift[:, :], EXP_SHIFT), writes=[shift])
    sel65 = K.cst["sel65"]
    VXv = K.VX.ap.rearrange("(kt p) c -> p kt c", p=128)
    spair = [Tile(K.psum_full[:, b * 1024:(b + 1) * 1024].rearrange("p (b n) -> p b n", b=2), f"spair{b}") for b in range(3)]

    def load_head(hh):
        fw.dma("sp", kt_t[hh % 2][0:96, :], K.KT.ap[hh], writes=[kt_t[hh % 2]])
        fw.dma("sp", vx_t[hh % 2][:, :, :], VXv[:, :, hh * 65:(hh + 1) * 65], writes=[vx_t[hh % 2]])
        fw.dma("sp", q_t[hh % 2][0:96, :], K.QT.ap[hh], writes=[q_t[hh % 2]])

    items = [(hh, qt, j) for hh in range(8) for qt in range(NQT) for j in range(NP)]
    load_head(0)
    load_head(1)
    for idx in range(len(items) + 2):
        if idx < len(items):
            hh, qt, j = items[idx]
            kt_, q_ = kt_t[hh % 2], q_t[hh % 2]
            sp = spair[idx % 3]
            for u in range(2):
                kt = 2 * j + u
                fw.op("pe", lambda e: e.matmul(sp[:, u, 0:TQ], kt_[0:96, kt * 128:(kt + 1) * 128], q_[0:96, qt * TQ:(qt + 1) * TQ],
                                               start=True, stop=True), reads=[kt_, q_], writes=[sp])
        if idx >= 2:
            k = idx - 2
            hh, qt, j = items[k]
            g = hh * NQT + qt
            vx_ = vx_t[hh % 2]
            sp = spair[k % 3]
            p = pt[k % 3]
            po = K.ps[6 + (g % 2)]
            if K.dbg.get("exp2", True):
                fw.op("act", lambda e: e.activation(p[:, :, :], sp[:, :, 0:TQ], AF.Exp, bias=shift[:, 0:1], scale=1.0),
                      reads=[sp, shift], writes=[p])
            else:
                for u in range(2):
                    fw.op("act", lambda e: e.activation(p[:, u, :], sp[:, u, 0:TQ], AF.Exp, bias=shift[:, 0:1], scale=1.0),
                          reads=[sp, shift], writes=[p])
            for u in range(2):
                kt = 2 * j + u
                fw.op("pe", lambda e: e.matmul(po[0:65, 0:TQ], vx_[:, kt, :], p[:, u, :], start=(kt == 0), stop=(kt == NKT - 1)),
                      reads=[vx_, p], writes=[po])
            if j == NP - 1:
                fw.op("act", lambda e: e.copy(o65[0:65, 0:TQ], po[0:65, 0:TQ]), reads=[po], writes=[o65])
                fw.op("pe", lambda e: e.matmul(po[0:64, 0:TQ], sel65[0:65, 0:64], o65[0:65, 0:TQ], start=True, stop=True),
                      reads=[sel65, o65], writes=[po])
                fw.op("dve", lambda e: e.reciprocal(rden[0:64, 0:TQ], po[0:64, 0:TQ]), reads=[po], writes=[rden])
                a = ao[g % 2]
                fw.op("dve", lambda e: e.tensor_tensor(a[0:64, 0:TQ], o65[0:64, 0:TQ], rden[0:64, 0:TQ], ALU.mult),
                      reads=[o65, rden], writes=[a])
                fw.dma("sp", K.AT[hh * 64:(hh + 1) * 64, qt * TQ:(qt + 1) * TQ], a[0:64, 0:TQ], reads=[a])
                if qt == NQT - 1 and hh + 2 < 8:
                    load_head(hh + 2)
    fw.barrier()
    sb.release(m0)


def even_e4(K, i):
    fw, sb, T = K.fw, K.sb, K.T
    m0 = sb.mark()
    NCH = T // 128
    tabc = K.cst["tabc"]
    th = sb.alloc([16], F32, name="rth")
    fw.dma("sp", th[:, 0:8], K.w["ret_theta_fwd"][i:i + 1, :].partition_broadcast(128), writes=[th])
    fw.dma("sp", th[:, 8:16], K.w["ret_theta_bwd"][i:i + 1, :].partition_broadcast(128), writes=[th])
    lg = sb.alloc([16], F32, name="rlg")
    fw.op("act", lambda e: e.activation(lg[:, :], th[:, :], AF.Exp, scale=-float(np.log(2.0))), reads=[th], writes=[lg])
    fw.op("act", lambda e: e.activation(lg[:, :], lg[:, :], AF.Ln, scale=-1.0, bias=1.0), reads=[lg], writes=[lg])
    mask = sb.alloc([8, 128], BF16, name="rmask")
    tm1 = sb.alloc([128], F32, name="rtm1")
    tm2 = sb.alloc([128], F32, name="rtm2")
    zf = sb.alloc([8], F32, name="rzf")
    zb = sb.alloc([8], F32, name="rzb")
    xf = sb.alloc([8, 128], F32, name="rxf")
    xb = sb.alloc([8, 128], F32, name="rxb")
    dec = sb.alloc([2, 512], F32, name="rdec")
    cs = K.cst
    for hh in range(8):
        lf, lb = lg[:, hh:hh + 1], lg[:, 8 + hh:9 + hh]
        fw.op("act", lambda e: e.activation(tm1[:, :], cs["relf"][:, :], AF.Exp, scale=lf), reads=[cs["relf"], lg], writes=[tm1])
        fw.op("dve", lambda e: e.tensor_tensor(tm1[:, :], tm1[:, :], cs["mf128"][:, :], ALU.mult), reads=[tm1, cs["mf128"]], writes=[tm1])
        fw.op("act", lambda e: e.activation(tm2[:, :], cs["relb"][:, :], AF.Exp, scale=lb), reads=[cs["relb"], lg], writes=[tm2])
        fw.op("dve", lambda e: e.tensor_tensor(tm2[:, :], tm2[:, :], cs["mb128"][:, :], ALU.mult), reads=[tm2, cs["mb128"]], writes=[tm2])
        fw.op("dve", lambda e: e.tensor_tensor(mask[:, hh, :], tm1[:, :], tm2[:, :], ALU.add), reads=[tm1, tm2], writes=[mask])
        fw.op("act", lambda e: e.activation(zf[:, hh:hh + 1], tabc[:, 6:7], AF.Exp, scale=lf), reads=[tabc, lg], writes=[zf])
        fw.op("act", lambda e: e.activation(zb[:, hh:hh + 1], tabc[:, 5:6], AF.Exp, scale=lb), reads=[tabc, lg], writes=[zb])
        fw.op("act", lambda e: e.activation(xf[:, hh, :], cs["iota1"][:, :], AF.Exp, scale=lf), reads=[cs["iota1"], lg], writes=[xf])
        fw.op("act", lambda e: e.activation(xb[:, hh, :], cs["iotar"][:, :], AF.Exp, scale=lb), reads=[cs["iotar"], lg], writes=[xb])
        fw.op("act", lambda e: e.activation(dec[:, 0, hh * 64:(hh + 1) * 64], cs["c128"][:, 0:64], AF.Exp, scale=lf), reads=[cs["c128"], lg], writes=[dec])
        fw.op("act", lambda e: e.activation(dec[:, 1, hh * 64:(hh + 1) * 64], cs["c128"][:, 0:64], AF.Exp, scale=lb), reads=[cs["c128"], lg], writes=[dec])
    ident = sb.alloc([128], BF16, name="rident")
    fw.op("dve", lambda e: e.tensor_copy(ident[:, :], cs["ident"][:, :]), reads=[cs["ident"]], writes=[ident])
    ss = [sb.alloc([NCH, 512], BF16, name=f"rss{d}") for d in range(2)]
    rk_t = [sb.alloc([T], BF16, name=f"rrk{j}") for j in range(2)]
    rq_t = [sb.alloc([T], BF16, name=f"rrq{j}") for j in range(2)]
    rv_t = [sb.alloc([NCH, 64], BF16, name=f"rrv{j}") for j in range(2)]
    kz = [sb.alloc([NCH, 64], BF16, name=f"rkz{d}") for d in range(2)]
    RVv = K.RV.ap.rearrange("(c p) n -> p c n", p=128)
    for hh in range(8):
        rk_h, rv_h = rk_t[hh % 2], rv_t[hh % 2]
        fw.dma("sp", rk_h[0:64, :], K.RK[hh * 64:(hh + 1) * 64, :], writes=[rk_h])
        fw.dma("sp", rv_h[:, :, :], RVv[:, :, hh * 64:(hh + 1) * 64], writes=[rv_h])
        for c0 in range(0, NCH, 8):
            nb = min(8, NCH - c0)
            ps = next_ps(K)
            psb = ps.ap.bitcast(BF16)
            for c_ in range(nb):
                fw.op("pe", lambda e: e.transpose(psb[:, c_ * 64:(c_ + 1) * 64], rk_h[0:64, (c0 + c_) * 128:(c0 + c_ + 1) * 128],
                                                  ident[0:64, 0:64]), reads=[rk_h, ident], writes=[ps])
            pv = psb[:, 0:nb * 64].rearrange("p (c d) -> p c d", d=64)
            fw.op("dve", lambda e: e.tensor_scalar(kz[0][:, c0:c0 + nb, :], pv, zf[:, hh:hh + 1], None, ALU.mult),
                  reads=[ps, zf], writes=[kz[0]])
            fw.op("dve", lambda e: e.tensor_scalar(kz[1][:, c0:c0 + nb, :], pv, zb[:, hh:hh + 1], None, ALU.mult),
                  reads=[ps, zb], writes=[kz[1]])
        for c0 in range(0, NCH, 8):
            nb = min(8, NCH - c0)
            for d in range(2):
                ps = next_ps(K)
                for c_ in range(nb):
                    fw.op("pe", lambda e: e.matmul(ps[0:64, c_ * 64:(c_ + 1) * 64], kz[d][:, c0 + c_, :], rv_h[:, c0 + c_, :],
                                                   start=True, stop=True), reads=[kz[d], rv_h], writes=[ps])
                fw.op("act", lambda e: e.copy(ss[d][0:64, c0:c0 + nb, hh * 64:(hh + 1) * 64],
                                              ps[0:64, 0:nb * 64].rearrange("p (c d) -> p c d", d=64)), reads=[ps], writes=[ss[d]])
    R = [sb.alloc([512], F32, name=f"rR{d}") for d in range(2)]
    order = [list(range(NCH)), list(reversed(range(NCH)))]
    for d in range(2):
        fw.op("dve", lambda e: e.memset(R[d][:, :], 0.0), writes=[R[d]])
        for c_ in order[d]:
            fw.op("dve", lambda e: e.tensor_tensor(R[d][0:64, :], R[d][0:64, :], dec[0:64, d, :], ALU.mult), reads=[R[d], dec], writes=[R[d]])
            fw.op("dve", lambda e: e.tensor_tensor(R[d][0:64, :], R[d][0:64, :], ss[d][0:64, c_, :], ALU.add), reads=[R[d], ss[d]], writes=[R[d]])
        fw.dma("sp", K.RST[:, d * 512:(d + 1) * 512], R[d][0:64, :], reads=[R[d]], writes=[K.RST])
    fw.collective("AllGather", ALU.bypass, K.groups, K.RST.ap, K.RSG.ap, reads=[K.RST], writes=[K.RSG])
    ini = [sb.alloc([512], F32, name=f"rini{d}") for d in range(2)]
    fw.dma("sp", ini[0][0:64, :], K.RSG[0:64, 0:512], reads=[K.RSG], writes=[ini[0]])
    fw.dma("sp", ini[1][0:64, :], K.RSG[64:128, 512:1024], reads=[K.RSG], writes=[ini[1]])
    rpt = sb.alloc([512], BF16, name="rrpt")
    for d in range(2):
        fw.op("dve", lambda e: e.tensor_scalar(R[d][0:64, :], ini[d][0:64, :], K.flg[0:64, d:d + 1], None, ALU.mult),
              reads=[ini[d], K.flg], writes=[R[d]])
        for c_ in order[d]:
            fw.op("act", lambda e: e.copy(rpt[0:64, :], R[d][0:64, :]), reads=[R[d]], writes=[rpt])
            fw.op("dve", lambda e: e.tensor_tensor(R[d][0:64, :], R[d][0:64, :], dec[0:64, d, :], ALU.mult), reads=[R[d], dec], writes=[R[d]])
            fw.op("dve", lambda e: e.tensor_tensor(R[d][0:64, :], R[d][0:64, :], ss[d][0:64, c_, :], ALU.add), reads=[R[d], ss[d]], writes=[R[d]])
            fw.op("act", lambda e: e.copy(ss[d][0:64, c_, :], rpt[0:64, :]), reads=[rpt], writes=[ss[d]])
    qx = [sb.alloc([T], BF16, name=f"rqx{d}") for d in range(2)]
    at = [sb.alloc([128], BF16, name=f"rat{j}") for j in range(2)]
    sq = sb.alloc([1, 512], BF16, name="rsq")
    rstd = sb.alloc([512], F32, name="rrstd")
    tmo = sb.alloc([512], F32, name="rtmo")
    sgt = [sb.alloc([512], BF16, name=f"rsg{j}") for j in range(2)]
    oo = [sb.alloc([512], BF16, name=f"roo{j}") for j in range(2)]
    acnt = 0
    GC = min(4, NCH)
    for hh in range(8):
        rk_h, rq_h, rv_h = rk_t[hh % 2], rq_t[hh % 2], rv_t[hh % 2]
        fw.dma("sp", rk_h[0:64, :], K.RK[hh * 64:(hh + 1) * 64, :], writes=[rk_h])
        fw.dma("sp", rq_h[0:64, :], K.RQ[hh * 64:(hh + 1) * 64, :], writes=[rq_h])
        fw.dma("sp", rv_h[:, :, :], RVv[:, :, hh * 64:(hh + 1) * 64], writes=[rv_h])
        for d, xt_ in ((0, xf), (1, xb)):
            fw.op("dve", lambda e: e.tensor_tensor(
                qx[d][0:64, :].rearrange("p (c n) -> p c n", n=128), rq_h[0:64, :].rearrange("p (c n) -> p c n", n=128),
                xt_[0:64, hh, :].rearrange("p (o n) -> p o n", o=1).to_broadcast([64, NCH, 128]), ALU.mult),
                reads=[rq_h, xt_], writes=[qx[d]])
        for g0 in range(0, NCH, GC):
            gi = g0 // GC
            po = K.ps[6 + (gi % 2)]
            W = GC * 128
            sgl = sgt[gi % 2]
            fw.dma("sp", sgl[0:64, 0:W], K.SGT[hh * 64:(hh + 1) * 64, g0 * 128:g0 * 128 + W], writes=[sgl])
            for cc in range(GC):
                c_ = g0 + cc
                csl = slice(c_ * 128, (c_ + 1) * 128)
                ps = next_ps(K)
                fw.op("pe", lambda e: e.matmul(ps[:, 0:128], rk_h[0:64, csl], rq_h[0:64, csl], start=True, stop=True),
                      reads=[rk_h, rq_h], writes=[ps])
                a = at[acnt % 2]
                acnt += 1
                fw.op("dve", lambda e: e.tensor_tensor(a[:, :], ps[:, 0:128], mask[:, hh, :], ALU.mult), reads=[ps, mask], writes=[a])
                osl = slice(cc * 128, (cc + 1) * 128)
                fw.op("pe", lambda e: e.matmul(po[0:64, osl], rv_h[:, c_, :], a[:, :], start=True, stop=False),
                      reads=[rv_h, a], writes=[po])
                fw.op("pe", lambda e: e.matmul(po[0:64, osl], ss[0][0:64, c_, hh * 64:(hh + 1) * 64], qx[0][0:64, csl],
                                               start=False, stop=False), reads=[ss[0], qx[0]], writes=[po])
                fw.op("pe", lambda e: e.matmul(po[0:64, osl], ss[1][0:64, c_, hh * 64:(hh + 1) * 64], qx[1][0:64, csl],
                                               start=False, stop=True), reads=[ss[1], qx[1]], writes=[po])
            fw.op("act", lambda e: e.activation(sq[0:64, 0, 0:W], po[0:64, 0:W], AF.Square), reads=[po], writes=[sq])
            rms_rstd(K, sq, 1, W, 64, rstd, P=64)
            fw.op("dve", lambda e: e.tensor_tensor(tmo[0:64, 0:W], po[0:64, 0:W], rstd[0:64, 0:W], ALU.mult), reads=[po, rstd], writes=[tmo])
            o = oo[gi % 2]
            fw.op("dve", lambda e: e.tensor_tensor(o[0:64, 0:W], tmo[0:64, 0:W], sgl[0:64, 0:W], ALU.mult), reads=[tmo, sgl], writes=[o])
            fw.dma("sp", K.AT[512 + hh * 64:512 + (hh + 1) * 64, g0 * 128:g0 * 128 + W], o[0:64, 0:W], reads=[o])
    fw.barrier()
    sb.release(m0)


_WEIGHT_NAMES = ("ffn_norm", "ffn_w_up", "ffn_conv_w", "ffn_conv_b", "ffn_w_down", "w_out_even", "w_out_odd",
                 "mix_norm_odd", "w_in_odd", "gla_w_gate_fwd", "gla_b_gate_fwd", "gla_w_gate_bwd", "gla_b_gate_bwd",
                 "gla_out_norm", "mix_norm_even", "w_in_even", "mla_q_norm", "mla_kv_norm", "mla_w_uq", "mla_w_ukv",
                 "mla_q_head_norm", "mla_k_head_norm", "ret_theta_fwd", "ret_theta_bwd", "ret_out_norm")


def kernel(**inputs):
    x = np.asarray(inputs["x"], dtype=np.float32)
    pos = np.asarray(inputs["positions"], dtype=np.int32)
    B, S, _ = x.shape
    T = S // 2
    ncores = 2 * B
    groups = [[2 * b, 2 * b + 1] for b in range(B)]
    nc, K = build_program(T, groups, n_layers=4)
    wts = {nm: np.ascontiguousarray(np.asarray(inputs[nm], dtype=np.float32)) for nm in _WEIGHT_NAMES}
    in_maps = []
    for core in range(ncores):
        b, r = core // 2, core % 2
        m = {"xT": np.ascontiguousarray(x[b, r * T:(r + 1) * T, :].T),
             "pos_loc": np.ascontiguousarray(pos[b, r * T:(r + 1) * T][None, :]),
             "pos_all": np.ascontiguousarray(pos[b][None, :]),
             "flags": np.tile(np.array([[r, 1 - r, 0, 0]], np.float32), (128, 1)),
             "consts": CONST_ARR}
        m.update(wts)
        in_maps.append(m)
    res = run_bass_kernel_spmd(nc, in_maps, core_ids=list(range(ncores)))
    out = np.empty((B, S, D), np.float32)
    for core in range(ncores):
        b, r = core // 2, core % 2
        out[b, r * T:(r + 1) * T, :] = np.asarray(res.results[core]["yT"]).T
    return out
```

```python
import sys
import numpy as np
from contextlib import ExitStack
import concourse.bass as bass
import concourse.mybir as mybir
from concourse.bass_utils import run_bass_kernel_spmd

F32 = mybir.dt.float32
BF16 = mybir.dt.bfloat16
I32 = mybir.dt.int32
AF = mybir.ActivationFunctionType
ALU = mybir.AluOpType

D = 1024
DFF = 2816
NGC = DFF // 128
EPS = 1e-6


class Buf:
    __slots__ = ("name", "w", "r")

    def __init__(self, name=""):
        self.name = name
        self.w = []
        self.r = []


class Tile:
    def __init__(self, ap, name=""):
        self.ap = ap
        self.b = Buf(name)

    def __getitem__(self, k):
        return self.ap[k]


class _Rec:
    def __init__(self):
        self.call = None

    def __getattr__(self, name):
        def f(*a, **kw):
            self.call = (name, a, kw)
            return self
        return f


class FW:
    ENGS = ("pe", "act", "dve", "pool", "sp")
    NDMA = 36
    NSW = 8
    NEPOCH = 1

    def __init__(self, nc, stack):
        self.nc = nc
        self.streams = {e: [] for e in self.ENGS}
        self.sem = {}
        for ep in range(self.NEPOCH):
            for e in self.ENGS:
                self.sem[(e, ep)] = stack.enter_context(nc.semaphore(f"s_{e}_{ep}"))
        self.dsem = [stack.enter_context(nc.semaphore(f"s_dma_{i}")) for i in range(self.NDMA)]
        self.dval = [0] * self.NDMA
        self.ccsem = stack.enter_context(nc.semaphore("s_cc"))
        self.ccval = 0
        self.dnext = 0
        self.dnext_sw = 0
        self.epoch = 0
        self.cnt = {e: 0 for e in self.ENGS}
        self.seen = {}
        self.n_ops = {e: 0 for e in self.ENGS}

    def _wait(self, eng, ev):
        semk, val = ev
        if semk[0] == "E":
            if semk[2] != self.epoch:
                return
            if semk[1] == eng and eng == "pe":
                return
        key = (eng, semk)
        if self.seen.get(key, 0) >= val:
            return
        self.seen[key] = val
        if semk[0] == "E":
            s = self.sem[(semk[1], semk[2])]
        elif semk[0] == "C":
            s = self.ccsem
        else:
            s = self.dsem[semk[1]]
        self.streams[eng].append(lambda e, s=s, val=val: e.wait_ge(s, val))

    @staticmethod
    def _bl(ts):
        return [t.b if isinstance(t, Tile) else t for t in ts]

    def _deps(self, reads, writes):
        deps = []
        for b in reads:
            deps.extend(b.w)
        for b in writes:
            deps.extend(b.w)
            deps.extend(b.r)
        return deps

    def _commit(self, ev, reads, writes):
        for b in reads:
            b.r.append(ev)
            if len(b.r) > 48:
                b.r = b.r[-48:]
        for b in writes:
            b.w = [ev]
            b.r = []

    def op(self, eng, fn, reads=(), writes=()):
        reads = self._bl(reads)
        writes = self._bl(writes)
        for ev in self._deps(reads, writes):
            self._wait(eng, ev)
        self.cnt[eng] += 1
        ev = (("E", eng, self.epoch), self.cnt[eng])
        s = self.sem[(eng, self.epoch)]
        ln = sys._getframe(1).f_lineno
        rec = _Rec()
        fn(rec)
        call = rec.call
        self.streams[eng].append(lambda e, call=call, s=s, ln=ln: getattr(e, call[0])(*call[1], **call[2]).then_inc(s, 1).annotate(f"L{ln}"))
        self.n_ops[eng] += 1
        self._commit(ev, reads, writes)
        return ev

    def dma(self, q, out, in_, reads=(), writes=(), tag="", **kw):
        reads = self._bl(reads)
        writes = self._bl(writes)
        if q == "pool":
            i = self.NDMA - self.NSW + self.dnext_sw
            self.dnext_sw = (self.dnext_sw + 1) % self.NSW
        else:
            i = self.dnext
            self.dnext = (self.dnext + 1) % (self.NDMA - self.NSW)
        if self.dval[i] > 0:
            self._wait(q, (("D", i), self.dval[i]))
        for ev in self._deps(reads, writes):
            self._wait(q, ev)
        self.dval[i] += 16
        ev = (("D", i), self.dval[i])
        s = self.dsem[i]
        self.streams[q].append(
            lambda e, s=s, out=out, in_=in_, kw=kw, ln=sys._getframe(1).f_lineno, tag=tag: e.dma_start(
                out=out, in_=in_, **kw).then_inc(s, 16).annotate(f"L{ln}{tag}"))
        self.n_ops[q] += 1
        self._commit(ev, reads, writes)
        return ev

    def collective(self, kind, op, groups, in_ap, out_ap, reads=(), writes=()):
        reads = self._bl(reads)
        writes = self._bl(writes)
        eng = "pool"
        for ev in self._deps(reads, writes):
            self._wait(eng, ev)
        self.ccval += 1
        ev = (("C",), self.ccval)
        s = self.ccsem
        self.streams[eng].append(
            lambda e, s=s: e.collective_compute(kind, op, replica_groups=groups, ins=[in_ap],
                                                outs=[out_ap]).then_inc(s, 1))
        self._commit(ev, reads, writes)
        return ev

    def barrier(self):
        for i in range(self.NDMA):
            if self.dval[i] > 0:
                self._wait("pool", (("D", i), self.dval[i]))
        for e in self.ENGS:
            if e != "pool" and self.cnt[e] > 0:
                self._wait("pool", (("E", e, self.epoch), self.cnt[e]))
        if self.ccval > 0:
            self._wait("pool", (("C",), self.ccval))
        self.cnt["pool"] += 1
        s = self.sem[("pool", self.epoch)]
        self.streams["pool"].append(lambda e, s=s: e.nop().then_inc(s, 1))
        for e in self.ENGS:
            if e != "pool":
                self._wait(e, (("E", "pool", self.epoch), self.cnt["pool"]))
        if self.epoch + 1 < self.NEPOCH:
            self.epoch += 1
            self.cnt = {e: 0 for e in self.ENGS}

    def finish(self, block):
        st = self.streams

        @block.tensor
        def _(e):
            for f in st["pe"]:
                f(e)

        @block.scalar
        def _(e):
            for f in st["act"]:
                f(e)

        @block.vector
        def _(e):
            for f in st["dve"]:
                f(e)

        @block.gpsimd
        def _(e):
            for f in st["pool"]:
                f(e)

        @block.sync
        def _(e):
            for f in st["sp"]:
                f(e)


class SBAlloc:
    WORDS = 52992

    def __init__(self, nc, stack):
        self.t = stack.enter_context(nc.sbuf_tensor("sbig", [128, self.WORDS], F32))
        self.off = 0
        self.n = 0

    def mark(self):
        return self.off

    def release(self, m):
        self.off = m

    def alloc(self, free_shape, dtype=F32, parts=128, name=None):
        n = int(np.prod(free_shape))
        esz = 4 if dtype in (F32, I32) else 2
        words = (n * esz + 3) // 4
        words = (words + 7) // 8 * 8
        assert self.off + words <= self.WORDS, f"SBUF overflow: {self.off}+{words}"
        ap = self.t[0:parts, self.off:self.off + words]
        self.off += words
        if dtype != F32:
            ap = ap.bitcast(dtype)
        ap = ap[:, 0:n]
        if len(free_shape) == 2:
            ap = ap.rearrange("p (a b) -> p a b", a=free_shape[0])
        elif len(free_shape) == 3:
            ap = ap.rearrange("p (a b c) -> p a b c", a=free_shape[0], b=free_shape[1])
        self.n += 1
        return Tile(ap, name or f"sb{self.n}")


def _make_consts():
    j = np.arange(128)[:, None]
    i = np.arange(128)[None, :]
    same64 = (j // 64) == (i // 64)
    c = {}
    g = -1.0 / 16.0
    c["m_le"] = np.where(same64 & (j <= i), g, 0.0)
    c["m_ge"] = np.where(same64 & (j >= i), g, 0.0)
    c["m_gt"] = np.where(same64 & (j > i), g, 0.0)
    c["m_lt"] = np.where(same64 & (j < i), g, 0.0)
    c["mf64"] = np.where(same64 & (j <= i), 1.0, 0.0)
    c["mb64"] = np.where(same64 & (j > i), 1.0, 0.0)
    c["relf"] = np.maximum(i - j, 0).astype(np.float64)
    c["relb"] = np.maximum(j - i, 0).astype(np.float64)
    c["mf128"] = (j <= i).astype(np.float64)
    c["mb128"] = (j > i).astype(np.float64)
    c["ident"] = (j == i).astype(np.float64)
    c["iota1"] = (i + 1.0) + 0 * j
    c["iotar"] = (128.0 - i) + 0 * j
    c["c128"] = np.full((128, 128), 128.0)
    c["ones96"] = ((j < 96) + 0 * i).astype(np.float64)
    c["ones64b"] = ((j // 64) == (i // 64)).astype(np.float64)
    sel = np.zeros((128, 128))
    for m in range(96):
        sel[m, m] = 1.0
    for m in range(32):
        sel[96 + m, 64 + m] = 1.0
    c["sel"] = sel
    sel65 = np.zeros((128, 128))
    sel65[64, :64] = 1.0
    c["sel65"] = sel65
    rm = np.zeros((128, 128))
    for m in range(32):
        rm[m, 64 + m] = 1.0
    for m in range(16):
        rm[m + 16, 96 + m] = -1.0
        rm[m, 96 + 16 + m] = 1.0
    c["ropemat"] = rm
    tabc = np.zeros((128, 128))
    p = np.arange(128)
    inv32 = 10000.0 ** (-(np.arange(32, dtype=np.float32) / np.float32(32))).astype(np.float32)
    inv16 = 10000.0 ** (-(np.arange(16, dtype=np.float32) / np.float32(16))).astype(np.float32)
    tabc[:, 0] = inv32[p % 32] / (2 * np.pi)
    tabc[64:, 1] = inv16[p[64:] % 16] / (2 * np.pi)
    tabc[:, 2] = 0.25
    tabc[:, 3] = 0.0
    tabc[:96, 4] = 0.25
    tabc[:, 5] = p
    tabc[:, 6] = 127.0 - p
    c["tabc"] = tabc
    names = list(c)
    arr = np.concatenate([c[n].astype(np.float32) for n in names], axis=1)
    return names, np.ascontiguousarray(arr)


CONST_NAMES, CONST_ARR = _make_consts()


class Ctx:
    pass


def build_program(T, groups, n_layers=4, dbg=None):
    dbg = dbg or {}
    nc = bass.Bass("TRN2", target_bir_lowering=False)
    S2 = 2 * T
    K = Ctx()
    K.nc, K.T, K.S2, K.dbg = nc, T, S2, dbg

    def din(name, shape, dt=F32):
        return nc.dram_tensor(name, list(shape), dt, kind="ExternalInput").ap()

    def dscr(name, shape, dt=F32):
        kind = "ExternalOutput" if name in dbg.get("dump", ()) else "Internal"
        return Tile(nc.dram_tensor(name, list(shape), dt, kind=kind).ap(), name)

    K.xT_in = din("xT", [D, T])
    K.flags = din("flags", [128, 4])
    K.w = {}
    for nm, shp in (("ffn_norm", [4, D]), ("ffn_w_up", [4, D, 2 * DFF]), ("ffn_conv_w", [4, 3, DFF]),
                    ("ffn_conv_b", [4, DFF]), ("ffn_w_down", [4, DFF, D]),
                    ("w_out_even", [2, D, D]), ("w_out_odd", [2, D, D])):
        K.w[nm] = din(nm, shp)
    for nm, shp in (("mix_norm_odd", [2, D]), ("w_in_odd", [2, D, 3104]), ("gla_w_gate_fwd", [2, 16, 512]),
                    ("gla_b_gate_fwd", [2, 512]), ("gla_w_gate_bwd", [2, 16, 512]), ("gla_b_gate_bwd", [2, 512]),
                    ("gla_out_norm", [2, 4, 256])):
        K.w[nm] = din(nm, shp)
    for nm, shp in (("mix_norm_even", [2, D]), ("w_in_even", [2, D, 2720]), ("mla_q_norm", [2, 384]),
                    ("mla_kv_norm", [2, 256]), ("mla_w_uq", [2, 384, 768]), ("mla_w_ukv", [2, 256, 1024]),
                    ("mla_q_head_norm", [2, 96]), ("mla_k_head_norm", [2, 96]), ("ret_theta_fwd", [2, 8]),
                    ("ret_theta_bwd", [2, 8]), ("ret_out_norm", [2, 8, 64])):
        K.w[nm] = din(nm, shp)
    K.pos_loc = din("pos_loc", [1, T], I32)
    K.pos_all = din("pos_all", [1, S2], I32)
    K.consts_in = din("consts", list(CONST_ARR.shape))
    if "dbg_at" in dbg:
        K.dbg_at = din("dbg_at", [D, T])
    K.yT = nc.dram_tensor("yT", [D, T], F32, kind="ExternalOutput").ap()

    K.xT = dscr("xT_s", [D, T])
    K.AT = dscr("AT", [D, T], BF16)
    K.H2T = dscr("H2T", [D, T + 2], BF16)
    K.HBX = dscr("HBX", [128, 16])
    K.HBG = dscr("HBG", [256, 16])
    NC64 = T // 64
    for nm in ("QCF", "KCF", "QBF", "QCB", "KCB", "QBB"):
        setattr(K, nm, dscr(nm, [512, T], BF16))
    K.KDF = dscr("KDF", [T, 512], BF16)
    K.KDB = dscr("KDB", [T, 512], BF16)
    K.VT = dscr("VT", [T, 1024], BF16)
    K.SGT = dscr("SGT", [D, T], BF16)
    K.RPB = dscr("RPB", [NC64, 128, 1024], BF16)
    K.GST = dscr("GST", [128, 2048])
    K.GSG = dscr("GSG", [256, 2048])
    K.TRC = dscr("TRC", [128, T])
    K.TRS = dscr("TRS", [128, T])
    K.TQ = dscr("TQ", [128, T])
    K.TK = dscr("TK", [128, S2])
    K.CQN = dscr("CQN", [384, T], BF16)
    TL = min(512, T)
    K.LAT = [dscr(f"LAT{j}", [288, TL]) for j in range(T // TL)]
    K.LATG = [dscr(f"LATG{j}", [576, TL]) for j in range(T // TL)]
    K.RQ = dscr("RQ", [512, T], BF16)
    K.RK = dscr("RK", [512, T], BF16)
    K.RV = dscr("RV", [T, 512], BF16)
    K.QT = dscr("QT", [8, 96, T], BF16)
    K.KT = dscr("KT", [8, 96, S2], BF16)
    K.VX = dscr("VX", [S2, 520], BF16)
    K.RST = dscr("RST", [64, 1024])
    K.RSG = dscr("RSG", [128, 1024])

    with ExitStack() as st:
        fw = FW(nc, st)
        sb = SBAlloc(nc, st)
        K.fw, K.sb, K.groups = fw, sb, groups
        psum = st.enter_context(nc.psum_tensor("psum", [128, 4096], F32))
        K.ps = [Tile(psum[:, i * 512:(i + 1) * 512], f"ps{i}") for i in range(8)]
        K.psum_full = psum
        K.ps_rr = 0

        K.ones_bf = sb.alloc([128], BF16, name="ones_bf")
        fw.op("dve", lambda e: e.memset(K.ones_bf[:], 1.0), writes=[K.ones_bf])
        K.flg = sb.alloc([4], F32, name="flg")
        fw.dma("sp", K.flg[:], K.flags[:, :], writes=[K.flg])
        K.cst = {}
        for ci, nm in enumerate(CONST_NAMES):
            t = sb.alloc([128], F32, name="c_" + nm)
            fw.dma("sp", t[:], K.consts_in[:, ci * 128:(ci + 1) * 128], writes=[t])
            K.cst[nm] = t
        K.ones_f = sb.alloc([128], F32, name="ones_f")
        fw.op("dve", lambda e: e.memset(K.ones_f[:], 1.0), writes=[K.ones_f])

        if "only_odd" not in dbg:
            make_tables(K)
        m0 = sb.mark()
        for c in range(8):
            t = sb.alloc([T], F32)
            fw.dma("sp", t[:], K.xT_in[c * 128:(c + 1) * 128, :], writes=[t])
            fw.dma("sp", K.xT[c * 128:(c + 1) * 128, :], t[:], reads=[t])
            if c % 2 == 1:
                pass
        fw.barrier()
        sb.release(m0)

        for L in range(n_layers):
            i = L // 2
            if "dbg_at" in dbg:
                m0 = sb.mark()
                for c in range(8):
                    t = sb.alloc([T], BF16)
                    fw.dma("pool", t[:], K.dbg_at[c * 128:(c + 1) * 128, :], writes=[t])
                    fw.dma("sp", K.AT[c * 128:(c + 1) * 128, :], t[:], reads=[t])
                fw.barrier()
                sb.release(m0)
            elif L % 2 == 1 or "only_odd" in dbg:
                gla_layer(K, i)
            else:
                even_layer(K, i)
            wout = K.w["w_out_even"][i] if (L % 2 == 0 and "only_odd" not in dbg) else K.w["w_out_odd"][i]
            phase_out(K, L, wout)
            phase_ffn(K, L, last=(L == n_layers - 1))

        fw.barrier()
        with nc.Block() as block:
            fw.finish(block)
    K.n_ops = fw.n_ops
    return nc, K


def next_ps(K):
    p = K.ps[K.ps_rr % getattr(K, "ps_mod", 6)]
    K.ps_rr += 1
    return p


def load_w_bf16(K, dst, src, kc_n, ncols, c0=0):
    v = src.rearrange("(c p) n -> p c n", p=128)
    step = max(1, 4096 // ncols)
    for k0 in range(0, kc_n, step):
        k1 = min(kc_n, k0 + step)
        K.fw.dma("pool", dst[:, k0:k1, c0:c0 + ncols], v[:, k0:k1, :], writes=[dst])


def load_col(K, dst, src_vec, nchunk):
    K.fw.dma("sp", dst[:, 0:nchunk], src_vec.rearrange("(c p) -> p c", p=128), writes=[dst],
             allow_slow_non_contiguous=True)


def rms_rstd(K, sq, nchunk, width, dim, out_rstd, ones=None, P=128):
    fw = K.fw
    ones = ones or K.ones_bf
    ps = next_ps(K)
    for c in range(nchunk):
        fw.op("pe", lambda e, c=c: e.matmul(ps[0:P, 0:width], ones[0:P, 0:P], sq[0:P, c, 0:width],
                                            start=(c == 0), stop=(c == nchunk - 1)),
              reads=[ones, sq], writes=[ps])
    fw.op("act", lambda e: e.activation(out_rstd[0:P, 0:width], ps[0:P, 0:width], AF.Ln, bias=EPS,
                                        scale=1.0 / dim), reads=[ps], writes=[out_rstd])
    fw.op("act", lambda e: e.activation(out_rstd[0:P, 0:width], out_rstd[0:P, 0:width], AF.Exp, scale=-0.5),
          reads=[out_rstd], writes=[out_rstd])


def phase_out(K, L, wout):
    fw, sb, T = K.fw, K.sb, K.T
    K.ffn_mark = sb.mark()
    K.wup = sb.alloc([8, 2 * DFF], BF16, name="wup")
    K.wdn = sb.alloc([NGC, D], BF16, name="wdn")
    m0 = sb.mark()
    w = sb.alloc([8, D], BF16, name="wout")
    load_w_bf16(K, w, wout, 8, D)
    load_w_bf16(K, K.wup, K.w["ffn_w_up"][L], 8, 2 * DFF)
    load_w_bf16(K, K.wdn, K.w["ffn_w_down"][L], NGC, D)
    TT = 256 if T >= 256 else T
    g2 = sb.alloc([8], F32, name="g2")
    load_col(K, g2, K.w["ffn_norm"][L], 8)
    hb = sb.alloc([8, 2], F32, name="hb")
    ATv = K.AT.ap.rearrange("(c p) t -> p c t", p=128)
    XTv = K.xT.ap.rearrange("(c p) t -> p c t", p=128)
    H2v = K.H2T.ap.rearrange("(c p) t -> p c t", p=128)
    nt = T // TT
    at_t = [sb.alloc([8, TT], BF16, name=f"at{j}") for j in range(2)]
    x_t = [sb.alloc([8, TT], F32, name=f"x{j}") for j in range(2)]
    sq_t = sb.alloc([8, TT], BF16, name="sq")
    h2_t = [sb.alloc([8, TT], BF16, name=f"h2{j}") for j in range(2)]
    rstd = sb.alloc([TT], F32, name="rstd")
    def load_out(ti):
        fw.dma("sp", at_t[ti % 2][:, :, :], ATv[:, :, ti * TT:(ti + 1) * TT], writes=[at_t[ti % 2]])
        fw.dma("sp", x_t[ti % 2][:, :, :], XTv[:, :, ti * TT:(ti + 1) * TT], writes=[x_t[ti % 2]])

    load_out(0)
    for ti in range(nt):
        t0 = ti * TT
        at, xt, h2 = at_t[ti % 2], x_t[ti % 2], h2_t[ti % 2]
        if ti + 1 < nt:
            load_out(ti + 1)
        for oc in range(8):
            ps = next_ps(K)
            for kc in range(8):
                fw.op("pe", lambda e, kc=kc, oc=oc, ps=ps, at=at: e.matmul(
                    ps[:, 0:TT], w[:, kc, oc * 128:(oc + 1) * 128], at[:, kc, :],
                    start=(kc == 0), stop=(kc == 7)), reads=[w, at], writes=[ps])
            fw.op("dve", lambda e, oc=oc, ps=ps, xt=xt: e.tensor_tensor(
                xt[:, oc, :], xt[:, oc, :], ps[:, 0:TT], ALU.add), reads=[ps, xt], writes=[xt])
        fw.dma("sp", XTv[:, :, t0:t0 + TT], xt[:, :, :], reads=[xt])
        for oc in range(8):
            fw.op("act", lambda e, oc=oc, xt=xt: e.activation(sq_t[:, oc, :], xt[:, oc, :], AF.Square),
                  reads=[xt], writes=[sq_t])
        rms_rstd(K, sq_t, 8, TT, D, rstd)
        for oc in range(8):
            fw.op("dve", lambda e, oc=oc, xt=xt, h2=h2: e.scalar_tensor_tensor(
                h2[:, oc, :], xt[:, oc, :], g2[:, oc:oc + 1], rstd[:, :], ALU.mult, ALU.mult),
                reads=[xt, g2, rstd], writes=[h2])
        fw.dma("sp", H2v[:, :, 1 + t0:1 + t0 + TT], h2[:, :, :], reads=[h2])
        if ti == 0:
            fw.op("dve", lambda e, h2=h2: e.tensor_copy(hb[:, :, 0:1], h2[:, :, 0:1]), reads=[h2], writes=[hb])
        if ti == nt - 1:
            fw.op("dve", lambda e, h2=h2: e.tensor_copy(hb[:, :, 1:2], h2[:, :, TT - 1:TT]), reads=[h2], writes=[hb])
    fw.dma("sp", K.HBX[:, :], hb[:, :, :].rearrange("p a b -> p (a b)"), reads=[hb], writes=[K.HBX])
    fw.collective("AllGather", ALU.bypass, K.groups, K.HBX.ap, K.HBG.ap, reads=[K.HBX], writes=[K.HBG])
    g0 = sb.alloc([8, 2], F32, name="g0")
    g1 = sb.alloc([8, 2], F32, name="g1")
    fw.dma("sp", g0[:, :, :].rearrange("p a b -> p (a b)"), K.HBG[0:128, :], reads=[K.HBG], writes=[g0])
    fw.dma("sp", g1[:, :, :].rearrange("p a b -> p (a b)"), K.HBG[128:256, :], reads=[K.HBG], writes=[g1])
    hl = sb.alloc([8, 1], BF16, name="hl")
    hr = sb.alloc([8, 1], BF16, name="hr")
    fw.op("dve", lambda e: e.tensor_scalar(hl[:, :, :], g0[:, :, 1:2], K.flg[:, 0:1], None, ALU.mult),
          reads=[g0, K.flg], writes=[hl])
    fw.op("dve", lambda e: e.tensor_scalar(hr[:, :, :], g1[:, :, 0:1], K.flg[:, 1:2], None, ALU.mult),
          reads=[g1, K.flg], writes=[hr])
    fw.dma("sp", H2v[:, :, 0:1], hl[:, :, :], reads=[hl], allow_slow_non_contiguous=True)
    fw.dma("sp", H2v[:, :, T + 1:T + 2], hr[:, :, :], reads=[hr], allow_slow_non_contiguous=True)
    fw.barrier()
    sb.release(m0)


def phase_ffn(K, L, last):
    fw, sb, T = K.fw, K.sb, K.T
    TF = 510 if T >= 510 else T
    wup, wdn = K.wup, K.wdn
    cw = sb.alloc([3, NGC], F32, name="cw")
    for k in range(3):
        fw.dma("sp", cw[:, k, :], K.w["ffn_conv_w"][L, k].rearrange("(c p) -> p c", p=128), writes=[cw],
               allow_slow_non_contiguous=True)
    cb = sb.alloc([NGC], F32, name="cb")
    load_col(K, cb, K.w["ffn_conv_b"][L], NGC)
    XTv = K.xT.ap.rearrange("(c p) t -> p c t", p=128)
    YTv = K.yT.rearrange("(c p) t -> p c t", p=128)
    H2v = K.H2T.ap.rearrange("(c p) t -> p c t", p=128)
    h_t = [sb.alloc([8, TF + 2], BF16, name=f"h{j}") for j in range(2)]
    xt = sb.alloc([8, TF], F32, name="xf")
    act = sb.alloc([NGC, TF], BF16, name="act")
    cv = [sb.alloc([TF], F32, name=f"cv{j}") for j in range(2)]
    starts = list(range(0, T, TF))

    def load_ffn(ti):
        t0 = starts[ti]
        W = min(TF, T - t0)
        fw.dma("sp", h_t[ti % 2][:, :, 0:W + 2], H2v[:, :, t0:t0 + W + 2], writes=[h_t[ti % 2]])

    load_ffn(0)
    for ti, t0 in enumerate(starts):
        W = min(TF, T - t0)
        h = h_t[ti % 2]
        if ti + 1 < len(starts):
            load_ffn(ti + 1)
        fw.dma("sp", xt[:, :, 0:W], XTv[:, :, t0:t0 + W], writes=[xt])
        for gc in range(NGC):
            psg = next_ps(K)
            psv = next_ps(K)
            c = cv[gc % 2]
            for kc in range(8):
                fw.op("pe", lambda e: e.matmul(psg[:, 0:W + 2], wup[:, kc, gc * 128:(gc + 1) * 128], h[:, kc, 0:W + 2],
                                               start=(kc == 0), stop=(kc == 7)), reads=[wup, h], writes=[psg])
            for kc in range(8):
                fw.op("pe", lambda e: e.matmul(psv[:, 0:W], wup[:, kc, DFF + gc * 128:DFF + (gc + 1) * 128], h[:, kc, 1:W + 1],
                                               start=(kc == 0), stop=(kc == 7)), reads=[wup, h], writes=[psv])
            fw.op("act", lambda e: e.activation(c[:, 0:W], psg[:, 1:W + 1], AF.Identity, bias=cb[:, gc:gc + 1],
                                                scale=cw[:, 1, gc:gc + 1]), reads=[psg, cb, cw], writes=[c])
            fw.op("dve", lambda e: e.scalar_tensor_tensor(c[:, 0:W], psg[:, 0:W], cw[:, 0, gc:gc + 1], c[:, 0:W], ALU.mult, ALU.add),
                  reads=[psg, cw, c], writes=[c])
            fw.op("dve", lambda e: e.scalar_tensor_tensor(c[:, 0:W], psg[:, 2:W + 2], cw[:, 2, gc:gc + 1], c[:, 0:W], ALU.mult, ALU.add),
                  reads=[psg, cw, c], writes=[c])
            fw.op("act", lambda e: e.activation(c[:, 0:W], c[:, 0:W], AF.Silu), reads=[c], writes=[c])
            fw.op("dve", lambda e: e.tensor_tensor(act[:, gc, 0:W], c[:, 0:W], psv[:, 0:W], ALU.mult), reads=[c, psv], writes=[act])
        for oc in range(8):
            ps = next_ps(K)
            for gc in range(NGC):
                fw.op("pe", lambda e: e.matmul(ps[:, 0:W], wdn[:, gc, oc * 128:(oc + 1) * 128], act[:, gc, 0:W],
                                               start=(gc == 0), stop=(gc == NGC - 1)), reads=[wdn, act], writes=[ps])
            fw.op("dve", lambda e: e.tensor_tensor(xt[:, oc, 0:W], xt[:, oc, 0:W], ps[:, 0:W], ALU.add), reads=[ps, xt], writes=[xt])
        if last:
            fw.dma("sp", YTv[:, :, t0:t0 + W], xt[:, :, 0:W], reads=[xt])
        else:
            fw.dma("sp", XTv[:, :, t0:t0 + W], xt[:, :, 0:W], reads=[xt])
    fw.barrier()
    sb.release(K.ffn_mark)


def norm_tile(K, xt, h, sq, rstd, gcol, TT):
    fw = K.fw
    for oc in range(8):
        fw.op("act", lambda e, oc=oc: e.activation(sq[:, oc, 0:TT], xt[:, oc, 0:TT], AF.Square),
              reads=[xt], writes=[sq])
    rms_rstd(K, sq, 8, TT, D, rstd)
    for oc in range(8):
        fw.op("dve", lambda e, oc=oc: e.scalar_tensor_tensor(
            h[:, oc, 0:TT], xt[:, oc, 0:TT], gcol[:, oc:oc + 1], rstd[:, 0:TT], ALU.mult, ALU.mult),
            reads=[xt, gcol, rstd], writes=[h])


def proj_fm(K, w, c0, ncol, h, TT, t0=0):
    ps = next_ps(K)
    for kc in range(8):
        K.fw.op("pe", lambda e, kc=kc: e.matmul(ps[0:ncol, 0:TT], w[:, kc, c0:c0 + ncol], h[:, kc, t0:t0 + TT],
                                                start=(kc == 0), stop=(kc == 7)), reads=[w, h], writes=[ps])
    return ps


def proj_tm(K, w, c0, ncol, h, t0, ps=None):
    ps = ps or next_ps(K)
    for kc in range(8):
        K.fw.op("pe", lambda e, kc=kc: e.matmul(ps[:, 0:ncol], h[:, kc, t0:t0 + 128], w[:, kc, c0:c0 + ncol],
                                                start=(kc == 0), stop=(kc == 7)), reads=[w, h], writes=[ps])
    return ps


def gla_layer(K, i):
    sb, T = K.sb, K.T
    mL = sb.mark()
    NC = T // 64
    decf = sb.alloc([4, NC], F32, name="decf")
    decb = sb.alloc([4, NC], F32, name="decb")
    gla_o1(K, i, decf, decb)
    gla_o23(K, i, decf, decb)
    sb.release(mL)


def gla_o1(K, i, decf, decb):
    fw, sb, T = K.fw, K.sb, K.T
    m0 = sb.mark()
    TT = min(512, T)
    NS = TT // 128
    w = sb.alloc([8, 3104], BF16, name="w_in_odd")
    load_w_bf16(K, w, K.w["w_in_odd"][i], 8, 3104)
    g1 = sb.alloc([8], F32, name="g1")
    load_col(K, g1, K.w["mix_norm_odd"][i], 8)
    gn = sb.alloc([8], F32, name="gn")
    load_col(K, gn, K.w["gla_out_norm"][i].rearrange("h e -> (h e)"), 8)
    wg = {}
    gae = {}
    for d, wn, bn in (("f", "gla_w_gate_fwd", "gla_b_gate_fwd"), ("b", "gla_w_gate_bwd", "gla_b_gate_bwd")):
        t = sb.alloc([512], F32, parts=32, name="wg" + d)
        fw.dma("sp", t[0:16, :], K.w[wn][i], writes=[t])
        fw.dma("sp", t[16:17, :], K.w[bn][i:i + 1, :], writes=[t])
        wg[d] = t
        ga = sb.alloc([TT], F32, parts=32, name="gae" + d)
        fw.op("dve", lambda e, ga=ga: e.memset(ga[:, :], 1.0), writes=[ga])
        gae[d] = ga
    XTv = K.xT.ap.rearrange("(c p) t -> p c t", p=128)
    x_t = [sb.alloc([8, TT], F32, name=f"gx{j}") for j in range(2)]
    h = sb.alloc([8, TT], BF16, name="gh")
    sq = sb.alloc([8, TT], BF16, name="gsq")
    rstd = sb.alloc([TT], F32, name="grstd")
    q = sb.alloc([4, TT], F32, name="gq")
    k = sb.alloc([4, TT], F32, name="gk")
    sg = sb.alloc([8, TT], BF16, name="gsg")
    tmp = [sb.alloc([512], F32, name=f"gtmp{j}") for j in range(2)]
    outs = {nm: sb.alloc([4, TT], BF16, name="o" + nm) for nm in ("QCF", "KCF", "QBF", "QCB", "KCB", "QBB")}
    lt = {d: sb.alloc([512], F32, name="l" + d) for d in "fb"}
    vt = sb.alloc([1024], BF16, name="gvt")
    kd = [sb.alloc([512], BF16, name=f"gkd{j}") for j in range(2)]
    Et = [sb.alloc([128], F32, name=f"gE{j}") for j in range(2)]
    rEt = [sb.alloc([128], F32, name=f"grE{j}") for j in range(2)]
    ecnt = 0
    fw.dma("sp", x_t[0][:, :, :], XTv[:, :, 0:TT], writes=[x_t[0]])
    for ti in range(T // TT):
        t0 = ti * TT
        xt = x_t[ti % 2]
        if ti + 1 < T // TT:
            fw.dma("sp", x_t[(ti + 1) % 2][:, :, :], XTv[:, :, t0 + TT:t0 + 2 * TT], writes=[x_t[(ti + 1) % 2]])
        norm_tile(K, xt, h, sq, rstd, g1, TT)
        for c in range(4):
            ps = proj_fm(K, w, c * 128, 128, h, TT)
            fw.op("act", lambda e, c=c, ps=ps: e.mul(q[:, c, :], ps[:, 0:TT], 128 ** -0.5), reads=[ps], writes=[q])
        for c in range(4):
            ps = proj_fm(K, w, 512 + c * 128, 128, h, TT)
            fw.op("dve", lambda e, c=c, ps=ps: e.tensor_copy(k[:, c, :], ps[:, 0:TT]), reads=[ps], writes=[k])
        for c in range(8):
            ps = proj_fm(K, w, 2048 + c * 128, 128, h, TT)
            tm = tmp[c % 2]
            fw.op("act", lambda e, ps=ps, tm=tm: e.activation(tm[:, 0:TT], ps[:, 0:TT], AF.Silu), reads=[ps], writes=[tm])
            fw.op("dve", lambda e, c=c, tm=tm: e.tensor_scalar(sg[:, c, :], tm[:, 0:TT], gn[:, c:c + 1], None, ALU.mult),
                  reads=[tm, gn], writes=[sg])
        fw.dma("sp", K.SGT.ap.rearrange("(c p) t -> p c t", p=128)[:, :, t0:t0 + TT], sg[:, :, :], reads=[sg])
        for d, c0 in (("f", 3072), ("b", 3088)):
            ps = proj_fm(K, w, c0, 16, h, TT)
            fw.op("dve", lambda e, d=d, ps=ps: e.tensor_copy(gae[d][0:16, :], ps[0:16, 0:TT]), reads=[ps], writes=[gae[d]])
        for s_ in range(NS):
            s0 = s_ * 128
            tok0 = t0 + s0
            for half in range(2):
                ps = proj_tm(K, w, 1024 + half * 512, 512, h, s0)
                if half == 0:
                    fw.op("act", lambda e, ps=ps: e.copy(vt[:, 0:512], ps[:, 0:512]), reads=[ps], writes=[vt])
                else:
                    fw.op("dve", lambda e, ps=ps: e.tensor_copy(vt[:, 512:1024], ps[:, 0:512]), reads=[ps], writes=[vt])
            fw.dma("sp", K.VT[tok0:tok0 + 128, :], vt[:, :], reads=[vt])
            psk = proj_tm(K, w, 512, 512, h, s0, ps=K.ps[6])
            for di, d in enumerate("fb"):
                l = lt[d]
                psx = next_ps(K)
                fw.op("pe", lambda e, d=d, psx=psx: e.matmul(psx[:, 0:512], gae[d][0:17, s0:s0 + 128], wg[d][0:17, :],
                                                             start=True, stop=True), reads=[gae[d], wg[d]], writes=[psx])
                fw.op("act", lambda e, psx=psx, l=l: e.activation(l[:, :], psx[:, 0:512], AF.Exp, scale=-1.0),
                      reads=[psx], writes=[l])
                fw.op("act", lambda e, l=l: e.activation(l[:, :], l[:, :], AF.Ln, bias=1.0), reads=[l], writes=[l])
                mstrict = K.cst["m_gt"] if d == "f" else K.cst["m_lt"]
                pse = next_ps(K)
                fw.op("pe", lambda e, pse=pse, l=l, m=mstrict: e.matmul(pse[:, 0:512], m[:, :], l[:, :], start=True, stop=True),
                      reads=[mstrict, l], writes=[pse])
                tm = tmp[di]
                fw.op("act", lambda e, pse=pse, tm=tm: e.activation(tm[:, :], pse[:, 0:512], AF.Exp), reads=[pse], writes=[tm])
                kdt = kd[di]
                fw.op("dve", lambda e, tm=tm, kdt=kdt: e.tensor_tensor(kdt[:, :], tm[:, :], psk[:, 0:512], ALU.mult),
                      reads=[tm, psk], writes=[kdt])
                fw.dma("sp", (K.KDF if d == "f" else K.KDB)[tok0:tok0 + 128, :], kdt[:, :], reads=[kdt])
                mincl = K.cst["m_le"] if d == "f" else K.cst["m_ge"]
                mid = 32 if d == "f" else 31
                last = 63 if d == "f" else 0
                qc, kc_, qb = (outs["QCF"], outs["KCF"], outs["QBF"]) if d == "f" else (outs["QCB"], outs["KCB"], outs["QBB"])
                dec = decf if d == "f" else decb
                for hh in range(4):
                    E, rE = Et[ecnt % 2], rEt[ecnt % 2]
                    ecnt += 1
                    psb = next_ps(K)
                    fw.op("pe", lambda e, psb=psb, l=l, hh=hh, m=mincl: e.matmul(
                        psb[:, 0:128], l[:, hh * 128:(hh + 1) * 128], m[:, :], start=True, stop=True),
                        reads=[l, mincl], writes=[psb])
                    fw.op("act", lambda e, psb=psb, E=E: e.activation(E[:, :], psb[:, 0:128], AF.Exp), reads=[psb], writes=[E])
                    fw.op("act", lambda e, psb=psb, rE=rE: e.activation(rE[:, :], psb[:, 0:128], AF.Exp, scale=-1.0), reads=[psb], writes=[rE])
                    for ch in range(2):
                        cs = slice(ch * 64, ch * 64 + 64)
                        ts_ = slice(s0 + ch * 64, s0 + ch * 64 + 64)
                        mc = ch * 64 + mid
                        fw.op("dve", lambda e, E=E, rE=rE, cs=cs, ts_=ts_, mc=mc, hh=hh, qc=qc: e.scalar_tensor_tensor(
                            qc[:, hh, ts_], E[:, cs], rE[:, mc:mc + 1], q[:, hh, ts_], ALU.mult, ALU.mult),
                            reads=[E, rE, q], writes=[qc])
                        fw.op("dve", lambda e, E=E, rE=rE, cs=cs, ts_=ts_, mc=mc, hh=hh, kc_=kc_: e.scalar_tensor_tensor(
                            kc_[:, hh, ts_], rE[:, cs], E[:, mc:mc + 1], k[:, hh, ts_], ALU.mult, ALU.mult),
                            reads=[E, rE, k], writes=[kc_])
                        cidx = (tok0 // 64) + ch
                        lc = ch * 64 + last
                        fw.op("dve", lambda e, E=E, lc=lc, hh=hh, cidx=cidx, dec=dec: e.tensor_copy(
                            dec[:, hh, cidx:cidx + 1], E[:, lc:lc + 1]), reads=[E], writes=[dec])
                    fw.op("dve", lambda e, E=E, hh=hh, qb=qb: e.tensor_tensor(
                        qb[:, hh, s0:s0 + 128], E[:, :], q[:, hh, s0:s0 + 128], ALU.mult), reads=[E, q], writes=[qb])
        for nm, tl in outs.items():
            fw.dma("sp", getattr(K, nm).ap.rearrange("(h p) t -> p h t", p=128)[:, :, t0:t0 + TT], tl[:, :, :], reads=[tl], tag=nm)
    fw.barrier()
    sb.release(m0)


def gla_o23(K, i, decf, decb):
    fw, sb, T = K.fw, K.sb, K.T
    m0 = sb.mark()
    NG = T // 128
    Sf = sb.alloc([1024], F32, name="Sf")
    Sb = sb.alloc([1024], F32, name="Sb")
    P = sb.alloc([4], F32, name="P")
    fw.op("dve", lambda e: e.memset(Sf[:, :], 0.0), writes=[Sf])
    fw.op("dve", lambda e: e.memset(Sb[:, :], 0.0), writes=[Sb])
    fw.op("dve", lambda e: e.memset(P[:, :], 1.0), writes=[P])
    kdf_t = [sb.alloc([512], BF16, name=f"kdf{j}") for j in range(2)]
    kdb_t = [sb.alloc([512], BF16, name=f"kdb{j}") for j in range(2)]
    v_t = [sb.alloc([1024], BF16, name=f"v{j}") for j in range(2)]

    def states(kdt, vtl, ch):
        p0 = ch * 64
        pa, pb = next_ps(K), next_ps(K)
        for hh in range(4):
            ps = pa if hh < 2 else pb
            col = (hh % 2) * 256
            fw.op("pe", lambda e, ps=ps, col=col, hh=hh: e.matmul(
                ps[:, col:col + 256], kdt[p0:p0 + 64, hh * 128:(hh + 1) * 128], vtl[p0:p0 + 64, hh * 256:(hh + 1) * 256],
                start=True, stop=True), reads=[kdt, vtl], writes=[ps])
        return pa, pb

    def psl(pa, pb, hh):
        ps = pa if hh < 2 else pb
        col = (hh % 2) * 256
        return ps, ps[:, col:col + 256]

    def load_p1(g):
        fw.dma("sp", kdf_t[g % 2][:, :], K.KDF[g * 128:(g + 1) * 128, :], writes=[kdf_t[g % 2]])
        fw.dma("sp", kdb_t[g % 2][:, :], K.KDB[g * 128:(g + 1) * 128, :], writes=[kdb_t[g % 2]])
        fw.dma("sp", v_t[g % 2][:, :], K.VT[g * 128:(g + 1) * 128, :], writes=[v_t[g % 2]])

    load_p1(0)
    for g in range(NG):
        kdf, kdb, v = kdf_t[g % 2], kdb_t[g % 2], v_t[g % 2]
        if g + 1 < NG:
            load_p1(g + 1)
        for ch in range(2):
            c = 2 * g + ch
            pa, pb = states(kdf, v, ch)
            for hh in range(4):
                ps, pv = psl(pa, pb, hh)
                fw.op("dve", lambda e, hh=hh, pv=pv, c=c: e.scalar_tensor_tensor(
                    Sf[:, hh * 256:(hh + 1) * 256], Sf[:, hh * 256:(hh + 1) * 256], decf[:, hh, c:c + 1], pv,
                    ALU.mult, ALU.add), reads=[Sf, decf, ps], writes=[Sf])
            pa, pb = states(kdb, v, ch)
            for hh in range(4):
                ps, pv = psl(pa, pb, hh)
                fw.op("dve", lambda e, hh=hh, pv=pv: e.scalar_tensor_tensor(
                    Sb[:, hh * 256:(hh + 1) * 256], pv, P[:, hh:hh + 1], Sb[:, hh * 256:(hh + 1) * 256],
                    ALU.mult, ALU.add), reads=[Sb, P, ps], writes=[Sb])
            fw.op("dve", lambda e, c=c: e.tensor_tensor(P[:, :], P[:, :], decb[:, :, c], ALU.mult),
                  reads=[P, decb], writes=[P])
    fw.dma("sp", K.GST[:, 0:1024], Sf[:, :], reads=[Sf], writes=[K.GST])
    fw.dma("sp", K.GST[:, 1024:2048], Sb[:, :], reads=[Sb], writes=[K.GST])
    fw.collective("AllGather", ALU.bypass, K.groups, K.GST.ap, K.GSG.ap, reads=[K.GST], writes=[K.GSG])
    i_f = sb.alloc([1024], F32, name="i_f")
    i_b = sb.alloc([1024], F32, name="i_b")
    fw.dma("sp", i_f[:, :], K.GSG[0:128, 0:1024], reads=[K.GSG], writes=[i_f])
    fw.dma("sp", i_b[:, :], K.GSG[128:256, 1024:2048], reads=[K.GSG], writes=[i_b])
    fw.op("dve", lambda e: e.tensor_scalar(Sf[:, :], i_f[:, :], K.flg[:, 0:1], None, ALU.mult), reads=[i_f, K.flg], writes=[Sf])
    fw.op("dve", lambda e: e.tensor_scalar(Sb[:, :], i_b[:, :], K.flg[:, 1:2], None, ALU.mult), reads=[i_b, K.flg], writes=[Sb])
    rp_t = [sb.alloc([1024], BF16, name=f"rp{j}") for j in range(2)]
    def load_bw(g):
        fw.dma("sp", kdb_t[g % 2][:, :], K.KDB[g * 128:(g + 1) * 128, :], writes=[kdb_t[g % 2]])
        fw.dma("sp", v_t[g % 2][:, :], K.VT[g * 128:(g + 1) * 128, :], writes=[v_t[g % 2]])

    load_bw(NG - 1)
    for g in reversed(range(NG)):
        kdb, v = kdb_t[g % 2], v_t[g % 2]
        if g - 1 >= 0:
            load_bw(g - 1)
        for ch in (1, 0):
            c = 2 * g + ch
            rp = rp_t[c % 2]
            fw.op("act", lambda e, rp=rp: e.copy(rp[:, :], Sb[:, :]), reads=[Sb], writes=[rp])
            fw.dma("sp", K.RPB[c], rp[:, :], reads=[rp])
            pa, pb = states(kdb, v, ch)
            for hh in range(4):
                ps, pv = psl(pa, pb, hh)
                fw.op("dve", lambda e, hh=hh, pv=pv, c=c: e.scalar_tensor_tensor(
                    Sb[:, hh * 256:(hh + 1) * 256], Sb[:, hh * 256:(hh + 1) * 256], decb[:, hh, c:c + 1], pv,
                    ALU.mult, ALU.add), reads=[Sb, decb, ps], writes=[Sb])
    fw.barrier()
    names = ("KCF", "QCF", "QBF", "KCB", "QCB", "QBB")
    in_t = [{nm: sb.alloc([4, 128], BF16, name=f"i{nm}{j}") for nm in names} for j in range(2)]
    rpb_t = [sb.alloc([2, 1024], BF16, name=f"rpb{j}") for j in range(2)]
    sg_t = [sb.alloc([8, 128], BF16, name=f"sg{j}") for j in range(2)]
    rpf = [sb.alloc([1024], BF16, name=f"rpf{j}") for j in range(2)]
    af = sb.alloc([4, 128], BF16, name="af")
    ab = sb.alloc([4, 128], BF16, name="ab")
    sq = sb.alloc([2, 512], BF16, name="osq")
    rstd = sb.alloc([512], F32, name="orstd")
    tmpo = sb.alloc([512], F32, name="otmp")
    at = [sb.alloc([8, 128], BF16, name=f"oat{j}") for j in range(2)]
    mf = K.cst["mf64"].ap.rearrange("p (o n) -> p o n", o=1).to_broadcast([128, 4, 128])
    mb = K.cst["mb64"].ap.rearrange("p (o n) -> p o n", o=1).to_broadcast([128, 4, 128])
    def load_fw(g):
        j = g % 2
        tsl = slice(g * 128, (g + 1) * 128)
        for nm in names:
            fw.dma("sp", in_t[j][nm][:, :, :], getattr(K, nm).ap.rearrange("(h p) t -> p h t", p=128)[:, :, tsl], writes=[in_t[j][nm]])
        fw.dma("sp", kdf_t[j][:, :], K.KDF[tsl, :], writes=[kdf_t[j]])
        fw.dma("sp", v_t[j][:, :], K.VT[tsl, :], writes=[v_t[j]])
        fw.dma("sp", rpb_t[j][:, :, :], K.RPB.ap[2 * g:2 * g + 2].rearrange("c p n -> p c n"), writes=[rpb_t[j]])
        fw.dma("sp", sg_t[j][:, :, :], K.SGT.ap.rearrange("(c p) t -> p c t", p=128)[:, :, tsl], writes=[sg_t[j]])

    load_fw(0)
    for g in range(NG):
        j = g % 2
        it, kdf, v, rpb, sgl = in_t[j], kdf_t[j], v_t[j], rpb_t[j], sg_t[j]
        tsl = slice(g * 128, (g + 1) * 128)
        if g + 1 < NG:
            load_fw(g + 1)
        psF, psB = next_ps(K), next_ps(K)
        for hh in range(4):
            fw.op("pe", lambda e, hh=hh, it=it, psF=psF: e.matmul(psF[:, hh * 128:(hh + 1) * 128], it["KCF"][:, hh, :], it["QCF"][:, hh, :],
                                                           start=True, stop=True), reads=[it["KCF"], it["QCF"]], writes=[psF])
        for hh in range(4):
            fw.op("pe", lambda e, hh=hh, it=it, psB=psB: e.matmul(psB[:, hh * 128:(hh + 1) * 128], it["KCB"][:, hh, :], it["QCB"][:, hh, :],
                                                           start=True, stop=True), reads=[it["KCB"], it["QCB"]], writes=[psB])
        fw.op("dve", lambda e, psF=psF: e.tensor_tensor(af[:, :, :], psF[:, 0:512].rearrange("p (h n) -> p h n", h=4), mf, ALU.mult),
              reads=[psF, K.cst["mf64"]], writes=[af])
        fw.op("dve", lambda e, psB=psB: e.tensor_tensor(ab[:, :, :], psB[:, 0:512].rearrange("p (h n) -> p h n", h=4), mb, ALU.mult),
              reads=[psB, K.cst["mb64"]], writes=[ab])
        for ch in range(2):
            c = 2 * g + ch
            fw.op("act", lambda e, ch=ch: e.copy(rpf[ch][:, :], Sf[:, :]), reads=[Sf], writes=[rpf[ch]])
            pa, pb = states(kdf, v, ch)
            for hh in range(4):
                ps, pv = psl(pa, pb, hh)
                fw.op("dve", lambda e, hh=hh, pv=pv, c=c: e.scalar_tensor_tensor(
                    Sf[:, hh * 256:(hh + 1) * 256], Sf[:, hh * 256:(hh + 1) * 256], decf[:, hh, c:c + 1], pv,
                    ALU.mult, ALU.add), reads=[Sf, decf, ps], writes=[Sf])
        po = []
        for half in range(2):
            p = next_ps(K)
            po.append(p)
            for hh in range(4):
                cols = hh * 128
                ec = hh * 256 + half * 128
                for ch in range(2):
                    cc = cols + ch * 64
                    r0 = ch * 64
                    fw.op("pe", lambda e: e.matmul(
                        p[:, cc:cc + 64], v[r0:r0 + 64, ec:ec + 128], af[r0:r0 + 64, hh, r0:r0 + 64], start=True, stop=False),
                        reads=[v, af], writes=[p])
                    fw.op("pe", lambda e: e.matmul(
                        p[:, cc:cc + 64], v[r0:r0 + 64, ec:ec + 128], ab[r0:r0 + 64, hh, r0:r0 + 64], start=False, stop=False),
                        reads=[v, ab], writes=[p])
                    fw.op("pe", lambda e: e.matmul(
                        p[:, cc:cc + 64], rpf[ch][:, ec:ec + 128], it["QBF"][:, hh, r0:r0 + 64],
                        start=False, stop=False), reads=[rpf[ch], it["QBF"]], writes=[p])
                    fw.op("pe", lambda e: e.matmul(
                        p[:, cc:cc + 64], rpb[:, ch, ec:ec + 128], it["QBB"][:, hh, r0:r0 + 64],
                        start=False, stop=True), reads=[rpb, it["QBB"]], writes=[p])
        for half in range(2):
            fw.op("act", lambda e, half=half, p=po[half]: e.activation(sq[:, half, :], p[:, 0:512], AF.Square),
                  reads=[po[half]], writes=[sq])
        rms_rstd(K, sq, 2, 512, 256, rstd)
        a = at[j]
        for half in range(2):
            fw.op("dve", lambda e, p=po[half]: e.tensor_tensor(tmpo[:, :], p[:, 0:512], rstd[:, :], ALU.mult),
                  reads=[po[half], rstd], writes=[tmpo])
            fw.op("dve", lambda e, half=half, a=a, sgl=sgl: e.tensor_tensor(
                a[:, half::2, :], tmpo[:, :].rearrange("p (h n) -> p h n", h=4), sgl[:, half::2, :], ALU.mult),
                reads=[tmpo, sgl], writes=[a])
        fw.dma("sp", K.AT.ap.rearrange("(c p) t -> p c t", p=128)[:, :, tsl], a[:, :, :], reads=[a])
    fw.barrier()
    sb.release(m0)


def make_tables(K):
    fw, sb, T, S2 = K.fw, K.sb, K.T, K.S2
    m0 = sb.mark()
    tabc = K.cst["tabc"]
    CH = min(1024, T)
    pi = sb.alloc([CH], I32, name="tpi")
    pf = sb.alloc([CH], F32, name="tpf")
    u = sb.alloc([CH], F32, name="tu")
    kf = sb.alloc([CH], F32, name="tkf")
    TWO_PI = float(2 * np.pi * (1 - 1e-6))
    for dst, pos, n, cc, pc in ((K.TRC, K.pos_loc, T, 0, 2), (K.TRS, K.pos_loc, T, 0, 3),
                                (K.TQ, K.pos_loc, T, 1, 4), (K.TK, K.pos_all, S2, 1, 4)):
        for c0 in range(0, n, CH):
            fw.dma("sp", pi[:, :], pos[:, c0:c0 + CH].partition_broadcast(128), writes=[pi])
            fw.op("dve", lambda e: e.tensor_copy(pf[:, :], pi[:, :]), reads=[pi], writes=[pf])
            fw.op("dve", lambda e: e.tensor_scalar(u[:, :], pf[:, :], tabc[:, cc:cc + 1], tabc[:, pc:pc + 1], ALU.mult, ALU.add),
                  reads=[pf, tabc], writes=[u])
            fw.op("dve", lambda e: e.tensor_copy(pi[:, :], u[:, :]), reads=[u], writes=[pi])
            fw.op("dve", lambda e: e.tensor_copy(kf[:, :], pi[:, :]), reads=[pi], writes=[kf])
            fw.op("dve", lambda e: e.tensor_tensor(u[:, :], u[:, :], kf[:, :], ALU.subtract), reads=[u, kf], writes=[u])
            fw.op("dve", lambda e: e.tensor_scalar(kf[:, :], u[:, :], 0.5, None, ALU.is_gt), reads=[u], writes=[kf])
            fw.op("dve", lambda e: e.tensor_tensor(u[:, :], u[:, :], kf[:, :], ALU.subtract), reads=[u, kf], writes=[u])
            fw.op("act", lambda e: e.activation(u[:, :], u[:, :], AF.Sin, scale=TWO_PI), reads=[u], writes=[u])
            fw.dma("sp", dst[:, c0:c0 + CH], u[:, :], reads=[u])
    fw.barrier()
    sb.release(m0)


def even_layer(K, i):
    sb = K.sb
    mL = sb.mark()
    c = Ctx()
    K.ev = c
    fw = K.fw
    c.ones96 = sb.alloc([128], BF16, name="ones96")
    fw.op("dve", lambda e: e.tensor_copy(c.ones96[:, :], K.cst["ones96"][:, :]), reads=[K.cst["ones96"]], writes=[c.ones96])
    c.sel = sb.alloc([128], BF16, name="selb")
    fw.op("dve", lambda e: e.tensor_copy(c.sel[:, :], K.cst["sel"][:, :]), reads=[K.cst["sel"]], writes=[c.sel])
    c.mtmp = [(sb.alloc([1, 512], BF16, name=f"msq{j}"), sb.alloc([512], F32, name=f"mrstd{j}"),
               sb.alloc([512], F32, name=f"my{j}"), sb.alloc([512], BF16, name=f"my2{j}")) for j in range(4)]
    c.mcnt = 0
    skip = K.dbg.get("skip", ())
    even_e1(K, i)
    if "e2" not in skip:
        even_e2(K, i)
    if "e3" not in skip:
        even_e3(K, i)
    if "e4" not in skip:
        even_e4(K, i)
    sb.release(mL)


def head_gain_col(K, vec96, scale, name):
    fw, sb = K.fw, K.sb
    t = sb.alloc([1], F32, name=name)
    v = vec96.rearrange("(p o) -> p o", o=1)
    fw.dma("sp", t[0:96, :], v[0:96, :], writes=[t])
    fw.dma("sp", t[96:112, :], v[80:96, :], writes=[t])
    fw.dma("sp", t[112:128, :], v[64:80, :], writes=[t])
    if scale != 1.0:
        fw.op("dve", lambda e: e.tensor_scalar(t[:, :], t[:, :], float(scale), None, ALU.mult), reads=[t], writes=[t])
    return t


def mla_post(K, ps, gcol, tab, TT, out):
    fw, c = K.fw, K.ev
    msq, mrstd, my, my2 = c.mtmp[c.mcnt % 4]
    c.mcnt += 1
    fw.op("act", lambda e: e.activation(msq[:, 0, 0:TT], ps[:, 0:TT], AF.Square), reads=[ps], writes=[msq])
    rms_rstd(K, msq, 1, TT, 96, mrstd, ones=c.ones96)
    fw.op("dve", lambda e: e.scalar_tensor_tensor(my[:, 0:TT], ps[:, 0:TT], gcol[:, 0:1], tab[:, 0:TT], ALU.mult, ALU.mult),
          reads=[ps, gcol, tab, msq], writes=[my])
    fw.op("dve", lambda e: e.tensor_tensor(my2[:, 0:TT], my[:, 0:TT], mrstd[:, 0:TT], ALU.mult),
          reads=[my, mrstd], writes=[my2])
    p2 = ps
    fw.op("pe", lambda e: e.matmul(p2[0:96, 0:TT], c.sel[:, 0:96], my2[:, 0:TT], start=True, stop=True),
          reads=[c.sel, my2], writes=[p2])
    fw.op("act", lambda e: e.copy(out[0:96, 0:TT], p2[0:96, 0:TT]), reads=[p2], writes=[out])


def norm_chunks(K, src, dst, sq, rstd, gcol, nch, dim, TT):
    fw = K.fw
    for c_ in range(nch):
        fw.op("act", lambda e: e.activation(sq[:, c_, 0:TT], src[:, c_, 0:TT], AF.Square), reads=[src], writes=[sq])
    rms_rstd(K, sq, nch, TT, dim, rstd)
    for c_ in range(nch):
        fw.op("dve", lambda e: e.scalar_tensor_tensor(dst[:, c_, 0:TT], src[:, c_, 0:TT], gcol[:, c_:c_ + 1], rstd[:, 0:TT],
                                                      ALU.mult, ALU.mult), reads=[src, gcol, rstd], writes=[dst])


def even_e1(K, i):
    fw, sb, T = K.fw, K.sb, K.T
    m0 = sb.mark()
    TT = min(512, T)
    NS = TT // 128
    w = sb.alloc([8, 2720], BF16, name="w_in_even")
    load_w_bf16(K, w, K.w["w_in_even"][i], 8, 2720)
    wrot = sb.alloc([8, 1024], BF16, name="wrot")
    for kc in range(8):
        src = w[:, kc, 672:1696].rearrange("p (h two d) -> p h two d", two=2, d=32)
        dst = wrot[:, kc, :].rearrange("p (h two d) -> p h two d", two=2, d=32)
        fw.op("act", lambda e: e.mul(dst[:, :, 0, :], src[:, :, 1, :], -1.0), reads=[w], writes=[wrot])
        fw.op("dve", lambda e: e.tensor_copy(dst[:, :, 1, :], src[:, :, 0, :]), reads=[w], writes=[wrot])
    wq = sb.alloc([3, 768], BF16, name="wq")
    load_w_bf16(K, wq, K.w["mla_w_uq"][i], 3, 768)
    wqx = sb.alloc([3, 1024], BF16, name="wqx")
    for kc in range(3):
        src = wq[:, kc, :].rearrange("p (h d) -> p h d", d=96)
        dst = wqx[:, kc, :].rearrange("p (h d) -> p h d", d=128)
        fw.op("dve", lambda e: e.tensor_copy(dst[:, :, 0:96], src[:, :, 0:96]), reads=[wq], writes=[wqx])
        fw.op("act", lambda e: e.mul(dst[:, :, 96:112], src[:, :, 80:96], -1.0), reads=[wq], writes=[wqx])
        fw.op("dve", lambda e: e.tensor_copy(dst[:, :, 112:128], src[:, :, 64:80]), reads=[wq], writes=[wqx])
    g1 = sb.alloc([8], F32, name="eg1")
    load_col(K, g1, K.w["mix_norm_even"][i], 8)
    gq3 = sb.alloc([3], F32, name="gq3")
    load_col(K, gq3, K.w["mla_q_norm"][i], 3)
    gkv2 = sb.alloc([2], F32, name="gkv2")
    load_col(K, gkv2, K.w["mla_kv_norm"][i], 2)
    gr4 = sb.alloc([4], F32, name="gr4")
    load_col(K, gr4, K.w["ret_out_norm"][i].rearrange("h e -> (h e)"), 4)
    Gq = head_gain_col(K, K.w["mla_q_head_norm"][i], 96 ** -0.5, "Gq")
    XTv = K.xT.ap.rearrange("(c p) t -> p c t", p=128)
    x_t = [sb.alloc([8, TT], F32, name=f"ex{j}") for j in range(2)]
    h = sb.alloc([8, TT], BF16, name="eh")
    sq = sb.alloc([8, TT], BF16, name="esq")
    rstd = sb.alloc([TT], F32, name="erstd")
    rstd2 = sb.alloc([TT], F32, name="erstd2")
    cq = sb.alloc([3, TT], F32, name="cq")
    cqn = sb.alloc([3, TT], BF16, name="cqn")
    ckn = sb.alloc([2, TT], F32, name="ckn")
    kr = sb.alloc([TT], F32, parts=32, name="kr")
    tq = sb.alloc([TT], F32, name="tq")
    tcs = sb.alloc([TT], F32, name="tcos")
    tsn = sb.alloc([TT], F32, name="tsin")
    qo = [sb.alloc([TT], BF16, name=f"qo{j}") for j in range(2)]
    t1 = sb.alloc([TT], F32, name="t1")
    t2 = sb.alloc([TT], F32, name="t2")
    rq = sb.alloc([4, TT], BF16, name="rq")
    rk = sb.alloc([4, TT], BF16, name="rk")
    rv = [sb.alloc([512], BF16, name=f"rv{j}") for j in range(2)]
    sg = sb.alloc([4, TT], BF16, name="esg")
    tq_t = [tq, sb.alloc([TT], F32, name="tq1")]
    tcs_t = [tcs, sb.alloc([TT], F32, name="tcos1")]
    tsn_t = [tsn, sb.alloc([TT], F32, name="tsin1")]

    def load_e1(ti):
        t0 = ti * TT
        fw.dma("sp", x_t[ti % 2][:, :, :], XTv[:, :, t0:t0 + TT], writes=[x_t[ti % 2]])
        fw.dma("sp", tq_t[ti % 2][:, :], K.TQ[:, t0:t0 + TT], writes=[tq_t[ti % 2]])
        fw.dma("sp", tcs_t[ti % 2][:, :], K.TRC[:, t0:t0 + TT], writes=[tcs_t[ti % 2]])
        fw.dma("sp", tsn_t[ti % 2][:, :], K.TRS[:, t0:t0 + TT], writes=[tsn_t[ti % 2]])

    load_e1(0)
    for ti in range(T // TT):
        t0 = ti * TT
        xt, tq, tcs, tsn = x_t[ti % 2], tq_t[ti % 2], tcs_t[ti % 2], tsn_t[ti % 2]
        if ti + 1 < T // TT:
            load_e1(ti + 1)
        norm_tile(K, xt, h, sq, rstd, g1, TT)
        for c_ in range(3):
            ps = proj_fm(K, w, c_ * 128, 128, h, TT)
            fw.op("act", lambda e: e.copy(cq[:, c_, :], ps[:, 0:TT]), reads=[ps], writes=[cq])
        norm_chunks(K, cq, cqn, sq, rstd2, gq3, 3, 384, TT)
        fw.dma("sp", K.CQN.ap.rearrange("(c p) t -> p c t", p=128)[:, :, t0:t0 + TT], cqn[:, :, :], reads=[cqn])
        for c_ in range(2):
            ps = proj_fm(K, w, 384 + c_ * 128, 128, h, TT)
            fw.op("act", lambda e: e.copy(cq[:, c_, :], ps[:, 0:TT]), reads=[ps], writes=[cq])
        norm_chunks(K, cq, ckn, sq, rstd2, gkv2, 2, 256, TT)
        fw.dma("sp", K.LAT[ti].ap[0:256, :].rearrange("(c p) t -> p c t", p=128), ckn[:, :, :], reads=[ckn])
        ps = proj_fm(K, w, 640, 32, h, TT)
        fw.op("act", lambda e: e.copy(kr[:, :], ps[0:32, 0:TT]), reads=[ps], writes=[kr])
        fw.dma("sp", K.LAT[ti][256:288, :], kr[:, :], reads=[kr])
        for hh in range(8):
            ps = next_ps(K)
            for kc in range(3):
                fw.op("pe", lambda e: e.matmul(ps[:, 0:TT], wqx[:, kc, hh * 128:(hh + 1) * 128], cqn[:, kc, :],
                                               start=(kc == 0), stop=(kc == 2)), reads=[wqx, cqn], writes=[ps])
            o = qo[hh % 2]
            mla_post(K, ps, Gq, tq, TT, o)
            fw.dma("sp", K.QT.ap[hh, :, t0:t0 + TT], o[0:96, :], reads=[o])
        for which, c0, r0, dst, scl in (("q", 672, 0, rq, 1.0), ("k", 1184, 512, rk, 0.125)):
            for c_ in range(4):
                pa = proj_fm(K, w, c0 + c_ * 128, 128, h, TT)
                pr = proj_fm(K, wrot, r0 + c_ * 128, 128, h, TT)
                fw.op("dve", lambda e: e.scalar_tensor_tensor(t1[:, :], pa[:, 0:TT], float(scl), tcs[:, :], ALU.mult, ALU.mult),
                      reads=[pa, tcs], writes=[t1])
                fw.op("dve", lambda e: e.scalar_tensor_tensor(t2[:, :], pr[:, 0:TT], float(scl), tsn[:, :], ALU.mult, ALU.mult),
                      reads=[pr, tsn], writes=[t2])
                fw.op("dve", lambda e: e.tensor_tensor(dst[:, c_, :], t1[:, :], t2[:, :], ALU.add), reads=[t1, t2], writes=[dst])
        fw.dma("sp", K.RQ.ap.rearrange("(c p) t -> p c t", p=128)[:, :, t0:t0 + TT], rq[:, :, :], reads=[rq])
        fw.dma("sp", K.RK.ap.rearrange("(c p) t -> p c t", p=128)[:, :, t0:t0 + TT], rk[:, :, :], reads=[rk])
        for s_ in range(NS):
            ps = proj_tm(K, w, 1696, 512, h, s_ * 128)
            r = rv[s_ % 2]
            fw.op("act", lambda e: e.copy(r[:, :], ps[:, 0:512]), reads=[ps], writes=[r])
            fw.dma("sp", K.RV[t0 + s_ * 128:t0 + (s_ + 1) * 128, :], r[:, :], reads=[r])
        for c_ in range(4):
            ps = proj_fm(K, w, 2208 + c_ * 128, 128, h, TT)
            fw.op("act", lambda e: e.activation(t1[:, :], ps[:, 0:TT], AF.Silu), reads=[ps], writes=[t1])
            fw.op("dve", lambda e: e.tensor_scalar(sg[:, c_, :], t1[:, :], gr4[:, c_:c_ + 1], None, ALU.mult),
                  reads=[t1, gr4], writes=[sg])
        fw.dma("sp", K.SGT.ap[0:512, :].rearrange("(c p) t -> p c t", p=128)[:, :, t0:t0 + TT], sg[:, :, :], reads=[sg])
    fw.barrier()
    for j in range(len(K.LAT)):
        fw.collective("AllGather", ALU.bypass, K.groups, K.LAT[j].ap, K.LATG[j].ap)
    fw.barrier()
    sb.release(m0)


def even_e2(K, i):
    fw, sb, T, S2 = K.fw, K.sb, K.T, K.S2
    c = K.ev
    m0 = sb.mark()
    TT = min(512, T)
    wkv = sb.alloc([2, 1024], BF16, name="wkv")
    load_w_bf16(K, wkv, K.w["mla_w_ukv"][i], 2, 1024)
    wkx = sb.alloc([2, 1024], BF16, name="wkx")
    wv = sb.alloc([2, 512], BF16, name="wv")
    fw.op("dve", lambda e: e.memset(wkx[:, :, :], 0.0), writes=[wkx])
    for kc in range(2):
        src = wkv[:, kc, :].rearrange("p (h d) -> p h d", d=128)
        fw.op("dve", lambda e: e.tensor_copy(wkx[:, kc, :].rearrange("p (h d) -> p h d", d=128)[:, :, 0:64], src[:, :, 0:64]),
              reads=[wkv], writes=[wkx])
        fw.op("act", lambda e: e.copy(wv[:, kc, :].rearrange("p (h d) -> p h d", d=64), src[:, :, 64:128]),
              reads=[wkv], writes=[wv])
    rmat = sb.alloc([128], BF16, parts=32, name="rmat")
    fw.op("dve", lambda e: e.tensor_copy(rmat[:, :], K.cst["ropemat"][0:32, :]), reads=[K.cst["ropemat"]], writes=[rmat])
    Gk = head_gain_col(K, K.w["mla_k_head_norm"][i], 1.0, "Gk")
    K.ps_mod = 8
    lat = [sb.alloc([2, TT], F32, name=f"lat{j}") for j in range(2)]
    krf = [sb.alloc([TT], F32, parts=32, name=f"krf{j}") for j in range(2)]
    ckb = sb.alloc([2, TT], BF16, name="ckb")
    krb = sb.alloc([TT], BF16, parts=32, name="krb")
    tk = [sb.alloc([TT], F32, name=f"tk{j}") for j in range(2)]
    ko = [sb.alloc([TT], BF16, name=f"ko{j}") for j in range(4)]
    vx = [sb.alloc([8, 65], BF16, name=f"vx{j}") for j in range(2)]
    for j in range(2):
        fw.op("dve", lambda e: e.memset(vx[j][:, :, :], 1.0), writes=[vx[j]])
    def load_e2(kt):
        k0 = kt * TT
        rr = k0 // T
        lj = (k0 - rr * T) // TT
        fw.dma("sp", lat[kt % 2][:, :, :], K.LATG[lj].ap[rr * 288:rr * 288 + 256, :].rearrange("(c p) t -> p c t", p=128), writes=[lat[kt % 2]])
        fw.dma("sp", krf[kt % 2][:, :], K.LATG[lj][rr * 288 + 256:rr * 288 + 288, :], writes=[krf[kt % 2]])
        fw.dma("sp", tk[kt % 2][:, :], K.TK[:, k0:k0 + TT], writes=[tk[kt % 2]])

    load_e2(0)
    for kt in range(S2 // TT):
        k0 = kt * TT
        la, kf_, tkt = lat[kt % 2], krf[kt % 2], tk[kt % 2]
        if kt + 1 < S2 // TT:
            load_e2(kt + 1)
        fw.op("act", lambda e: e.copy(ckb[:, :, :], la[:, :, :]), reads=[la], writes=[ckb])
        fw.op("dve", lambda e: e.tensor_copy(krb[:, :], kf_[:, :]), reads=[kf_], writes=[krb])
        for hh in range(8):
            ps = next_ps(K)
            for kc in range(2):
                fw.op("pe", lambda e: e.matmul(ps[:, 0:TT], wkx[:, kc, hh * 128:(hh + 1) * 128], ckb[:, kc, :],
                                               start=(kc == 0), stop=False), reads=[wkx, ckb], writes=[ps])
            fw.op("pe", lambda e: e.matmul(ps[:, 0:TT], rmat[0:32, :], krb[0:32, :], start=False, stop=True),
                  reads=[rmat, krb], writes=[ps])
            o = ko[hh % 4]
            mla_post(K, ps, Gk, tkt, TT, o)
            fw.dma("sp", K.KT.ap[hh, :, k0:k0 + TT], o[0:96, :], reads=[o])
        for s_ in range(TT // 128):
            ps = next_ps(K)
            for kc in range(2):
                fw.op("pe", lambda e: e.matmul(ps[:, 0:512], ckb[:, kc, s_ * 128:(s_ + 1) * 128], wv[:, kc, :],
                                               start=(kc == 0), stop=(kc == 1)), reads=[ckb, wv], writes=[ps])
            v = vx[s_ % 2]
            fw.op("act", lambda e: e.copy(v[:, :, 0:64], ps[:, 0:512].rearrange("p (h d) -> p h d", d=64)), reads=[ps], writes=[v])
            fw.dma("sp", K.VX[k0 + s_ * 128:k0 + (s_ + 1) * 128, :], v[:, :, :].rearrange("p h d -> p (h d)"), reads=[v])
    K.ps_mod = 6
    fw.barrier()
    sb.release(m0)


EXP_SHIFT = -8.0


def even_e3(K, i):
    fw, sb, T, S2 = K.fw, K.sb, K.T, K.S2
    m0 = sb.mark()
    TQ = min(512, T)
    NKT = S2 // 128
    NP = NKT // 2
    NQT = T // TQ
    kt_t = [sb.alloc([S2], BF16, name=f"akt{j}") for j in range(2)]
    vx_t = [sb.alloc([NKT, 65], BF16, name=f"avx{j}") for j in range(2)]
    q_t = [sb.alloc([T], BF16, name=f"aq{j}") for j in range(2)]
    pt = [sb.alloc([2, TQ], BF16, name=f"ap{j}") for j in range(3)]
    o65 = sb.alloc([512], F32, name="o65")
    rden = sb.alloc([512], F32, name="rden")
    ao = [sb.alloc([512], BF16, name=f"ao{j}") for j in range(2)]
    shift = sb.alloc([1], F32, name="shift")
    fw.op("dve", lambda e: e.memset(shift[:, :], EXP_SHIFT), writes=[shift])
    sel65 = K.cst["sel65"]
    VXv = K.VX.ap.rearrange("(kt p) c -> p kt c", p=128)
    spair = [Tile(K.psum_full[:, b * 1024:(b + 1) * 1024].rearrange("p (b n) -> p b n", b=2), f"spair{b}") for b in range(3)]

    def load_head(hh):
        fw.dma("sp", kt_t[hh % 2][0:96, :], K.KT.ap[hh], writes=[kt_t[hh % 2]])
        fw.dma("sp", vx_t[hh % 2][:, :, :], VXv[:, :, hh * 65:(hh + 1) * 65], writes=[vx_t[hh % 2]])
        fw.dma("sp", q_t[hh % 2][0:96, :], K.QT.ap[hh], writes=[q_t[hh % 2]])

    items = [(hh, qt, j) for hh in range(8) for qt in range(NQT) for j in range(NP)]
    load_head(0)
    load_head(1)
    for idx in range(len(items) + 2):
        if idx < len(items):
            hh, qt, j = items[idx]
            kt_, q_ = kt_t[hh % 2], q_t[hh % 2]
            sp = spair[idx % 3]
            for u in range(2):
                kt = 2 * j + u
                fw.op("pe", lambda e: e.matmul(sp[:, u, 0:TQ], kt_[0:96, kt * 128:(kt + 1) * 128], q_[0:96, qt * TQ:(qt + 1) * TQ],
                                               start=True, stop=True), reads=[kt_, q_], writes=[sp])
        if idx >= 2:
            k = idx - 2
            hh, qt, j = items[k]
            g = hh * NQT + qt
            vx_ = vx_t[hh % 2]
            sp = spair[k % 3]
            p = pt[k % 3]
            po = K.ps[6 + (g % 2)]
            if K.dbg.get("exp2", True):
                fw.op("act", lambda e: e.activation(p[:, :, :], sp[:, :, 0:TQ], AF.Exp, bias=shift[:, 0:1], scale=1.0),
                      reads=[sp, shift], writes=[p])
            else:
                for u in range(2):
                    fw.op("act", lambda e: e.activation(p[:, u, :], sp[:, u, 0:TQ], AF.Exp, bias=shift[:, 0:1], scale=1.0),
                          reads=[sp, shift], writes=[p])
            for u in range(2):
                kt = 2 * j + u
                fw.op("pe", lambda e: e.matmul(po[0:65, 0:TQ], vx_[:, kt, :], p[:, u, :], start=(kt == 0), stop=(kt == NKT - 1)),
                      reads=[vx_, p], writes=[po])
            if j == NP - 1:
                fw.op("act", lambda e: e.copy(o65[0:65, 0:TQ], po[0:65, 0:TQ]), reads=[po], writes=[o65])
                fw.op("pe", lambda e: e.matmul(po[0:64, 0:TQ], sel65[0:65, 0:64], o65[0:65, 0:TQ], start=True, stop=True),
                      reads=[sel65, o65], writes=[po])
                fw.op("dve", lambda e: e.reciprocal(rden[0:64, 0:TQ], po[0:64, 0:TQ]), reads=[po], writes=[rden])
                a = ao[g % 2]
                fw.op("dve", lambda e: e.tensor_tensor(a[0:64, 0:TQ], o65[0:64, 0:TQ], rden[0:64, 0:TQ], ALU.mult),
                      reads=[o65, rden], writes=[a])
                fw.dma("sp", K.AT[hh * 64:(hh + 1) * 64, qt * TQ:(qt + 1) * TQ], a[0:64, 0:TQ], reads=[a])
                if qt == NQT - 1 and hh + 2 < 8:
                    load_head(hh + 2)
    fw.barrier()
    sb.release(m0)


def even_e4(K, i):
    fw, sb, T = K.fw, K.sb, K.T
    m0 = sb.mark()
    NCH = T // 128
    tabc = K.cst["tabc"]
    th = sb.alloc([16], F32, name="rth")
    fw.dma("sp", th[:, 0:8], K.w["ret_theta_fwd"][i:i + 1, :].partition_broadcast(128), writes=[th])
    fw.dma("sp", th[:, 8:16], K.w["ret_theta_bwd"][i:i + 1, :].partition_broadcast(128), writes=[th])
    lg = sb.alloc([16], F32, name="rlg")
    fw.op("act", lambda e: e.activation(lg[:, :], th[:, :], AF.Exp, scale=-float(np.log(2.0))), reads=[th], writes=[lg])
    fw.op("act", lambda e: e.activation(lg[:, :], lg[:, :], AF.Ln, scale=-1.0, bias=1.0), reads=[lg], writes=[lg])
    mask = sb.alloc([8, 128], BF16, name="rmask")
    tm1 = sb.alloc([128], F32, name="rtm1")
    tm2 = sb.alloc([128], F32, name="rtm2")
    zf = sb.alloc([8], F32, name="rzf")
    zb = sb.alloc([8], F32, name="rzb")
    xf = sb.alloc([8, 128], F32, name="rxf")
    xb = sb.alloc([8, 128], F32, name="rxb")
    dec = sb.alloc([2, 512], F32, name="rdec")
    cs = K.cst
    for hh in range(8):
        lf, lb = lg[:, hh:hh + 1], lg[:, 8 + hh:9 + hh]
        fw.op("act", lambda e: e.activation(tm1[:, :], cs["relf"][:, :], AF.Exp, scale=lf), reads=[cs["relf"], lg], writes=[tm1])
        fw.op("dve", lambda e: e.tensor_tensor(tm1[:, :], tm1[:, :], cs["mf128"][:, :], ALU.mult), reads=[tm1, cs["mf128"]], writes=[tm1])
        fw.op("act", lambda e: e.activation(tm2[:, :], cs["relb"][:, :], AF.Exp, scale=lb), reads=[cs["relb"], lg], writes=[tm2])
        fw.op("dve", lambda e: e.tensor_tensor(tm2[:, :], tm2[:, :], cs["mb128"][:, :], ALU.mult), reads=[tm2, cs["mb128"]], writes=[tm2])
        fw.op("dve", lambda e: e.tensor_tensor(mask[:, hh, :], tm1[:, :], tm2[:, :], ALU.add), reads=[tm1, tm2], writes=[mask])
        fw.op("act", lambda e: e.activation(zf[:, hh:hh + 1], tabc[:, 6:7], AF.Exp, scale=lf), reads=[tabc, lg], writes=[zf])
        fw.op("act", lambda e: e.activation(zb[:, hh:hh + 1], tabc[:, 5:6], AF.Exp, scale=lb), reads=[tabc, lg], writes=[zb])
        fw.op("act", lambda e: e.activation(xf[:, hh, :], cs["iota1"][:, :], AF.Exp, scale=lf), reads=[cs["iota1"], lg], writes=[xf])
        fw.op("act", lambda e: e.activation(xb[:, hh, :], cs["iotar"][:, :], AF.Exp, scale=lb), reads=[cs["iotar"], lg], writes=[xb])
        fw.op("act", lambda e: e.activation(dec[:, 0, hh * 64:(hh + 1) * 64], cs["c128"][:, 0:64], AF.Exp, scale=lf), reads=[cs["c128"], lg], writes=[dec])
        fw.op("act", lambda e: e.activation(dec[:, 1, hh * 64:(hh + 1) * 64], cs["c128"][:, 0:64], AF.Exp, scale=lb), reads=[cs["c128"], lg], writes=[dec])
    ident = sb.alloc([128], BF16, name="rident")
    fw.op("dve", lambda e: e.tensor_copy(ident[:, :], cs["ident"][:, :]), reads=[cs["ident"]], writes=[ident])
    ss = [sb.alloc([NCH, 512], BF16, name=f"rss{d}") for d in range(2)]
    rk_t = [sb.alloc([T], BF16, name=f"rrk{j}") for j in range(2)]
    rq_t = [sb.alloc([T], BF16, name=f"rrq{j}") for j in range(2)]
    rv_t = [sb.alloc([NCH, 64], BF16, name=f"rrv{j}") for j in range(2)]
    kz = [sb.alloc([NCH, 64], BF16, name=f"rkz{d}") for d in range(2)]
    RVv = K.RV.ap.rearrange("(c p) n -> p c n", p=128)
    for hh in range(8):
        rk_h, rv_h = rk_t[hh % 2], rv_t[hh % 2]
        fw.dma("sp", rk_h[0:64, :], K.RK[hh * 64:(hh + 1) * 64, :], writes=[rk_h])
        fw.dma("sp", rv_h[:, :, :], RVv[:, :, hh * 64:(hh + 1) * 64], writes=[rv_h])
        for c0 in range(0, NCH, 8):
            nb = min(8, NCH - c0)
            ps = next_ps(K)
            psb = ps.ap.bitcast(BF16)
            for c_ in range(nb):
                fw.op("pe", lambda e: e.transpose(psb[:, c_ * 64:(c_ + 1) * 64], rk_h[0:64, (c0 + c_) * 128:(c0 + c_ + 1) * 128],
                                                  ident[0:64, 0:64]), reads=[rk_h, ident], writes=[ps])
            pv = psb[:, 0:nb * 64].rearrange("p (c d) -> p c d", d=64)
            fw.op("dve", lambda e: e.tensor_scalar(kz[0][:, c0:c0 + nb, :], pv, zf[:, hh:hh + 1], None, ALU.mult),
                  reads=[ps, zf], writes=[kz[0]])
            fw.op("dve", lambda e: e.tensor_scalar(kz[1][:, c0:c0 + nb, :], pv, zb[:, hh:hh + 1], None, ALU.mult),
                  reads=[ps, zb], writes=[kz[1]])
        for c0 in range(0, NCH, 8):
            nb = min(8, NCH - c0)
            for d in range(2):
                ps = next_ps(K)
                for c_ in range(nb):
                    fw.op("pe", lambda e: e.matmul(ps[0:64, c_ * 64:(c_ + 1) * 64], kz[d][:, c0 + c_, :], rv_h[:, c0 + c_, :],
                                                   start=True, stop=True), reads=[kz[d], rv_h], writes=[ps])
                fw.op("act", lambda e: e.copy(ss[d][0:64, c0:c0 + nb, hh * 64:(hh + 1) * 64],
                                              ps[0:64, 0:nb * 64].rearrange("p (c d) -> p c d", d=64)), reads=[ps], writes=[ss[d]])
    R = [sb.alloc([512], F32, name=f"rR{d}") for d in range(2)]
    order = [list(range(NCH)), list(reversed(range(NCH)))]
    for d in range(2):
        fw.op("dve", lambda e: e.memset(R[d][:, :], 0.0), writes=[R[d]])
        for c_ in order[d]:
            fw.op("dve", lambda e: e.tensor_tensor(R[d][0:64, :], R[d][0:64, :], dec[0:64, d, :], ALU.mult), reads=[R[d], dec], writes=[R[d]])
            fw.op("dve", lambda e: e.tensor_tensor(R[d][0:64, :], R[d][0:64, :], ss[d][0:64, c_, :], ALU.add), reads=[R[d], ss[d]], writes=[R[d]])
        fw.dma("sp", K.RST[:, d * 512:(d + 1) * 512], R[d][0:64, :], reads=[R[d]], writes=[K.RST])
    fw.collective("AllGather", ALU.bypass, K.groups, K.RST.ap, K.RSG.ap, reads=[K.RST], writes=[K.RSG])
    ini = [sb.alloc([512], F32, name=f"rini{d}") for d in range(2)]
    fw.dma("sp", ini[0][0:64, :], K.RSG[0:64, 0:512], reads=[K.RSG], writes=[ini[0]])
    fw.dma("sp", ini[1][0:64, :], K.RSG[64:128, 512:1024], reads=[K.RSG], writes=[ini[1]])
    rpt = sb.alloc([512], BF16, name="rrpt")
    for d in range(2):
        fw.op("dve", lambda e: e.tensor_scalar(R[d][0:64, :], ini[d][0:64, :], K.flg[0:64, d:d + 1], None, ALU.mult),
              reads=[ini[d], K.flg], writes=[R[d]])
        for c_ in order[d]:
            fw.op("act", lambda e: e.copy(rpt[0:64, :], R[d][0:64, :]), reads=[R[d]], writes=[rpt])
            fw.op("dve", lambda e: e.tensor_tensor(R[d][0:64, :], R[d][0:64, :], dec[0:64, d, :], ALU.mult), reads=[R[d], dec], writes=[R[d]])
            fw.op("dve", lambda e: e.tensor_tensor(R[d][0:64, :], R[d][0:64, :], ss[d][0:64, c_, :], ALU.add), reads=[R[d], ss[d]], writes=[R[d]])
            fw.op("act", lambda e: e.copy(ss[d][0:64, c_, :], rpt[0:64, :]), reads=[rpt], writes=[ss[d]])
    qx = [sb.alloc([T], BF16, name=f"rqx{d}") for d in range(2)]
    at = [sb.alloc([128], BF16, name=f"rat{j}") for j in range(2)]
    sq = sb.alloc([1, 512], BF16, name="rsq")
    rstd = sb.alloc([512], F32, name="rrstd")
    tmo = sb.alloc([512], F32, name="rtmo")
    sgt = [sb.alloc([512], BF16, name=f"rsg{j}") for j in range(2)]
    oo = [sb.alloc([512], BF16, name=f"roo{j}") for j in range(2)]
    acnt = 0
    GC = min(4, NCH)
    for hh in range(8):
        rk_h, rq_h, rv_h = rk_t[hh % 2], rq_t[hh % 2], rv_t[hh % 2]
        fw.dma("sp", rk_h[0:64, :], K.RK[hh * 64:(hh + 1) * 64, :], writes=[rk_h])
        fw.dma("sp", rq_h[0:64, :], K.RQ[hh * 64:(hh + 1) * 64, :], writes=[rq_h])
        fw.dma("sp", rv_h[:, :, :], RVv[:, :, hh * 64:(hh + 1) * 64], writes=[rv_h])
        for d, xt_ in ((0, xf), (1, xb)):
            fw.op("dve", lambda e: e.tensor_tensor(
                qx[d][0:64, :].rearrange("p (c n) -> p c n", n=128), rq_h[0:64, :].rearrange("p (c n) -> p c n", n=128),
                xt_[0:64, hh, :].rearrange("p (o n) -> p o n", o=1).to_broadcast([64, NCH, 128]), ALU.mult),
                reads=[rq_h, xt_], writes=[qx[d]])
        for g0 in range(0, NCH, GC):
            gi = g0 // GC
            po = K.ps[6 + (gi % 2)]
            W = GC * 128
            sgl = sgt[gi % 2]
            fw.dma("sp", sgl[0:64, 0:W], K.SGT[hh * 64:(hh + 1) * 64, g0 * 128:g0 * 128 + W], writes=[sgl])
            for cc in range(GC):
                c_ = g0 + cc
                csl = slice(c_ * 128, (c_ + 1) * 128)
                ps = next_ps(K)
                fw.op("pe", lambda e: e.matmul(ps[:, 0:128], rk_h[0:64, csl], rq_h[0:64, csl], start=True, stop=True),
                      reads=[rk_h, rq_h], writes=[ps])
                a = at[acnt % 2]
                acnt += 1
                fw.op("dve", lambda e: e.tensor_tensor(a[:, :], ps[:, 0:128], mask[:, hh, :], ALU.mult), reads=[ps, mask], writes=[a])
                osl = slice(cc * 128, (cc + 1) * 128)
                fw.op("pe", lambda e: e.matmul(po[0:64, osl], rv_h[:, c_, :], a[:, :], start=True, stop=False),
                      reads=[rv_h, a], writes=[po])
                fw.op("pe", lambda e: e.matmul(po[0:64, osl], ss[0][0:64, c_, hh * 64:(hh + 1) * 64], qx[0][0:64, csl],
                                               start=False, stop=False), reads=[ss[0], qx[0]], writes=[po])
                fw.op("pe", lambda e: e.matmul(po[0:64, osl], ss[1][0:64, c_, hh * 64:(hh + 1) * 64], qx[1][0:64, csl],
                                               start=False, stop=True), reads=[ss[1], qx[1]], writes=[po])
            fw.op("act", lambda e: e.activation(sq[0:64, 0, 0:W], po[0:64, 0:W], AF.Square), reads=[po], writes=[sq])
            rms_rstd(K, sq, 1, W, 64, rstd, P=64)
            fw.op("dve", lambda e: e.tensor_tensor(tmo[0:64, 0:W], po[0:64, 0:W], rstd[0:64, 0:W], ALU.mult), reads=[po, rstd], writes=[tmo])
            o = oo[gi % 2]
            fw.op("dve", lambda e: e.tensor_tensor(o[0:64, 0:W], tmo[0:64, 0:W], sgl[0:64, 0:W], ALU.mult), reads=[tmo, sgl], writes=[o])
            fw.dma("sp", K.AT[512 + hh * 64:512 + (hh + 1) * 64, g0 * 128:g0 * 128 + W], o[0:64, 0:W], reads=[o])
    fw.barrier()
    sb.release(m0)


_WEIGHT_NAMES = ("ffn_norm", "ffn_w_up", "ffn_conv_w", "ffn_conv_b", "ffn_w_down", "w_out_even", "w_out_odd",
                 "mix_norm_odd", "w_in_odd", "gla_w_gate_fwd", "gla_b_gate_fwd", "gla_w_gate_bwd", "gla_b_gate_bwd",
                 "gla_out_norm", "mix_norm_even", "w_in_even", "mla_q_norm", "mla_kv_norm", "mla_w_uq", "mla_w_ukv",
                 "mla_q_head_norm", "mla_k_head_norm", "ret_theta_fwd", "ret_theta_bwd", "ret_out_norm")


def kernel(**inputs):
    x = np.asarray(inputs["x"], dtype=np.float32)
    pos = np.asarray(inputs["positions"], dtype=np.int32)
    B, S, _ = x.shape
    T = S // 2
    ncores = 2 * B
    groups = [[2 * b, 2 * b + 1] for b in range(B)]
    nc, K = build_program(T, groups, n_layers=4)
    wts = {nm: np.ascontiguousarray(np.asarray(inputs[nm], dtype=np.float32)) for nm in _WEIGHT_NAMES}
    in_maps = []
    for core in range(ncores):
        b, r = core // 2, core % 2
        m = {"xT": np.ascontiguousarray(x[b, r * T:(r + 1) * T, :].T),
             "pos_loc": np.ascontiguousarray(pos[b, r * T:(r + 1) * T][None, :]),
             "pos_all": np.ascontiguousarray(pos[b][None, :]),
             "flags": np.tile(np.array([[r, 1 - r, 0, 0]], np.float32), (128, 1)),
             "consts": CONST_ARR}
        m.update(wts)
        in_maps.append(m)
    res = run_bass_kernel_spmd(nc, in_maps, core_ids=list(range(ncores)))
    out = np.empty((B, S, D), np.float32)
    for core in range(ncores):
        b, r = core // 2, core % 2
        out[b, r * T:(r + 1) * T, :] = np.asarray(res.results[core]["yT"]).T
    return out
```

```python
import sys
import numpy as np
from contextlib import ExitStack
import concourse.bass as bass
import concourse.mybir as mybir
from concourse.bass_utils import run_bass_kernel_spmd

F32 = mybir.dt.float32
BF16 = mybir.dt.bfloat16
I32 = mybir.dt.int32
AF = mybir.ActivationFunctionType
ALU = mybir.AluOpType

D = 1024
DFF = 2816
NGC = DFF // 128
EPS = 1e-6


class Buf:
    __slots__ = ("name", "w", "r")

    def __init__(self, name=""):
        self.name = name
        self.w = []
        self.r = []


class Tile:
    def __init__(self, ap, name=""):
        self.ap = ap
        self.b = Buf(name)

    def __getitem__(self, k):
        return self.ap[k]


class _Rec:
    def __init__(self):
        self.call = None

    def __getattr__(self, name):
        def f(*a, **kw):
            self.call = (name, a, kw)
            return self
        return f


class FW:
    ENGS = ("pe", "act", "dve", "pool", "sp")
    NDMA = 36
    NSW = 8
    NEPOCH = 1

    def __init__(self, nc, stack):
        self.nc = nc
        self.streams = {e: [] for e in self.ENGS}
        self.sem = {}
        for ep in range(self.NEPOCH):
            for e in self.ENGS:
                self.sem[(e, ep)] = stack.enter_context(nc.semaphore(f"s_{e}_{ep}"))
        self.dsem = [stack.enter_context(nc.semaphore(f"s_dma_{i}")) for i in range(self.NDMA)]
        self.dval = [0] * self.NDMA
        self.ccsem = stack.enter_context(nc.semaphore("s_cc"))
        self.ccval = 0
        self.dnext = 0
        self.dnext_sw = 0
        self.epoch = 0
        self.cnt = {e: 0 for e in self.ENGS}
        self.seen = {}
        self.n_ops = {e: 0 for e in self.ENGS}

    def _wait(self, eng, ev):
        semk, val = ev
        if semk[0] == "E":
            if semk[2] != self.epoch:
                return
            if semk[1] == eng and eng == "pe":
                return
        key = (eng, semk)
        if self.seen.get(key, 0) >= val:
            return
        self.seen[key] = val
        if semk[0] == "E":
            s = self.sem[(semk[1], semk[2])]
        elif semk[0] == "C":
            s = self.ccsem
        else:
            s = self.dsem[semk[1]]
        self.streams[eng].append(lambda e, s=s, val=val: e.wait_ge(s, val))

    @staticmethod
    def _bl(ts):
        return [t.b if isinstance(t, Tile) else t for t in ts]

    def _deps(self, reads, writes):
        deps = []
        for b in reads:
            deps.extend(b.w)
        for b in writes:
            deps.extend(b.w)
            deps.extend(b.r)
        return deps

    def _commit(self, ev, reads, writes):
        for b in reads:
            b.r.append(ev)
            if len(b.r) > 48:
                b.r = b.r[-48:]
        for b in writes:
            b.w = [ev]
            b.r = []

    def op(self, eng, fn, reads=(), writes=()):
        reads = self._bl(reads)
        writes = self._bl(writes)
        for ev in self._deps(reads, writes):
            self._wait(eng, ev)
        self.cnt[eng] += 1
        ev = (("E", eng, self.epoch), self.cnt[eng])
        s = self.sem[(eng, self.epoch)]
        ln = sys._getframe(1).f_lineno
        rec = _Rec()
        fn(rec)
        call = rec.call
        self.streams[eng].append(lambda e, call=call, s=s, ln=ln: getattr(e, call[0])(*call[1], **call[2]).then_inc(s, 1).annotate(f"L{ln}"))
        self.n_ops[eng] += 1
        self._commit(ev, reads, writes)
        return ev

    def dma(self, q, out, in_, reads=(), writes=(), tag="", **kw):
        reads = self._bl(reads)
        writes = self._bl(writes)
        if q == "pool":
            i = self.NDMA - self.NSW + self.dnext_sw
            self.dnext_sw = (self.dnext_sw + 1) % self.NSW
        else:
            i = self.dnext
            self.dnext = (self.dnext + 1) % (self.NDMA - self.NSW)
        if self.dval[i] > 0:
            self._wait(q, (("D", i), self.dval[i]))
        for ev in self._deps(reads, writes):
            self._wait(q, ev)
        self.dval[i] += 16
        ev = (("D", i), self.dval[i])
        s = self.dsem[i]
        self.streams[q].append(
            lambda e, s=s, out=out, in_=in_, kw=kw, ln=sys._getframe(1).f_lineno, tag=tag: e.dma_start(
                out=out, in_=in_, **kw).then_inc(s, 16).annotate(f"L{ln}{tag}"))
        self.n_ops[q] += 1
        self._commit(ev, reads, writes)
        return ev

    def collective(self, kind, op, groups, in_ap, out_ap, reads=(), writes=()):
        reads = self._bl(reads)
        writes = self._bl(writes)
        eng = "pool"
        for ev in self._deps(reads, writes):
            self._wait(eng, ev)
        self.ccval += 1
        ev = (("C",), self.ccval)
        s = self.ccsem
        self.streams[eng].append(
            lambda e, s=s: e.collective_compute(kind, op, replica_groups=groups, ins=[in_ap],
                                                outs=[out_ap]).then_inc(s, 1))
        self._commit(ev, reads, writes)
        return ev

    def barrier(self):
        for i in range(self.NDMA):
            if self.dval[i] > 0:
                self._wait("pool", (("D", i), self.dval[i]))
        for e in self.ENGS:
            if e != "pool" and self.cnt[e] > 0:
                self._wait("pool", (("E", e, self.epoch), self.cnt[e]))
        if self.ccval > 0:
            self._wait("pool", (("C",), self.ccval))
        self.cnt["pool"] += 1
        s = self.sem[("pool", self.epoch)]
        self.streams["pool"].append(lambda e, s=s: e.nop().then_inc(s, 1))
        for e in self.ENGS:
            if e != "pool":
                self._wait(e, (("E", "pool", self.epoch), self.cnt["pool"]))
        if self.epoch + 1 < self.NEPOCH:
            self.epoch += 1
            self.cnt = {e: 0 for e in self.ENGS}

    def finish(self, block):
        st = self.streams

        @block.tensor
        def _(e):
            for f in st["pe"]:
                f(e)

        @block.scalar
        def _(e):
            for f in st["act"]:
                f(e)

        @block.vector
        def _(e):
            for f in st["dve"]:
                f(e)

        @block.gpsimd
        def _(e):
            for f in st["pool"]:
                f(e)

        @block.sync
        def _(e):
            for f in st["sp"]:
                f(e)


class SBAlloc:
    WORDS = 52992

    def __init__(self, nc, stack):
        self.t = stack.enter_context(nc.sbuf_tensor("sbig", [128, self.WORDS], F32))
        self.off = 0
        self.n = 0

    def mark(self):
        return self.off

    def release(self, m):
        self.off = m

    def alloc(self, free_shape, dtype=F32, parts=128, name=None):
        n = int(np.prod(free_shape))
        esz = 4 if dtype in (F32, I32) else 2
        words = (n * esz + 3) // 4
        words = (words + 7) // 8 * 8
        assert self.off + words <= self.WORDS, f"SBUF overflow: {self.off}+{words}"
        ap = self.t[0:parts, self.off:self.off + words]
        self.off += words
        if dtype != F32:
            ap = ap.bitcast(dtype)
        ap = ap[:, 0:n]
        if len(free_shape) == 2:
            ap = ap.rearrange("p (a b) -> p a b", a=free_shape[0])
        elif len(free_shape) == 3:
            ap = ap.rearrange("p (a b c) -> p a b c", a=free_shape[0], b=free_shape[1])
        self.n += 1
        return Tile(ap, name or f"sb{self.n}")


def _make_consts():
    j = np.arange(128)[:, None]
    i = np.arange(128)[None, :]
    same64 = (j // 64) == (i // 64)
    c = {}
    g = -1.0 / 16.0
    c["m_le"] = np.where(same64 & (j <= i), g, 0.0)
    c["m_ge"] = np.where(same64 & (j >= i), g, 0.0)
    c["m_gt"] = np.where(same64 & (j > i), g, 0.0)
    c["m_lt"] = np.where(same64 & (j < i), g, 0.0)
    c["mf64"] = np.where(same64 & (j <= i), 1.0, 0.0)
    c["mb64"] = np.where(same64 & (j > i), 1.0, 0.0)
    c["relf"] = np.maximum(i - j, 0).astype(np.float64)
    c["relb"] = np.maximum(j - i, 0).astype(np.float64)
    c["mf128"] = (j <= i).astype(np.float64)
    c["mb128"] = (j > i).astype(np.float64)
    c["ident"] = (j == i).astype(np.float64)
    c["iota1"] = (i + 1.0) + 0 * j
    c["iotar"] = (128.0 - i) + 0 * j
    c["c128"] = np.full((128, 128), 128.0)
    c["ones96"] = ((j < 96) + 0 * i).astype(np.float64)
    c["ones64b"] = ((j // 64) == (i // 64)).astype(np.float64)
    sel = np.zeros((128, 128))
    for m in range(96):
        sel[m, m] = 1.0
    for m in range(32):
        sel[96 + m, 64 + m] = 1.0
    c["sel"] = sel
    sel65 = np.zeros((128, 128))
    sel65[64, :64] = 1.0
    c["sel65"] = sel65
    rm = np.zeros((128, 128))
    for m in range(32):
        rm[m, 64 + m] = 1.0
    for m in range(16):
        rm[m + 16, 96 + m] = -1.0
        rm[m, 96 + 16 + m] = 1.0
    c["ropemat"] = rm
    tabc = np.zeros((128, 128))
    p = np.arange(128)
    inv32 = 10000.0 ** (-(np.arange(32, dtype=np.float32) / np.float32(32))).astype(np.float32)
    inv16 = 10000.0 ** (-(np.arange(16, dtype=np.float32) / np.float32(16))).astype(np.float32)
    tabc[:, 0] = inv32[p % 32] / (2 * np.pi)
    tabc[64:, 1] = inv16[p[64:] % 16] / (2 * np.pi)
    tabc[:, 2] = 0.25
    tabc[:, 3] = 0.0
    tabc[:96, 4] = 0.25
    tabc[:, 5] = p
    tabc[:, 6] = 127.0 - p
    c["tabc"] = tabc
    names = list(c)
    arr = np.concatenate([c[n].astype(np.float32) for n in names], axis=1)
    return names, np.ascontiguousarray(arr)


CONST_NAMES, CONST_ARR = _make_consts()


class Ctx:
    pass


def build_program(T, groups, n_layers=4, dbg=None):
    dbg = dbg or {}
    nc = bass.Bass("TRN2", target_bir_lowering=False)
    S2 = 2 * T
    K = Ctx()
    K.nc, K.T, K.S2, K.dbg = nc, T, S2, dbg

    def din(name, shape, dt=F32):
        return nc.dram_tensor(name, list(shape), dt, kind="ExternalInput").ap()

    def dscr(name, shape, dt=F32):
        kind = "ExternalOutput" if name in dbg.get("dump", ()) else "Internal"
        return Tile(nc.dram_tensor(name, list(shape), dt, kind=kind).ap(), name)

    K.xT_in = din("xT", [D, T])
    K.flags = din("flags", [128, 4])
    K.w = {}
    for nm, shp in (("ffn_norm", [4, D]), ("ffn_w_up", [4, D, 2 * DFF]), ("ffn_conv_w", [4, 3, DFF]),
                    ("ffn_conv_b", [4, DFF]), ("ffn_w_down", [4, DFF, D]),
                    ("w_out_even", [2, D, D]), ("w_out_odd", [2, D, D])):
        K.w[nm] = din(nm, shp)
    for nm, shp in (("mix_norm_odd", [2, D]), ("w_in_odd", [2, D, 3104]), ("gla_w_gate_fwd", [2, 16, 512]),
                    ("gla_b_gate_fwd", [2, 512]), ("gla_w_gate_bwd", [2, 16, 512]), ("gla_b_gate_bwd", [2, 512]),
                    ("gla_out_norm", [2, 4, 256])):
        K.w[nm] = din(nm, shp)
    for nm, shp in (("mix_norm_even", [2, D]), ("w_in_even", [2, D, 2720]), ("mla_q_norm", [2, 384]),
                    ("mla_kv_norm", [2, 256]), ("mla_w_uq", [2, 384, 768]), ("mla_w_ukv", [2, 256, 1024]),
                    ("mla_q_head_norm", [2, 96]), ("mla_k_head_norm", [2, 96]), ("ret_theta_fwd", [2, 8]),
                    ("ret_theta_bwd", [2, 8]), ("ret_out_norm", [2, 8, 64])):
        K.w[nm] = din(nm, shp)
    K.pos_loc = din("pos_loc", [1, T], I32)
    K.pos_all = din("pos_all", [1, S2], I32)
    K.consts_in = din("consts", list(CONST_ARR.shape))
    if "dbg_at" in dbg:
        K.dbg_at = din("dbg_at", [D, T])
    K.yT = nc.dram_tensor("yT", [D, T], F32, kind="ExternalOutput").ap()

    K.xT = dscr("xT_s", [D, T])
    K.AT = dscr("AT", [D, T], BF16)
    K.H2T = dscr("H2T", [D, T + 2], BF16)
    K.HBX = dscr("HBX", [128, 16])
    K.HBG = dscr("HBG", [256, 16])
    NC64 = T // 64
    for nm in ("QCF", "KCF", "QBF", "QCB", "KCB", "QBB"):
        setattr(K, nm, dscr(nm, [512, T], BF16))
    K.KDF = dscr("KDF", [T, 512], BF16)
    K.KDB = dscr("KDB", [T, 512], BF16)
    K.VT = dscr("VT", [T, 1024], BF16)
    K.SGT = dscr("SGT", [D, T], BF16)
    K.RPB = dscr("RPB", [NC64, 128, 1024], BF16)
    K.GST = dscr("GST", [128, 2048])
    K.GSG = dscr("GSG", [256, 2048])
    K.TRC = dscr("TRC", [128, T])
    K.TRS = dscr("TRS", [128, T])
    K.TQ = dscr("TQ", [128, T])
    K.TK = dscr("TK", [128, S2])
    K.CQN = dscr("CQN", [384, T], BF16)
    TL = min(512, T)
    K.LAT = [dscr(f"LAT{j}", [288, TL]) for j in range(T // TL)]
    K.LATG = [dscr(f"LATG{j}", [576, TL]) for j in range(T // TL)]
    K.RQ = dscr("RQ", [512, T], BF16)
    K.RK = dscr("RK", [512, T], BF16)
    K.RV = dscr("RV", [T, 512], BF16)
    K.QT = dscr("QT", [8, 96, T], BF16)
    K.KT = dscr("KT", [8, 96, S2], BF16)
    K.VX = dscr("VX", [S2, 520], BF16)
    K.RST = dscr("RST", [64, 1024])
    K.RSG = dscr("RSG", [128, 1024])

    with ExitStack() as st:
        fw = FW(nc, st)
        sb = SBAlloc(nc, st)
        K.fw, K.sb, K.groups = fw, sb, groups
        psum = st.enter_context(nc.psum_tensor("psum", [128, 4096], F32))
        K.ps = [Tile(psum[:, i * 512:(i + 1) * 512], f"ps{i}") for i in range(8)]
        K.psum_full = psum
        K.ps_rr = 0

        K.ones_bf = sb.alloc([128], BF16, name="ones_bf")
        fw.op("dve", lambda e: e.memset(K.ones_bf[:], 1.0), writes=[K.ones_bf])
        K.flg = sb.alloc([4], F32, name="flg")
        fw.dma("sp", K.flg[:], K.flags[:, :], writes=[K.flg])
        K.cst = {}
        for ci, nm in enumerate(CONST_NAMES):
            t = sb.alloc([128], F32, name="c_" + nm)
            fw.dma("sp", t[:], K.consts_in[:, ci * 128:(ci + 1) * 128], writes=[t])
            K.cst[nm] = t
        K.ones_f = sb.alloc([128], F32, name="ones_f")
        fw.op("dve", lambda e: e.memset(K.ones_f[:], 1.0), writes=[K.ones_f])

        if "only_odd" not in dbg:
            make_tables(K)
        m0 = sb.mark()
        for c in range(8):
            t = sb.alloc([T], F32)
            fw.dma("sp", t[:], K.xT_in[c * 128:(c + 1) * 128, :], writes=[t])
            fw.dma("sp", K.xT[c * 128:(c + 1) * 128, :], t[:], reads=[t])
            if c % 2 == 1:
                pass
        fw.barrier()
        sb.release(m0)

        for L in range(n_layers):
            i = L // 2
            if "dbg_at" in dbg:
                m0 = sb.mark()
                for c in range(8):
                    t = sb.alloc([T], BF16)
                    fw.dma("pool", t[:], K.dbg_at[c * 128:(c + 1) * 128, :], writes=[t])
                    fw.dma("sp", K.AT[c * 128:(c + 1) * 128, :], t[:], reads=[t])
                fw.barrier()
                sb.release(m0)
            elif L % 2 == 1 or "only_odd" in dbg:
                gla_layer(K, i)
            else:
                even_layer(K, i)
            wout = K.w["w_out_even"][i] if (L % 2 == 0 and "only_odd" not in dbg) else K.w["w_out_odd"][i]
            phase_out(K, L, wout)
            phase_ffn(K, L, last=(L == n_layers - 1))

        fw.barrier()
        with nc.Block() as block:
            fw.finish(block)
    K.n_ops = fw.n_ops
    return nc, K


def next_ps(K):
    p = K.ps[K.ps_rr % getattr(K, "ps_mod", 6)]
    K.ps_rr += 1
    return p


def load_w_bf16(K, dst, src, kc_n, ncols, c0=0):
    v = src.rearrange("(c p) n -> p c n", p=128)
    step = max(1, 4096 // ncols)
    for k0 in range(0, kc_n, step):
        k1 = min(kc_n, k0 + step)
        K.fw.dma("pool", dst[:, k0:k1, c0:c0 + ncols], v[:, k0:k1, :], writes=[dst])


def load_col(K, dst, src_vec, nchunk):
    K.fw.dma("sp", dst[:, 0:nchunk], src_vec.rearrange("(c p) -> p c", p=128), writes=[dst],
             allow_slow_non_contiguous=True)


def rms_rstd(K, sq, nchunk, width, dim, out_rstd, ones=None, P=128):
    fw = K.fw
    ones = ones or K.ones_bf
    ps = next_ps(K)
    for c in range(nchunk):
        fw.op("pe", lambda e, c=c: e.matmul(ps[0:P, 0:width], ones[0:P, 0:P], sq[0:P, c, 0:width],
                                            start=(c == 0), stop=(c == nchunk - 1)),
              reads=[ones, sq], writes=[ps])
    fw.op("act", lambda e: e.activation(out_rstd[0:P, 0:width], ps[0:P, 0:width], AF.Ln, bias=EPS,
                                        scale=1.0 / dim), reads=[ps], writes=[out_rstd])
    fw.op("act", lambda e: e.activation(out_rstd[0:P, 0:width], out_rstd[0:P, 0:width], AF.Exp, scale=-0.5),
          reads=[out_rstd], writes=[out_rstd])


def phase_out(K, L, wout):
    fw, sb, T = K.fw, K.sb, K.T
    K.ffn_mark = sb.mark()
    K.wup = sb.alloc([8, 2 * DFF], BF16, name="wup")
    K.wdn = sb.alloc([NGC, D], BF16, name="wdn")
    m0 = sb.mark()
    w = sb.alloc([8, D], BF16, name="wout")
    load_w_bf16(K, w, wout, 8, D)
    load_w_bf16(K, K.wup, K.w["ffn_w_up"][L], 8, 2 * DFF)
    load_w_bf16(K, K.wdn, K.w["ffn_w_down"][L], NGC, D)
    TT = 256 if T >= 256 else T
    g2 = sb.alloc([8], F32, name="g2")
    load_col(K, g2, K.w["ffn_norm"][L], 8)
    hb = sb.alloc([8, 2], F32, name="hb")
    ATv = K.AT.ap.rearrange("(c p) t -> p c t", p=128)
    XTv = K.xT.ap.rearrange("(c p) t -> p c t", p=128)
    H2v = K.H2T.ap.rearrange("(c p) t -> p c t", p=128)
    nt = T // TT
    at_t = [sb.alloc([8, TT], BF16, name=f"at{j}") for j in range(2)]
    x_t = [sb.alloc([8, TT], F32, name=f"x{j}") for j in range(2)]
    sq_t = sb.alloc([8, TT], BF16, name="sq")
    h2_t = [sb.alloc([8, TT], BF16, name=f"h2{j}") for j in range(2)]
    rstd = sb.alloc([TT], F32, name="rstd")
    def load_out(ti):
        fw.dma("sp", at_t[ti % 2][:, :, :], ATv[:, :, ti * TT:(ti + 1) * TT], writes=[at_t[ti % 2]])
        fw.dma("sp", x_t[ti % 2][:, :, :], XTv[:, :, ti * TT:(ti + 1) * TT], writes=[x_t[ti % 2]])

    load_out(0)
    for ti in range(nt):
        t0 = ti * TT
        at, xt, h2 = at_t[ti % 2], x_t[ti % 2], h2_t[ti % 2]
        if ti + 1 < nt:
            load_out(ti + 1)
        for oc in range(8):
            ps = next_ps(K)
            for kc in range(8):
                fw.op("pe", lambda e, kc=kc, oc=oc, ps=ps, at=at: e.matmul(
                    ps[:, 0:TT], w[:, kc, oc * 128:(oc + 1) * 128], at[:, kc, :],
                    start=(kc == 0), stop=(kc == 7)), reads=[w, at], writes=[ps])
            fw.op("dve", lambda e, oc=oc, ps=ps, xt=xt: e.tensor_tensor(
                xt[:, oc, :], xt[:, oc, :], ps[:, 0:TT], ALU.add), reads=[ps, xt], writes=[xt])
        fw.dma("sp", XTv[:, :, t0:t0 + TT], xt[:, :, :], reads=[xt])
        for oc in range(8):
            fw.op("act", lambda e, oc=oc, xt=xt: e.activation(sq_t[:, oc, :], xt[:, oc, :], AF.Square),
                  reads=[xt], writes=[sq_t])
        rms_rstd(K, sq_t, 8, TT, D, rstd)
        for oc in range(8):
            fw.op("dve", lambda e, oc=oc, xt=xt, h2=h2: e.scalar_tensor_tensor(
                h2[:, oc, :], xt[:, oc, :], g2[:, oc:oc + 1], rstd[:, :], ALU.mult, ALU.mult),
                reads=[xt, g2, rstd], writes=[h2])
        fw.dma("sp", H2v[:, :, 1 + t0:1 + t0 + TT], h2[:, :, :], reads=[h2])
        if ti == 0:
            fw.op("dve", lambda e, h2=h2: e.tensor_copy(hb[:, :, 0:1], h2[:, :, 0:1]), reads=[h2], writes=[hb])
        if ti == nt - 1:
            fw.op("dve", lambda e, h2=h2: e.tensor_copy(hb[:, :, 1:2], h2[:, :, TT - 1:TT]), reads=[h2], writes=[hb])
    fw.dma("sp", K.HBX[:, :], hb[:, :, :].rearrange("p a b -> p (a b)"), reads=[hb], writes=[K.HBX])
    fw.collective("AllGather", ALU.bypass, K.groups, K.HBX.ap, K.HBG.ap, reads=[K.HBX], writes=[K.HBG])
    g0 = sb.alloc([8, 2], F32, name="g0")
    g1 = sb.alloc([8, 2], F32, name="g1")
    fw.dma("sp", g0[:, :, :].rearrange("p a b -> p (a b)"), K.HBG[0:128, :], reads=[K.HBG], writes=[g0])
    fw.dma("sp", g1[:, :, :].rearrange("p a b -> p (a b)"), K.HBG[128:256, :], reads=[K.HBG], writes=[g1])
    hl = sb.alloc([8, 1], BF16, name="hl")
    hr = sb.alloc([8, 1], BF16, name="hr")
    fw.op("dve", lambda e: e.tensor_scalar(hl[:, :, :], g0[:, :, 1:2], K.flg[:, 0:1], None, ALU.mult),
          reads=[g0, K.flg], writes=[hl])
    fw.op("dve", lambda e: e.tensor_scalar(hr[:, :, :], g1[:, :, 0:1], K.flg[:, 1:2], None, ALU.mult),
          reads=[g1, K.flg], writes=[hr])
    fw.dma("sp", H2v[:, :, 0:1], hl[:, :, :], reads=[hl], allow_slow_non_contiguous=True)
    fw.dma("sp", H2v[:, :, T + 1:T + 2], hr[:, :, :], reads=[hr], allow_slow_non_contiguous=True)
    fw.barrier()
    sb.release(m0)


def phase_ffn(K, L, last):
    fw, sb, T = K.fw, K.sb, K.T
    TF = 510 if T >= 510 else T
    wup, wdn = K.wup, K.wdn
    cw = sb.alloc([3, NGC], F32, name="cw")
    for k in range(3):
        fw.dma("sp", cw[:, k, :], K.w["ffn_conv_w"][L, k].rearrange("(c p) -> p c", p=128), writes=[cw],
               allow_slow_non_contiguous=True)
    cb = sb.alloc([NGC], F32, name="cb")
    load_col(K, cb, K.w["ffn_conv_b"][L], NGC)
    XTv = K.xT.ap.rearrange("(c p) t -> p c t", p=128)
    YTv = K.yT.rearrange("(c p) t -> p c t", p=128)
    H2v = K.H2T.ap.rearrange("(c p) t -> p c t", p=128)
    h_t = [sb.alloc([8, TF + 2], BF16, name=f"h{j}") for j in range(2)]
    xt = sb.alloc([8, TF], F32, name="xf")
    act = sb.alloc([NGC, TF], BF16, name="act")
    cv = [sb.alloc([TF], F32, name=f"cv{j}") for j in range(2)]
    starts = list(range(0, T, TF))

    def load_ffn(ti):
        t0 = starts[ti]
        W = min(TF, T - t0)
        fw.dma("sp", h_t[ti % 2][:, :, 0:W + 2], H2v[:, :, t0:t0 + W + 2], writes=[h_t[ti % 2]])

    load_ffn(0)
    for ti, t0 in enumerate(starts):
        W = min(TF, T - t0)
        h = h_t[ti % 2]
        if ti + 1 < len(starts):
            load_ffn(ti + 1)
        fw.dma("sp", xt[:, :, 0:W], XTv[:, :, t0:t0 + W], writes=[xt])
        for gc in range(NGC):
            psg = next_ps(K)
            psv = next_ps(K)
            c = cv[gc % 2]
            for kc in range(8):
                fw.op("pe", lambda e: e.matmul(psg[:, 0:W + 2], wup[:, kc, gc * 128:(gc + 1) * 128], h[:, kc, 0:W + 2],
                                               start=(kc == 0), stop=(kc == 7)), reads=[wup, h], writes=[psg])
            for kc in range(8):
                fw.op("pe", lambda e: e.matmul(psv[:, 0:W], wup[:, kc, DFF + gc * 128:DFF + (gc + 1) * 128], h[:, kc, 1:W + 1],
                                               start=(kc == 0), stop=(kc == 7)), reads=[wup, h], writes=[psv])
            fw.op("act", lambda e: e.activation(c[:, 0:W], psg[:, 1:W + 1], AF.Identity, bias=cb[:, gc:gc + 1],
                                                scale=cw[:, 1, gc:gc + 1]), reads=[psg, cb, cw], writes=[c])
            fw.op("dve", lambda e: e.scalar_tensor_tensor(c[:, 0:W], psg[:, 0:W], cw[:, 0, gc:gc + 1], c[:, 0:W], ALU.mult, ALU.add),
                  reads=[psg, cw, c], writes=[c])
            fw.op("dve", lambda e: e.scalar_tensor_tensor(c[:, 0:W], psg[:, 2:W + 2], cw[:, 2, gc:gc + 1], c[:, 0:W], ALU.mult, ALU.add),
                  reads=[psg, cw, c], writes=[c])
            fw.op("act", lambda e: e.activation(c[:, 0:W], c[:, 0:W], AF.Silu), reads=[c], writes=[c])
            fw.op("dve", lambda e: e.tensor_tensor(act[:, gc, 0:W], c[:, 0:W], psv[:, 0:W], ALU.mult), reads=[c, psv], writes=[act])
        for oc in range(8):
            ps = next_ps(K)
            for gc in range(NGC):
                fw.op("pe", lambda e: e.matmul(ps[:, 0:W], wdn[:, gc, oc * 128:(oc + 1) * 128], act[:, gc, 0:W],
                                               start=(gc == 0), stop=(gc == NGC - 1)), reads=[wdn, act], writes=[ps])
            fw.op("dve", lambda e: e.tensor_tensor(xt[:, oc, 0:W], xt[:, oc, 0:W], ps[:, 0:W], ALU.add), reads=[ps, xt], writes=[xt])
        if last:
            fw.dma("sp", YTv[:, :, t0:t0 + W], xt[:, :, 0:W], reads=[xt])
        else:
            fw.dma("sp", XTv[:, :, t0:t0 + W], xt[:, :, 0:W], reads=[xt])
    fw.barrier()
    sb.release(K.ffn_mark)


def norm_tile(K, xt, h, sq, rstd, gcol, TT):
    fw = K.fw
    for oc in range(8):
        fw.op("act", lambda e, oc=oc: e.activation(sq[:, oc, 0:TT], xt[:, oc, 0:TT], AF.Square),
              reads=[xt], writes=[sq])
    rms_rstd(K, sq, 8, TT, D, rstd)
    for oc in range(8):
        fw.op("dve", lambda e, oc=oc: e.scalar_tensor_tensor(
            h[:, oc, 0:TT], xt[:, oc, 0:TT], gcol[:, oc:oc + 1], rstd[:, 0:TT], ALU.mult, ALU.mult),
            reads=[xt, gcol, rstd], writes=[h])


def proj_fm(K, w, c0, ncol, h, TT, t0=0):
    ps = next_ps(K)
    for kc in range(8):
        K.fw.op("pe", lambda e, kc=kc: e.matmul(ps[0:ncol, 0:TT], w[:, kc, c0:c0 + ncol], h[:, kc, t0:t0 + TT],
                                                start=(kc == 0), stop=(kc == 7)), reads=[w, h], writes=[ps])
    return ps


def proj_tm(K, w, c0, ncol, h, t0, ps=None):
    ps = ps or next_ps(K)
    for kc in range(8):
        K.fw.op("pe", lambda e, kc=kc: e.matmul(ps[:, 0:ncol], h[:, kc, t0:t0 + 128], w[:, kc, c0:c0 + ncol],
                                                start=(kc == 0), stop=(kc == 7)), reads=[w, h], writes=[ps])
    return ps


def gla_layer(K, i):
    sb, T = K.sb, K.T
    mL = sb.mark()
    NC = T // 64
    decf = sb.alloc([4, NC], F32, name="decf")
    decb = sb.alloc([4, NC], F32, name="decb")
    gla_o1(K, i, decf, decb)
    gla_o23(K, i, decf, decb)
    sb.release(mL)


def gla_o1(K, i, decf, decb):
    fw, sb, T = K.fw, K.sb, K.T
    m0 = sb.mark()
    TT = min(512, T)
    NS = TT // 128
    w = sb.alloc([8, 3104], BF16, name="w_in_odd")
    load_w_bf16(K, w, K.w["w_in_odd"][i], 8, 3104)
    g1 = sb.alloc([8], F32, name="g1")
    load_col(K, g1, K.w["mix_norm_odd"][i], 8)
    gn = sb.alloc([8], F32, name="gn")
    load_col(K, gn, K.w["gla_out_norm"][i].rearrange("h e -> (h e)"), 8)
    wg = {}
    gae = {}
    for d, wn, bn in (("f", "gla_w_gate_fwd", "gla_b_gate_fwd"), ("b", "gla_w_gate_bwd", "gla_b_gate_bwd")):
        t = sb.alloc([512], F32, parts=32, name="wg" + d)
        fw.dma("sp", t[0:16, :], K.w[wn][i], writes=[t])
        fw.dma("sp", t[16:17, :], K.w[bn][i:i + 1, :], writes=[t])
        wg[d] = t
        ga = sb.alloc([TT], F32, parts=32, name="gae" + d)
        fw.op("dve", lambda e, ga=ga: e.memset(ga[:, :], 1.0), writes=[ga])
        gae[d] = ga
    XTv = K.xT.ap.rearrange("(c p) t -> p c t", p=128)
    x_t = [sb.alloc([8, TT], F32, name=f"gx{j}") for j in range(2)]
    h = sb.alloc([8, TT], BF16, name="gh")
    sq = sb.alloc([8, TT], BF16, name="gsq")
    rstd = sb.alloc([TT], F32, name="grstd")
    q = sb.alloc([4, TT], F32, name="gq")
    k = sb.alloc([4, TT], F32, name="gk")
    sg = sb.alloc([8, TT], BF16, name="gsg")
    tmp = [sb.alloc([512], F32, name=f"gtmp{j}") for j in range(2)]
    outs = {nm: sb.alloc([4, TT], BF16, name="o" + nm) for nm in ("QCF", "KCF", "QBF", "QCB", "KCB", "QBB")}
    lt = {d: sb.alloc([512], F32, name="l" + d) for d in "fb"}
    vt = sb.alloc([1024], BF16, name="gvt")
    kd = [sb.alloc([512], BF16, name=f"gkd{j}") for j in range(2)]
    Et = [sb.alloc([4, 128], F32, name=f"gE{j}") for j in range(2)]
    rEt = [sb.alloc([4, 128], F32, name=f"grE{j}") for j in range(2)]
    tmpx = [sb.alloc([4, 128], F32, name=f"gtx{j}") for j in range(2)]
    ecnt = 0
    fw.dma("sp", x_t[0][:, :, :], XTv[:, :, 0:TT], writes=[x_t[0]])
    for ti in range(T // TT):
        t0 = ti * TT
        xt = x_t[ti % 2]
        if ti + 1 < T // TT:
            fw.dma("sp", x_t[(ti + 1) % 2][:, :, :], XTv[:, :, t0 + TT:t0 + 2 * TT], writes=[x_t[(ti + 1) % 2]])
        norm_tile(K, xt, h, sq, rstd, g1, TT)
        for c in range(4):
            ps = proj_fm(K, w, c * 128, 128, h, TT)
            fw.op("act", lambda e, c=c, ps=ps: e.mul(q[:, c, :], ps[:, 0:TT], 128 ** -0.5), reads=[ps], writes=[q])
        for c in range(4):
            ps = proj_fm(K, w, 512 + c * 128, 128, h, TT)
            fw.op("dve", lambda e, c=c, ps=ps: e.tensor_copy(k[:, c, :], ps[:, 0:TT]), reads=[ps], writes=[k])
        for c in range(8):
            ps = proj_fm(K, w, 2048 + c * 128, 128, h, TT)
            tm = tmp[c % 2]
            fw.op("act", lambda e, ps=ps, tm=tm: e.activation(tm[:, 0:TT], ps[:, 0:TT], AF.Silu), reads=[ps], writes=[tm])
            fw.op("dve", lambda e, c=c, tm=tm: e.tensor_scalar(sg[:, c, :], tm[:, 0:TT], gn[:, c:c + 1], None, ALU.mult),
                  reads=[tm, gn], writes=[sg])
        fw.dma("sp", K.SGT.ap.rearrange("(c p) t -> p c t", p=128)[:, :, t0:t0 + TT], sg[:, :, :], reads=[sg])
        for d, c0 in (("f", 3072), ("b", 3088)):
            ps = proj_fm(K, w, c0, 16, h, TT)
            fw.op("dve", lambda e, d=d, ps=ps: e.tensor_copy(gae[d][0:16, :], ps[0:16, 0:TT]), reads=[ps], writes=[gae[d]])
        for s_ in range(NS):
            s0 = s_ * 128
            tok0 = t0 + s0
            for half in range(2):
                ps = proj_tm(K, w, 1024 + half * 512, 512, h, s0)
                if half == 0:
                    fw.op("act", lambda e, ps=ps: e.copy(vt[:, 0:512], ps[:, 0:512]), reads=[ps], writes=[vt])
                else:
                    fw.op("dve", lambda e, ps=ps: e.tensor_copy(vt[:, 512:1024], ps[:, 0:512]), reads=[ps], writes=[vt])
            fw.dma("sp", K.VT[tok0:tok0 + 128, :], vt[:, :], reads=[vt])
            psk = proj_tm(K, w, 512, 512, h, s0, ps=K.ps[6])
            for di, d in enumerate("fb"):
                l = lt[d]
                psx = next_ps(K)
                fw.op("pe", lambda e, d=d, psx=psx: e.matmul(psx[:, 0:512], gae[d][0:17, s0:s0 + 128], wg[d][0:17, :],
                                                             start=True, stop=True), reads=[gae[d], wg[d]], writes=[psx])
                fw.op("act", lambda e, psx=psx, l=l: e.activation(l[:, :], psx[:, 0:512], AF.Exp, scale=-1.0),
                      reads=[psx], writes=[l])
                fw.op("act", lambda e, l=l: e.activation(l[:, :], l[:, :], AF.Ln, bias=1.0), reads=[l], writes=[l])
                mstrict = K.cst["m_gt"] if d == "f" else K.cst["m_lt"]
                pse = next_ps(K)
                fw.op("pe", lambda e, pse=pse, l=l, m=mstrict: e.matmul(pse[:, 0:512], m[:, :], l[:, :], start=True, stop=True),
                      reads=[mstrict, l], writes=[pse])
                tm = tmp[di]
                fw.op("act", lambda e, pse=pse, tm=tm: e.activation(tm[:, :], pse[:, 0:512], AF.Exp), reads=[pse], writes=[tm])
                kdt = kd[di]
                fw.op("dve", lambda e, tm=tm, kdt=kdt: e.tensor_tensor(kdt[:, :], tm[:, :], psk[:, 0:512], ALU.mult),
                      reads=[tm, psk], writes=[kdt])
                fw.dma("sp", (K.KDF if d == "f" else K.KDB)[tok0:tok0 + 128, :], kdt[:, :], reads=[kdt])
                mincl = K.cst["m_le"] if d == "f" else K.cst["m_ge"]
                mid = 32 if d == "f" else 31
                last = 63 if d == "f" else 0
                qc, kc_, qb = (outs["QCF"], outs["KCF"], outs["QBF"]) if d == "f" else (outs["QCB"], outs["KCB"], outs["QBB"])
                dec = decf if d == "f" else decb
                E, rE = Et[ecnt % 2], rEt[ecnt % 2]
                ecnt += 1
                psb = next_ps(K)
                for hh in range(4):
                    fw.op("pe", lambda e: e.matmul(psb[:, hh * 128:(hh + 1) * 128], l[:, hh * 128:(hh + 1) * 128], mincl[:, :],
                                                   start=True, stop=True), reads=[l, mincl], writes=[psb])
                fw.op("act", lambda e: e.activation(E[:, :, :], psb[:, 0:512].rearrange("p (h n) -> p h n", h=4), AF.Exp),
                      reads=[psb], writes=[E])
                fw.op("act", lambda e: e.activation(rE[:, :, :], psb[:, 0:512].rearrange("p (h n) -> p h n", h=4), AF.Exp, scale=-1.0),
                      reads=[psb], writes=[rE])
                E4 = E[:, :, :].rearrange("p h (c n) -> p h c n", n=64)
                rE4 = rE[:, :, :].rearrange("p h (c n) -> p h c n", n=64)
                q4 = q[:, :, s0:s0 + 128].rearrange("p h (c n) -> p h c n", n=64)
                k4 = k[:, :, s0:s0 + 128].rearrange("p h (c n) -> p h c n", n=64)
                tq4 = tmpx[0][:, :, :].rearrange("p h (c n) -> p h c n", n=64)
                tk4 = tmpx[1][:, :, :].rearrange("p h (c n) -> p h c n", n=64)
                fw.op("dve", lambda e: e.tensor_tensor(tq4, E4, rE4[:, :, :, mid:mid + 1].to_broadcast([128, 4, 2, 64]), ALU.mult),
                      reads=[E, rE], writes=[tmpx[0]])
                fw.op("dve", lambda e: e.tensor_tensor(qc[:, :, s0:s0 + 128].rearrange("p h (c n) -> p h c n", n=64), tq4, q4, ALU.mult),
                      reads=[tmpx[0], q], writes=[qc])
                fw.op("dve", lambda e: e.tensor_tensor(tk4, rE4, E4[:, :, :, mid:mid + 1].to_broadcast([128, 4, 2, 64]), ALU.mult),
                      reads=[E, rE], writes=[tmpx[1]])
                fw.op("dve", lambda e: e.tensor_tensor(kc_[:, :, s0:s0 + 128].rearrange("p h (c n) -> p h c n", n=64), tk4, k4, ALU.mult),
                      reads=[tmpx[1], k], writes=[kc_])
                fw.op("dve", lambda e: e.tensor_tensor(qb[:, :, s0:s0 + 128], E[:, :, :], q[:, :, s0:s0 + 128], ALU.mult),
                      reads=[E, q], writes=[qb])
                cidx = tok0 // 64
                fw.op("dve", lambda e: e.tensor_copy(dec[:, :, cidx:cidx + 2], E4[:, :, :, last]), reads=[E], writes=[dec])
        for nm, tl in outs.items():
            fw.dma("sp", getattr(K, nm).ap.rearrange("(h p) t -> p h t", p=128)[:, :, t0:t0 + TT], tl[:, :, :], reads=[tl], tag=nm)
    fw.barrier()
    sb.release(m0)


def gla_o23(K, i, decf, decb):
    fw, sb, T = K.fw, K.sb, K.T
    m0 = sb.mark()
    NG = T // 128
    Sf = sb.alloc([1024], F32, name="Sf")
    Sb = sb.alloc([1024], F32, name="Sb")
    P = sb.alloc([4], F32, name="P")
    fw.op("dve", lambda e: e.memset(Sf[:, :], 0.0), writes=[Sf])
    fw.op("dve", lambda e: e.memset(Sb[:, :], 0.0), writes=[Sb])
    fw.op("dve", lambda e: e.memset(P[:, :], 1.0), writes=[P])
    kdf_t = [sb.alloc([512], BF16, name=f"kdf{j}") for j in range(2)]
    kdb_t = [sb.alloc([512], BF16, name=f"kdb{j}") for j in range(2)]
    v_t = [sb.alloc([1024], BF16, name=f"v{j}") for j in range(2)]

    def states(kdt, vtl, ch):
        p0 = ch * 64
        pa, pb = next_ps(K), next_ps(K)
        for hh in range(4):
            ps = pa if hh < 2 else pb
            col = (hh % 2) * 256
            fw.op("pe", lambda e, ps=ps, col=col, hh=hh: e.matmul(
                ps[:, col:col + 256], kdt[p0:p0 + 64, hh * 128:(hh + 1) * 128], vtl[p0:p0 + 64, hh * 256:(hh + 1) * 256],
                start=True, stop=True), reads=[kdt, vtl], writes=[ps])
        return pa, pb

    def psl(pa, pb, hh):
        ps = pa if hh < 2 else pb
        col = (hh % 2) * 256
        return ps, ps[:, col:col + 256]

    def load_p1(g):
        fw.dma("sp", kdf_t[g % 2][:, :], K.KDF[g * 128:(g + 1) * 128, :], writes=[kdf_t[g % 2]])
        fw.dma("sp", kdb_t[g % 2][:, :], K.KDB[g * 128:(g + 1) * 128, :], writes=[kdb_t[g % 2]])
        fw.dma("sp", v_t[g % 2][:, :], K.VT[g * 128:(g + 1) * 128, :], writes=[v_t[g % 2]])

    load_p1(0)
    for g in range(NG):
        kdf, kdb, v = kdf_t[g % 2], kdb_t[g % 2], v_t[g % 2]
        if g + 1 < NG:
            load_p1(g + 1)
        for ch in range(2):
            c = 2 * g + ch
            pa, pb = states(kdf, v, ch)
            for hh in range(4):
                ps, pv = psl(pa, pb, hh)
                fw.op("dve", lambda e, hh=hh, pv=pv, c=c: e.scalar_tensor_tensor(
                    Sf[:, hh * 256:(hh + 1) * 256], Sf[:, hh * 256:(hh + 1) * 256], decf[:, hh, c:c + 1], pv,
                    ALU.mult, ALU.add), reads=[Sf, decf, ps], writes=[Sf])
            pa, pb = states(kdb, v, ch)
            for hh in range(4):
                ps, pv = psl(pa, pb, hh)
                fw.op("dve", lambda e, hh=hh, pv=pv: e.scalar_tensor_tensor(
                    Sb[:, hh * 256:(hh + 1) * 256], pv, P[:, hh:hh + 1], Sb[:, hh * 256:(hh + 1) * 256],
                    ALU.mult, ALU.add), reads=[Sb, P, ps], writes=[Sb])
            fw.op("dve", lambda e, c=c: e.tensor_tensor(P[:, :], P[:, :], decb[:, :, c], ALU.mult),
                  reads=[P, decb], writes=[P])
    fw.dma("sp", K.GST[:, 0:1024], Sf[:, :], reads=[Sf], writes=[K.GST])
    fw.dma("sp", K.GST[:, 1024:2048], Sb[:, :], reads=[Sb], writes=[K.GST])
    fw.collective("AllGather", ALU.bypass, K.groups, K.GST.ap, K.GSG.ap, reads=[K.GST], writes=[K.GSG])
    i_f = sb.alloc([1024], F32, name="i_f")
    i_b = sb.alloc([1024], F32, name="i_b")
    fw.dma("sp", i_f[:, :], K.GSG[0:128, 0:1024], reads=[K.GSG], writes=[i_f])
    fw.dma("sp", i_b[:, :], K.GSG[128:256, 1024:2048], reads=[K.GSG], writes=[i_b])
    fw.op("dve", lambda e: e.tensor_scalar(Sf[:, :], i_f[:, :], K.flg[:, 0:1], None, ALU.mult), reads=[i_f, K.flg], writes=[Sf])
    fw.op("dve", lambda e: e.tensor_scalar(Sb[:, :], i_b[:, :], K.flg[:, 1:2], None, ALU.mult), reads=[i_b, K.flg], writes=[Sb])
    rp_t = [sb.alloc([1024], BF16, name=f"rp{j}") for j in range(2)]
    def load_bw(g):
        fw.dma("sp", kdb_t[g % 2][:, :], K.KDB[g * 128:(g + 1) * 128, :], writes=[kdb_t[g % 2]])
        fw.dma("sp", v_t[g % 2][:, :], K.VT[g * 128:(g + 1) * 128, :], writes=[v_t[g % 2]])

    load_bw(NG - 1)
    for g in reversed(range(NG)):
        kdb, v = kdb_t[g % 2], v_t[g % 2]
        if g - 1 >= 0:
            load_bw(g - 1)
        for ch in (1, 0):
            c = 2 * g + ch
            rp = rp_t[c % 2]
            fw.op("act", lambda e, rp=rp: e.copy(rp[:, :], Sb[:, :]), reads=[Sb], writes=[rp])
            fw.dma("sp", K.RPB[c], rp[:, :], reads=[rp])
            pa, pb = states(kdb, v, ch)
            for hh in range(4):
                ps, pv = psl(pa, pb, hh)
                fw.op("dve", lambda e, hh=hh, pv=pv, c=c: e.scalar_tensor_tensor(
                    Sb[:, hh * 256:(hh + 1) * 256], Sb[:, hh * 256:(hh + 1) * 256], decb[:, hh, c:c + 1], pv,
                    ALU.mult, ALU.add), reads=[Sb, decb, ps], writes=[Sb])
    fw.barrier()
    names = ("KCF", "QCF", "QBF", "KCB", "QCB", "QBB")
    in_t = [{nm: sb.alloc([4, 128], BF16, name=f"i{nm}{j}") for nm in names} for j in range(2)]
    rpb_t = [sb.alloc([2, 1024], BF16, name=f"rpb{j}") for j in range(2)]
    sg_t = [sb.alloc([8, 128], BF16, name=f"sg{j}") for j in range(2)]
    rpf = [sb.alloc([1024], BF16, name=f"rpf{j}") for j in range(2)]
    af = sb.alloc([4, 128], BF16, name="af")
    ab = sb.alloc([4, 128], BF16, name="ab")
    sq = sb.alloc([2, 512], BF16, name="osq")
    rstd = sb.alloc([512], F32, name="orstd")
    tmpo = sb.alloc([512], F32, name="otmp")
    at = [sb.alloc([8, 128], BF16, name=f"oat{j}") for j in range(2)]
    mf = K.cst["mf64"].ap.rearrange("p (o n) -> p o n", o=1).to_broadcast([128, 4, 128])
    mb = K.cst["mb64"].ap.rearrange("p (o n) -> p o n", o=1).to_broadcast([128, 4, 128])
    def load_fw(g):
        j = g % 2
        tsl = slice(g * 128, (g + 1) * 128)
        for nm in names:
            fw.dma("sp", in_t[j][nm][:, :, :], getattr(K, nm).ap.rearrange("(h p) t -> p h t", p=128)[:, :, tsl], writes=[in_t[j][nm]])
        fw.dma("sp", kdf_t[j][:, :], K.KDF[tsl, :], writes=[kdf_t[j]])
        fw.dma("sp", v_t[j][:, :], K.VT[tsl, :], writes=[v_t[j]])
        fw.dma("sp", rpb_t[j][:, :, :], K.RPB.ap[2 * g:2 * g + 2].rearrange("c p n -> p c n"), writes=[rpb_t[j]])
        fw.dma("sp", sg_t[j][:, :, :], K.SGT.ap.rearrange("(c p) t -> p c t", p=128)[:, :, tsl], writes=[sg_t[j]])

    load_fw(0)
    for g in range(NG):
        j = g % 2
        it, kdf, v, rpb, sgl = in_t[j], kdf_t[j], v_t[j], rpb_t[j], sg_t[j]
        tsl = slice(g * 128, (g + 1) * 128)
        if g + 1 < NG:
            load_fw(g + 1)
        psF, psB = next_ps(K), next_ps(K)
        for hh in range(4):
            fw.op("pe", lambda e, hh=hh, it=it, psF=psF: e.matmul(psF[:, hh * 128:(hh + 1) * 128], it["KCF"][:, hh, :], it["QCF"][:, hh, :],
                                                           start=True, stop=True), reads=[it["KCF"], it["QCF"]], writes=[psF])
        for hh in range(4):
            fw.op("pe", lambda e, hh=hh, it=it, psB=psB: e.matmul(psB[:, hh * 128:(hh + 1) * 128], it["KCB"][:, hh, :], it["QCB"][:, hh, :],
                                                           start=True, stop=True), reads=[it["KCB"], it["QCB"]], writes=[psB])
        fw.op("dve", lambda e, psF=psF: e.tensor_tensor(af[:, :, :], psF[:, 0:512].rearrange("p (h n) -> p h n", h=4), mf, ALU.mult),
              reads=[psF, K.cst["mf64"]], writes=[af])
        fw.op("dve", lambda e, psB=psB: e.tensor_tensor(ab[:, :, :], psB[:, 0:512].rearrange("p (h n) -> p h n", h=4), mb, ALU.mult),
              reads=[psB, K.cst["mb64"]], writes=[ab])
        for ch in range(2):
            c = 2 * g + ch
            fw.op("act", lambda e, ch=ch: e.copy(rpf[ch][:, :], Sf[:, :]), reads=[Sf], writes=[rpf[ch]])
            pa, pb = states(kdf, v, ch)
            for hh in range(4):
                ps, pv = psl(pa, pb, hh)
                fw.op("dve", lambda e, hh=hh, pv=pv, c=c: e.scalar_tensor_tensor(
                    Sf[:, hh * 256:(hh + 1) * 256], Sf[:, hh * 256:(hh + 1) * 256], decf[:, hh, c:c + 1], pv,
                    ALU.mult, ALU.add), reads=[Sf, decf, ps], writes=[Sf])
        po = []
        for half in range(2):
            p = next_ps(K)
            po.append(p)
            for hh in range(4):
                cols = hh * 128
                ec = hh * 256 + half * 128
                for ch in range(2):
                    cc = cols + ch * 64
                    r0 = ch * 64
                    fw.op("pe", lambda e: e.matmul(
                        p[:, cc:cc + 64], v[r0:r0 + 64, ec:ec + 128], af[r0:r0 + 64, hh, r0:r0 + 64], start=True, stop=False),
                        reads=[v, af], writes=[p])
                    fw.op("pe", lambda e: e.matmul(
                        p[:, cc:cc + 64], v[r0:r0 + 64, ec:ec + 128], ab[r0:r0 + 64, hh, r0:r0 + 64], start=False, stop=False),
                        reads=[v, ab], writes=[p])
                    fw.op("pe", lambda e: e.matmul(
                        p[:, cc:cc + 64], rpf[ch][:, ec:ec + 128], it["QBF"][:, hh, r0:r0 + 64],
                        start=False, stop=False), reads=[rpf[ch], it["QBF"]], writes=[p])
                    fw.op("pe", lambda e: e.matmul(
                        p[:, cc:cc + 64], rpb[:, ch, ec:ec + 128], it["QBB"][:, hh, r0:r0 + 64],
                        start=False, stop=True), reads=[rpb, it["QBB"]], writes=[p])
        for half in range(2):
            fw.op("act", lambda e, half=half, p=po[half]: e.activation(sq[:, half, :], p[:, 0:512], AF.Square),
                  reads=[po[half]], writes=[sq])
        rms_rstd(K, sq, 2, 512, 256, rstd)
        a = at[j]
        for half in range(2):
            fw.op("dve", lambda e, p=po[half]: e.tensor_tensor(tmpo[:, :], p[:, 0:512], rstd[:, :], ALU.mult),
                  reads=[po[half], rstd], writes=[tmpo])
            fw.op("dve", lambda e, half=half, a=a, sgl=sgl: e.tensor_tensor(
                a[:, half::2, :], tmpo[:, :].rearrange("p (h n) -> p h n", h=4), sgl[:, half::2, :], ALU.mult),
                reads=[tmpo, sgl], writes=[a])
        fw.dma("sp", K.AT.ap.rearrange("(c p) t -> p c t", p=128)[:, :, tsl], a[:, :, :], reads=[a])
    fw.barrier()
    sb.release(m0)


def make_tables(K):
    fw, sb, T, S2 = K.fw, K.sb, K.T, K.S2
    m0 = sb.mark()
    tabc = K.cst["tabc"]
    CH = min(1024, T)
    pi = sb.alloc([CH], I32, name="tpi")
    pf = sb.alloc([CH], F32, name="tpf")
    u = sb.alloc([CH], F32, name="tu")
    kf = sb.alloc([CH], F32, name="tkf")
    TWO_PI = float(2 * np.pi * (1 - 1e-6))
    for dst, pos, n, cc, pc in ((K.TRC, K.pos_loc, T, 0, 2), (K.TRS, K.pos_loc, T, 0, 3),
                                (K.TQ, K.pos_loc, T, 1, 4), (K.TK, K.pos_all, S2, 1, 4)):
        for c0 in range(0, n, CH):
            fw.dma("sp", pi[:, :], pos[:, c0:c0 + CH].partition_broadcast(128), writes=[pi])
            fw.op("dve", lambda e: e.tensor_copy(pf[:, :], pi[:, :]), reads=[pi], writes=[pf])
            fw.op("dve", lambda e: e.tensor_scalar(u[:, :], pf[:, :], tabc[:, cc:cc + 1], tabc[:, pc:pc + 1], ALU.mult, ALU.add),
                  reads=[pf, tabc], writes=[u])
            fw.op("dve", lambda e: e.tensor_copy(pi[:, :], u[:, :]), reads=[u], writes=[pi])
            fw.op("dve", lambda e: e.tensor_copy(kf[:, :], pi[:, :]), reads=[pi], writes=[kf])
            fw.op("dve", lambda e: e.tensor_tensor(u[:, :], u[:, :], kf[:, :], ALU.subtract), reads=[u, kf], writes=[u])
            fw.op("dve", lambda e: e.tensor_scalar(kf[:, :], u[:, :], 0.5, None, ALU.is_gt), reads=[u], writes=[kf])
            fw.op("dve", lambda e: e.tensor_tensor(u[:, :], u[:, :], kf[:, :], ALU.subtract), reads=[u, kf], writes=[u])
            fw.op("act", lambda e: e.activation(u[:, :], u[:, :], AF.Sin, scale=TWO_PI), reads=[u], writes=[u])
            fw.dma("sp", dst[:, c0:c0 + CH], u[:, :], reads=[u])
    fw.barrier()
    sb.release(m0)


def even_layer(K, i):
    sb = K.sb
    mL = sb.mark()
    c = Ctx()
    K.ev = c
    fw = K.fw
    c.ones96 = sb.alloc([128], BF16, name="ones96")
    fw.op("dve", lambda e: e.tensor_copy(c.ones96[:, :], K.cst["ones96"][:, :]), reads=[K.cst["ones96"]], writes=[c.ones96])
    c.sel = sb.alloc([128], BF16, name="selb")
    fw.op("dve", lambda e: e.tensor_copy(c.sel[:, :], K.cst["sel"][:, :]), reads=[K.cst["sel"]], writes=[c.sel])
    c.mtmp = [(sb.alloc([1, 512], BF16, name=f"msq{j}"), sb.alloc([512], F32, name=f"mrstd{j}"),
               sb.alloc([512], F32, name=f"my{j}"), sb.alloc([512], BF16, name=f"my2{j}")) for j in range(4)]
    c.mcnt = 0
    skip = K.dbg.get("skip", ())
    even_e1(K, i)
    if "e2" not in skip:
        even_e2(K, i)
    if "e3" not in skip:
        even_e3(K, i)
    if "e4" not in skip:
        even_e4(K, i)
    sb.release(mL)


def head_gain_col(K, vec96, scale, name):
    fw, sb = K.fw, K.sb
    t = sb.alloc([1], F32, name=name)
    v = vec96.rearrange("(p o) -> p o", o=1)
    fw.dma("sp", t[0:96, :], v[0:96, :], writes=[t])
    fw.dma("sp", t[96:112, :], v[80:96, :], writes=[t])
    fw.dma("sp", t[112:128, :], v[64:80, :], writes=[t])
    if scale != 1.0:
        fw.op("dve", lambda e: e.tensor_scalar(t[:, :], t[:, :], float(scale), None, ALU.mult), reads=[t], writes=[t])
    return t


def mla_post(K, ps, gcol, tab, TT, out):
    fw, c = K.fw, K.ev
    msq, mrstd, my, my2 = c.mtmp[c.mcnt % 4]
    c.mcnt += 1
    fw.op("act", lambda e: e.activation(msq[:, 0, 0:TT], ps[:, 0:TT], AF.Square), reads=[ps], writes=[msq])
    rms_rstd(K, msq, 1, TT, 96, mrstd, ones=c.ones96)
    fw.op("dve", lambda e: e.scalar_tensor_tensor(my[:, 0:TT], ps[:, 0:TT], gcol[:, 0:1], tab[:, 0:TT], ALU.mult, ALU.mult),
          reads=[ps, gcol, tab, msq], writes=[my])
    fw.op("dve", lambda e: e.tensor_tensor(my2[:, 0:TT], my[:, 0:TT], mrstd[:, 0:TT], ALU.mult),
          reads=[my, mrstd], writes=[my2])
    p2 = ps
    fw.op("pe", lambda e: e.matmul(p2[0:96, 0:TT], c.sel[:, 0:96], my2[:, 0:TT], start=True, stop=True),
          reads=[c.sel, my2], writes=[p2])
    fw.op("act", lambda e: e.copy(out[0:96, 0:TT], p2[0:96, 0:TT]), reads=[p2], writes=[out])


def norm_chunks(K, src, dst, sq, rstd, gcol, nch, dim, TT):
    fw = K.fw
    for c_ in range(nch):
        fw.op("act", lambda e: e.activation(sq[:, c_, 0:TT], src[:, c_, 0:TT], AF.Square), reads=[src], writes=[sq])
    rms_rstd(K, sq, nch, TT, dim, rstd)
    for c_ in range(nch):
        fw.op("dve", lambda e: e.scalar_tensor_tensor(dst[:, c_, 0:TT], src[:, c_, 0:TT], gcol[:, c_:c_ + 1], rstd[:, 0:TT],
                                                      ALU.mult, ALU.mult), reads=[src, gcol, rstd], writes=[dst])


def even_e1(K, i):
    fw, sb, T = K.fw, K.sb, K.T
    m0 = sb.mark()
    TT = min(512, T)
    NS = TT // 128
    w = sb.alloc([8, 2720], BF16, name="w_in_even")
    load_w_bf16(K, w, K.w["w_in_even"][i], 8, 2720)
    wrot = sb.alloc([8, 1024], BF16, name="wrot")
    for kc in range(8):
        src = w[:, kc, 672:1696].rearrange("p (h two d) -> p h two d", two=2, d=32)
        dst = wrot[:, kc, :].rearrange("p (h two d) -> p h two d", two=2, d=32)
        fw.op("act", lambda e: e.mul(dst[:, :, 0, :], src[:, :, 1, :], -1.0), reads=[w], writes=[wrot])
        fw.op("dve", lambda e: e.tensor_copy(dst[:, :, 1, :], src[:, :, 0, :]), reads=[w], writes=[wrot])
    wq = sb.alloc([3, 768], BF16, name="wq")
    load_w_bf16(K, wq, K.w["mla_w_uq"][i], 3, 768)
    wqx = sb.alloc([3, 1024], BF16, name="wqx")
    for kc in range(3):
        src = wq[:, kc, :].rearrange("p (h d) -> p h d", d=96)
        dst = wqx[:, kc, :].rearrange("p (h d) -> p h d", d=128)
        fw.op("dve", lambda e: e.tensor_copy(dst[:, :, 0:96], src[:, :, 0:96]), reads=[wq], writes=[wqx])
        fw.op("act", lambda e: e.mul(dst[:, :, 96:112], src[:, :, 80:96], -1.0), reads=[wq], writes=[wqx])
        fw.op("dve", lambda e: e.tensor_copy(dst[:, :, 112:128], src[:, :, 64:80]), reads=[wq], writes=[wqx])
    g1 = sb.alloc([8], F32, name="eg1")
    load_col(K, g1, K.w["mix_norm_even"][i], 8)
    gq3 = sb.alloc([3], F32, name="gq3")
    load_col(K, gq3, K.w["mla_q_norm"][i], 3)
    gkv2 = sb.alloc([2], F32, name="gkv2")
    load_col(K, gkv2, K.w["mla_kv_norm"][i], 2)
    gr4 = sb.alloc([4], F32, name="gr4")
    load_col(K, gr4, K.w["ret_out_norm"][i].rearrange("h e -> (h e)"), 4)
    Gq = head_gain_col(K, K.w["mla_q_head_norm"][i], 96 ** -0.5, "Gq")
    XTv = K.xT.ap.rearrange("(c p) t -> p c t", p=128)
    x_t = [sb.alloc([8, TT], F32, name=f"ex{j}") for j in range(2)]
    h = sb.alloc([8, TT], BF16, name="eh")
    sq = sb.alloc([8, TT], BF16, name="esq")
    rstd = sb.alloc([TT], F32, name="erstd")
    rstd2 = sb.alloc([TT], F32, name="erstd2")
    cq = sb.alloc([3, TT], F32, name="cq")
    cqn = sb.alloc([3, TT], BF16, name="cqn")
    ckn = sb.alloc([2, TT], F32, name="ckn")
    kr = sb.alloc([TT], F32, parts=32, name="kr")
    tq = sb.alloc([TT], F32, name="tq")
    tcs = sb.alloc([TT], F32, name="tcos")
    tsn = sb.alloc([TT], F32, name="tsin")
    qo = [sb.alloc([TT], BF16, name=f"qo{j}") for j in range(2)]
    t1 = sb.alloc([TT], F32, name="t1")
    t2 = sb.alloc([TT], F32, name="t2")
    rq = sb.alloc([4, TT], BF16, name="rq")
    rk = sb.alloc([4, TT], BF16, name="rk")
    rv = [sb.alloc([512], BF16, name=f"rv{j}") for j in range(2)]
    sg = sb.alloc([4, TT], BF16, name="esg")
    tq_t = [tq, sb.alloc([TT], F32, name="tq1")]
    tcs_t = [tcs, sb.alloc([TT], F32, name="tcos1")]
    tsn_t = [tsn, sb.alloc([TT], F32, name="tsin1")]

    def load_e1(ti):
        t0 = ti * TT
        fw.dma("sp", x_t[ti % 2][:, :, :], XTv[:, :, t0:t0 + TT], writes=[x_t[ti % 2]])
        fw.dma("sp", tq_t[ti % 2][:, :], K.TQ[:, t0:t0 + TT], writes=[tq_t[ti % 2]])
        fw.dma("sp", tcs_t[ti % 2][:, :], K.TRC[:, t0:t0 + TT], writes=[tcs_t[ti % 2]])
        fw.dma("sp", tsn_t[ti % 2][:, :], K.TRS[:, t0:t0 + TT], writes=[tsn_t[ti % 2]])

    load_e1(0)
    for ti in range(T // TT):
        t0 = ti * TT
        xt, tq, tcs, tsn = x_t[ti % 2], tq_t[ti % 2], tcs_t[ti % 2], tsn_t[ti % 2]
        if ti + 1 < T // TT:
            load_e1(ti + 1)
        norm_tile(K, xt, h, sq, rstd, g1, TT)
        for c_ in range(3):
            ps = proj_fm(K, w, c_ * 128, 128, h, TT)
            fw.op("act", lambda e: e.copy(cq[:, c_, :], ps[:, 0:TT]), reads=[ps], writes=[cq])
        norm_chunks(K, cq, cqn, sq, rstd2, gq3, 3, 384, TT)
        fw.dma("sp", K.CQN.ap.rearrange("(c p) t -> p c t", p=128)[:, :, t0:t0 + TT], cqn[:, :, :], reads=[cqn])
        for c_ in range(2):
            ps = proj_fm(K, w, 384 + c_ * 128, 128, h, TT)
            fw.op("act", lambda e: e.copy(cq[:, c_, :], ps[:, 0:TT]), reads=[ps], writes=[cq])
        norm_chunks(K, cq, ckn, sq, rstd2, gkv2, 2, 256, TT)
        fw.dma("sp", K.LAT[ti].ap[0:256, :].rearrange("(c p) t -> p c t", p=128), ckn[:, :, :], reads=[ckn])
        ps = proj_fm(K, w, 640, 32, h, TT)
        fw.op("act", lambda e: e.copy(kr[:, :], ps[0:32, 0:TT]), reads=[ps], writes=[kr])
        fw.dma("sp", K.LAT[ti][256:288, :], kr[:, :], reads=[kr])
        for hh in range(8):
            ps = next_ps(K)
            for kc in range(3):
                fw.op("pe", lambda e: e.matmul(ps[:, 0:TT], wqx[:, kc, hh * 128:(hh + 1) * 128], cqn[:, kc, :],
                                               start=(kc == 0), stop=(kc == 2)), reads=[wqx, cqn], writes=[ps])
            o = qo[hh % 2]
            mla_post(K, ps, Gq, tq, TT, o)
            fw.dma("sp", K.QT.ap[hh, :, t0:t0 + TT], o[0:96, :], reads=[o])
        for which, c0, r0, dst, scl in (("q", 672, 0, rq, 1.0), ("k", 1184, 512, rk, 0.125)):
            for c_ in range(4):
                pa = proj_fm(K, w, c0 + c_ * 128, 128, h, TT)
                pr = proj_fm(K, wrot, r0 + c_ * 128, 128, h, TT)
                fw.op("dve", lambda e: e.scalar_tensor_tensor(t1[:, :], pa[:, 0:TT], float(scl), tcs[:, :], ALU.mult, ALU.mult),
                      reads=[pa, tcs], writes=[t1])
                fw.op("dve", lambda e: e.scalar_tensor_tensor(t2[:, :], pr[:, 0:TT], float(scl), tsn[:, :], ALU.mult, ALU.mult),
                      reads=[pr, tsn], writes=[t2])
                fw.op("dve", lambda e: e.tensor_tensor(dst[:, c_, :], t1[:, :], t2[:, :], ALU.add), reads=[t1, t2], writes=[dst])
        fw.dma("sp", K.RQ.ap.rearrange("(c p) t -> p c t", p=128)[:, :, t0:t0 + TT], rq[:, :, :], reads=[rq])
        fw.dma("sp", K.RK.ap.rearrange("(c p) t -> p c t", p=128)[:, :, t0:t0 + TT], rk[:, :, :], reads=[rk])
        for s_ in range(NS):
            ps = proj_tm(K, w, 1696, 512, h, s_ * 128)
            r = rv[s_ % 2]
            fw.op("act", lambda e: e.copy(r[:, :], ps[:, 0:512]), reads=[ps], writes=[r])
            fw.dma("sp", K.RV[t0 + s_ * 128:t0 + (s_ + 1) * 128, :], r[:, :], reads=[r])
        for c_ in range(4):
            ps = proj_fm(K, w, 2208 + c_ * 128, 128, h, TT)
            fw.op("act", lambda e: e.activation(t1[:, :], ps[:, 0:TT], AF.Silu), reads=[ps], writes=[t1])
            fw.op("dve", lambda e: e.tensor_scalar(sg[:, c_, :], t1[:, :], gr4[:, c_:c_ + 1], None, ALU.mult),
                  reads=[t1, gr4], writes=[sg])
        fw.dma("sp", K.SGT.ap[0:512, :].rearrange("(c p) t -> p c t", p=128)[:, :, t0:t0 + TT], sg[:, :, :], reads=[sg])
    fw.barrier()
    for j in range(len(K.LAT)):
        fw.collective("AllGather", ALU.bypass, K.groups, K.LAT[j].ap, K.LATG[j].ap)
    fw.barrier()
    sb.release(m0)


def even_e2(K, i):
    fw, sb, T, S2 = K.fw, K.sb, K.T, K.S2
    c = K.ev
    m0 = sb.mark()
    TT = min(512, T)
    wkv = sb.alloc([2, 1024], BF16, name="wkv")
    load_w_bf16(K, wkv, K.w["mla_w_ukv"][i], 2, 1024)
    wkx = sb.alloc([2, 1024], BF16, name="wkx")
    wv = sb.alloc([2, 512], BF16, name="wv")
    fw.op("dve", lambda e: e.memset(wkx[:, :, :], 0.0), writes=[wkx])
    for kc in range(2):
        src = wkv[:, kc, :].rearrange("p (h d) -> p h d", d=128)
        fw.op("dve", lambda e: e.tensor_copy(wkx[:, kc, :].rearrange("p (h d) -> p h d", d=128)[:, :, 0:64], src[:, :, 0:64]),
              reads=[wkv], writes=[wkx])
        fw.op("act", lambda e: e.copy(wv[:, kc, :].rearrange("p (h d) -> p h d", d=64), src[:, :, 64:128]),
              reads=[wkv], writes=[wv])
    rmat = sb.alloc([128], BF16, parts=32, name="rmat")
    fw.op("dve", lambda e: e.tensor_copy(rmat[:, :], K.cst["ropemat"][0:32, :]), reads=[K.cst["ropemat"]], writes=[rmat])
    Gk = head_gain_col(K, K.w["mla_k_head_norm"][i], 1.0, "Gk")
    K.ps_mod = 8
    lat = [sb.alloc([2, TT], F32, name=f"lat{j}") for j in range(2)]
    krf = [sb.alloc([TT], F32, parts=32, name=f"krf{j}") for j in range(2)]
    ckb = sb.alloc([2, TT], BF16, name="ckb")
    krb = sb.alloc([TT], BF16, parts=32, name="krb")
    tk = [sb.alloc([TT], F32, name=f"tk{j}") for j in range(2)]
    ko = [sb.alloc([TT], BF16, name=f"ko{j}") for j in range(4)]
    vx = [sb.alloc([8, 65], BF16, name=f"vx{j}") for j in range(2)]
    for j in range(2):
        fw.op("dve", lambda e: e.memset(vx[j][:, :, :], 1.0), writes=[vx[j]])
    def load_e2(kt):
        k0 = kt * TT
        rr = k0 // T
        lj = (k0 - rr * T) // TT
        fw.dma("sp", lat[kt % 2][:, :, :], K.LATG[lj].ap[rr * 288:rr * 288 + 256, :].rearrange("(c p) t -> p c t", p=128), writes=[lat[kt % 2]])
        fw.dma("sp", krf[kt % 2][:, :], K.LATG[lj][rr * 288 + 256:rr * 288 + 288, :], writes=[krf[kt % 2]])
        fw.dma("sp", tk[kt % 2][:, :], K.TK[:, k0:k0 + TT], writes=[tk[kt % 2]])

    load_e2(0)
    for kt in range(S2 // TT):
        k0 = kt * TT
        la, kf_, tkt = lat[kt % 2], krf[kt % 2], tk[kt % 2]
        if kt + 1 < S2 // TT:
            load_e2(kt + 1)
        fw.op("act", lambda e: e.copy(ckb[:, :, :], la[:, :, :]), reads=[la], writes=[ckb])
        fw.op("dve", lambda e: e.tensor_copy(krb[:, :], kf_[:, :]), reads=[kf_], writes=[krb])
        for hh in range(8):
            ps = next_ps(K)
            for kc in range(2):
                fw.op("pe", lambda e: e.matmul(ps[:, 0:TT], wkx[:, kc, hh * 128:(hh + 1) * 128], ckb[:, kc, :],
                                               start=(kc == 0), stop=False), reads=[wkx, ckb], writes=[ps])
            fw.op("pe", lambda e: e.matmul(ps[:, 0:TT], rmat[0:32, :], krb[0:32, :], start=False, stop=True),
                  reads=[rmat, krb], writes=[ps])
            o = ko[hh % 4]
            mla_post(K, ps, Gk, tkt, TT, o)
            fw.dma("sp", K.KT.ap[hh, :, k0:k0 + TT], o[0:96, :], reads=[o])
        for s_ in range(TT // 128):
            ps = next_ps(K)
            for kc in range(2):
                fw.op("pe", lambda e: e.matmul(ps[:, 0:512], ckb[:, kc, s_ * 128:(s_ + 1) * 128], wv[:, kc, :],
                                               start=(kc == 0), stop=(kc == 1)), reads=[ckb, wv], writes=[ps])
            v = vx[s_ % 2]
            fw.op("act", lambda e: e.copy(v[:, :, 0:64], ps[:, 0:512].rearrange("p (h d) -> p h d", d=64)), reads=[ps], writes=[v])
            fw.dma("sp", K.VX[k0 + s_ * 128:k0 + (s_ + 1) * 128, :], v[:, :, :].rearrange("p h d -> p (h d)"), reads=[v])
    K.ps_mod = 6
    fw.barrier()
    sb.release(m0)


EXP_SHIFT = -8.0


def even_e3(K, i):
    fw, sb, T, S2 = K.fw, K.sb, K.T, K.S2
    m0 = sb.mark()
    TQ = min(512, T)
    NKT = S2 // 128
    NP = NKT // 2
    NQT = T // TQ
    kt_t = [sb.alloc([S2], BF16, name=f"akt{j}") for j in range(2)]
    vx_t = [sb.alloc([NKT, 65], BF16, name=f"avx{j}") for j in range(2)]
    q_t = [sb.alloc([T], BF16, name=f"aq{j}") for j in range(2)]
    pt = [sb.alloc([2, TQ], BF16, name=f"ap{j}") for j in range(3)]
    o65 = sb.alloc([512], F32, name="o65")
    rden = sb.alloc([512], F32, name="rden")
    ao = [sb.alloc([512], BF16, name=f"ao{j}") for j in range(2)]
    shift = sb.alloc([1], F32, name="shift")
    fw.op("dve", lambda e: e.memset(shift[:, :], EXP_SHIFT), writes=[shift])
    sel65 = K.cst["sel65"]
    VXv = K.VX.ap.rearrange("(kt p) c -> p kt c", p=128)
    spair = [Tile(K.psum_full[:, b * 1024:(b + 1) * 1024].rearrange("p (b n) -> p b n", b=2), f"spair{b}") for b in range(3)]

    def load_head(hh):
        fw.dma("sp", kt_t[hh % 2][0:96, :], K.KT.ap[hh], writes=[kt_t[hh % 2]])
        fw.dma("sp", vx_t[hh % 2][:, :, :], VXv[:, :, hh * 65:(hh + 1) * 65], writes=[vx_t[hh % 2]])
        fw.dma("sp", q_t[hh % 2][0:96, :], K.QT.ap[hh], writes=[q_t[hh % 2]])

    items = [(hh, qt, j) for hh in range(8) for qt in range(NQT) for j in range(NP)]
    load_head(0)
    load_head(1)
    for idx in range(len(items) + 2):
        if idx < len(items):
            hh, qt, j = items[idx]
            kt_, q_ = kt_t[hh % 2], q_t[hh % 2]
            sp = spair[idx % 3]
            for u in range(2):
                kt = 2 * j + u
                fw.op("pe", lambda e: e.matmul(sp[:, u, 0:TQ], kt_[0:96, kt * 128:(kt + 1) * 128], q_[0:96, qt * TQ:(qt + 1) * TQ],
                                               start=True, stop=True), reads=[kt_, q_], writes=[sp])
        if idx >= 2:
            k = idx - 2
            hh, qt, j = items[k]
            g = hh * NQT + qt
            vx_ = vx_t[hh % 2]
            sp = spair[k % 3]
            p = pt[k % 3]
            po = K.ps[6 + (g % 2)]
            if K.dbg.get("exp2", True):
                fw.op("act", lambda e: e.activation(p[:, :, :], sp[:, :, 0:TQ], AF.Exp, bias=shift[:, 0:1], scale=1.0),
                      reads=[sp, shift], writes=[p])
            else:
                for u in range(2):
                    fw.op("act", lambda e: e.activation(p[:, u, :], sp[:, u, 0:TQ], AF.Exp, bias=shift[:, 0:1], scale=1.0),
                          reads=[sp, shift], writes=[p])
            for u in range(2):
                kt = 2 * j + u
                fw.op("pe", lambda e: e.matmul(po[0:65, 0:TQ], vx_[:, kt, :], p[:, u, :], start=(kt == 0), stop=(kt == NKT - 1)),
                      reads=[vx_, p], writes=[po])
            if j == NP - 1:
                fw.op("act", lambda e: e.copy(o65[0:65, 0:TQ], po[0:65, 0:TQ]), reads=[po], writes=[o65])
                fw.op("pe", lambda e: e.matmul(po[0:64, 0:TQ], sel65[0:65, 0:64], o65[0:65, 0:TQ], start=True, stop=True),
                      reads=[sel65, o65], writes=[po])
                fw.op("dve", lambda e: e.reciprocal(rden[0:64, 0:TQ], po[0:64, 0:TQ]), reads=[po], writes=[rden])
                a = ao[g % 2]
                fw.op("dve", lambda e: e.tensor_tensor(a[0:64, 0:TQ], o65[0:64, 0:TQ], rden[0:64, 0:TQ], ALU.mult),
                      reads=[o65, rden], writes=[a])
                fw.dma("sp", K.AT[hh * 64:(hh + 1) * 64, qt * TQ:(qt + 1) * TQ], a[0:64, 0:TQ], reads=[a])
                if qt == NQT - 1 and hh + 2 < 8:
                    load_head(hh + 2)
    fw.barrier()
    sb.release(m0)


def even_e4(K, i):
    fw, sb, T = K.fw, K.sb, K.T
    m0 = sb.mark()
    NCH = T // 128
    tabc = K.cst["tabc"]
    th = sb.alloc([16], F32, name="rth")
    fw.dma("sp", th[:, 0:8], K.w["ret_theta_fwd"][i:i + 1, :].partition_broadcast(128), writes=[th])
    fw.dma("sp", th[:, 8:16], K.w["ret_theta_bwd"][i:i + 1, :].partition_broadcast(128), writes=[th])
    lg = sb.alloc([16], F32, name="rlg")
    fw.op("act", lambda e: e.activation(lg[:, :], th[:, :], AF.Exp, scale=-float(np.log(2.0))), reads=[th], writes=[lg])
    fw.op("act", lambda e: e.activation(lg[:, :], lg[:, :], AF.Ln, scale=-1.0, bias=1.0), reads=[lg], writes=[lg])
    mask = sb.alloc([8, 128], BF16, name="rmask")
    tm1 = sb.alloc([128], F32, name="rtm1")
    tm2 = sb.alloc([128], F32, name="rtm2")
    zf = sb.alloc([8], F32, name="rzf")
    zb = sb.alloc([8], F32, name="rzb")
    xf = sb.alloc([8, 128], F32, name="rxf")
    xb = sb.alloc([8, 128], F32, name="rxb")
    dec = sb.alloc([2, 512], F32, name="rdec")
    cs = K.cst
    for hh in range(8):
        lf, lb = lg[:, hh:hh + 1], lg[:, 8 + hh:9 + hh]
        fw.op("act", lambda e: e.activation(tm1[:, :], cs["relf"][:, :], AF.Exp, scale=lf), reads=[cs["relf"], lg], writes=[tm1])
        fw.op("dve", lambda e: e.tensor_tensor(tm1[:, :], tm1[:, :], cs["mf128"][:, :], ALU.mult), reads=[tm1, cs["mf128"]], writes=[tm1])
        fw.op("act", lambda e: e.activation(tm2[:, :], cs["relb"][:, :], AF.Exp, scale=lb), reads=[cs["relb"], lg], writes=[tm2])
        fw.op("dve", lambda e: e.tensor_tensor(tm2[:, :], tm2[:, :], cs["mb128"][:, :], ALU.mult), reads=[tm2, cs["mb128"]], writes=[tm2])
        fw.op("dve", lambda e: e.tensor_tensor(mask[:, hh, :], tm1[:, :], tm2[:, :], ALU.add), reads=[tm1, tm2], writes=[mask])
        fw.op("act", lambda e: e.activation(zf[:, hh:hh + 1], tabc[:, 6:7], AF.Exp, scale=lf), reads=[tabc, lg], writes=[zf])
        fw.op("act", lambda e: e.activation(zb[:, hh:hh + 1], tabc[:, 5:6], AF.Exp, scale=lb), reads=[tabc, lg], writes=[zb])
        fw.op("act", lambda e: e.activation(xf[:, hh, :], cs["iota1"][:, :], AF.Exp, scale=lf), reads=[cs["iota1"], lg], writes=[xf])
        fw.op("act", lambda e: e.activation(xb[:, hh, :], cs["iotar"][:, :], AF.Exp, scale=lb), reads=[cs["iotar"], lg], writes=[xb])
        fw.op("act", lambda e: e.activation(dec[:, 0, hh * 64:(hh + 1) * 64], cs["c128"][:, 0:64], AF.Exp, scale=lf), reads=[cs["c128"], lg], writes=[dec])
        fw.op("act", lambda e: e.activation(dec[:, 1, hh * 64:(hh + 1) * 64], cs["c128"][:, 0:64], AF.Exp, scale=lb), reads=[cs["c128"], lg], writes=[dec])
    ident = sb.alloc([128], BF16, name="rident")
    fw.op("dve", lambda e: e.tensor_copy(ident[:, :], cs["ident"][:, :]), reads=[cs["ident"]], writes=[ident])
    ss = [sb.alloc([NCH, 512], BF16, name=f"rss{d}") for d in range(2)]
    rk_t = [sb.alloc([T], BF16, name=f"rrk{j}") for j in range(2)]
    rq_t = [sb.alloc([T], BF16, name=f"rrq{j}") for j in range(2)]
    rv_t = [sb.alloc([NCH, 64], BF16, name=f"rrv{j}") for j in range(2)]
    kz = [sb.alloc([NCH, 64], BF16, name=f"rkz{d}") for d in range(2)]
    RVv = K.RV.ap.rearrange("(c p) n -> p c n", p=128)
    for hh in range(8):
        rk_h, rv_h = rk_t[hh % 2], rv_t[hh % 2]
        fw.dma("sp", rk_h[0:64, :], K.RK[hh * 64:(hh + 1) * 64, :], writes=[rk_h])
        fw.dma("sp", rv_h[:, :, :], RVv[:, :, hh * 64:(hh + 1) * 64], writes=[rv_h])
        for c0 in range(0, NCH, 8):
            nb = min(8, NCH - c0)
            ps = next_ps(K)
            psb = ps.ap.bitcast(BF16)
            for c_ in range(nb):
                fw.op("pe", lambda e: e.transpose(psb[:, c_ * 64:(c_ + 1) * 64], rk_h[0:64, (c0 + c_) * 128:(c0 + c_ + 1) * 128],
                                                  ident[0:64, 0:64]), reads=[rk_h, ident], writes=[ps])
            pv = psb[:, 0:nb * 64].rearrange("p (c d) -> p c d", d=64)
            fw.op("dve", lambda e: e.tensor_scalar(kz[0][:, c0:c0 + nb, :], pv, zf[:, hh:hh + 1], None, ALU.mult),
                  reads=[ps, zf], writes=[kz[0]])
            fw.op("dve", lambda e: e.tensor_scalar(kz[1][:, c0:c0 + nb, :], pv, zb[:, hh:hh + 1], None, ALU.mult),
                  reads=[ps, zb], writes=[kz[1]])
        for c0 in range(0, NCH, 8):
            nb = min(8, NCH - c0)
            for d in range(2):
                ps = next_ps(K)
                for c_ in range(nb):
                    fw.op("pe", lambda e: e.matmul(ps[0:64, c_ * 64:(c_ + 1) * 64], kz[d][:, c0 + c_, :], rv_h[:, c0 + c_, :],
                                                   start=True, stop=True), reads=[kz[d], rv_h], writes=[ps])
                fw.op("act", lambda e: e.copy(ss[d][0:64, c0:c0 + nb, hh * 64:(hh + 1) * 64],
                                              ps[0:64, 0:nb * 64].rearrange("p (c d) -> p c d", d=64)), reads=[ps], writes=[ss[d]])
    R = [sb.alloc([512], F32, name=f"rR{d}") for d in range(2)]
    order = [list(range(NCH)), list(reversed(range(NCH)))]
    for d in range(2):
        fw.op("dve", lambda e: e.memset(R[d][:, :], 0.0), writes=[R[d]])
    for st_ in range(NCH):
        for d in range(2):
            c_ = order[d][st_]
            fw.op("dve", lambda e: e.tensor_tensor(R[d][0:64, :], R[d][0:64, :], dec[0:64, d, :], ALU.mult), reads=[R[d], dec], writes=[R[d]])
            fw.op("dve", lambda e: e.tensor_tensor(R[d][0:64, :], R[d][0:64, :], ss[d][0:64, c_, :], ALU.add), reads=[R[d], ss[d]], writes=[R[d]])
    for d in range(2):
        fw.dma("sp", K.RST[:, d * 512:(d + 1) * 512], R[d][0:64, :], reads=[R[d]], writes=[K.RST])
    fw.collective("AllGather", ALU.bypass, K.groups, K.RST.ap, K.RSG.ap, reads=[K.RST], writes=[K.RSG])
    ini = [sb.alloc([512], F32, name=f"rini{d}") for d in range(2)]
    fw.dma("sp", ini[0][0:64, :], K.RSG[0:64, 0:512], reads=[K.RSG], writes=[ini[0]])
    fw.dma("sp", ini[1][0:64, :], K.RSG[64:128, 512:1024], reads=[K.RSG], writes=[ini[1]])
    rpt = [sb.alloc([512], BF16, name=f"rrpt{d}") for d in range(2)]
    for d in range(2):
        fw.op("dve", lambda e: e.tensor_scalar(R[d][0:64, :], ini[d][0:64, :], K.flg[0:64, d:d + 1], None, ALU.mult),
              reads=[ini[d], K.flg], writes=[R[d]])
    for st_ in range(NCH):
        for d in range(2):
            c_ = order[d][st_]
            fw.op("act", lambda e: e.copy(rpt[d][0:64, :], R[d][0:64, :]), reads=[R[d]], writes=[rpt[d]])
            fw.op("dve", lambda e: e.tensor_tensor(R[d][0:64, :], R[d][0:64, :], dec[0:64, d, :], ALU.mult), reads=[R[d], dec], writes=[R[d]])
            fw.op("dve", lambda e: e.tensor_tensor(R[d][0:64, :], R[d][0:64, :], ss[d][0:64, c_, :], ALU.add), reads=[R[d], ss[d]], writes=[R[d]])
            fw.op("act", lambda e: e.copy(ss[d][0:64, c_, :], rpt[d][0:64, :]), reads=[rpt[d]], writes=[ss[d]])
    qx = [sb.alloc([T], BF16, name=f"rqx{d}") for d in range(2)]
    at = [sb.alloc([128], BF16, name=f"rat{j}") for j in range(3)]
    sq = sb.alloc([1, 512], BF16, name="rsq")
    rstd = sb.alloc([512], F32, name="rrstd")
    tmo = sb.alloc([512], F32, name="rtmo")
    sgt = [sb.alloc([512], BF16, name=f"rsg{j}") for j in range(2)]
    oo = [sb.alloc([512], BF16, name=f"roo{j}") for j in range(2)]
    acnt = 0
    GC = min(4, NCH)
    for hh in range(8):
        rk_h, rq_h, rv_h = rk_t[hh % 2], rq_t[hh % 2], rv_t[hh % 2]
        fw.dma("sp", rk_h[0:64, :], K.RK[hh * 64:(hh + 1) * 64, :], writes=[rk_h])
        fw.dma("sp", rq_h[0:64, :], K.RQ[hh * 64:(hh + 1) * 64, :], writes=[rq_h])
        fw.dma("sp", rv_h[:, :, :], RVv[:, :, hh * 64:(hh + 1) * 64], writes=[rv_h])
        for d, xt_ in ((0, xf), (1, xb)):
            fw.op("dve", lambda e: e.tensor_tensor(
                qx[d][0:64, :].rearrange("p (c n) -> p c n", n=128), rq_h[0:64, :].rearrange("p (c n) -> p c n", n=128),
                xt_[0:64, hh, :].rearrange("p (o n) -> p o n", o=1).to_broadcast([64, NCH, 128]), ALU.mult),
                reads=[rq_h, xt_], writes=[qx[d]])
        def s_stage(c_):
            csl = slice(c_ * 128, (c_ + 1) * 128)
            ps = next_ps(K)
            fw.op("pe", lambda e: e.matmul(ps[:, 0:128], rk_h[0:64, csl], rq_h[0:64, csl], start=True, stop=True),
                  reads=[rk_h, rq_h], writes=[ps])
            a = at[c_ % 3]
            fw.op("dve", lambda e: e.tensor_tensor(a[:, :], ps[:, 0:128], mask[:, hh, :], ALU.mult), reads=[ps, mask], writes=[a])

        s_stage(0)
        if NCH > 1:
            s_stage(1)
        for g0 in range(0, NCH, GC):
            gi = g0 // GC
            po = K.ps[6 + (gi % 2)]
            W = GC * 128
            sgl = sgt[gi % 2]
            fw.dma("sp", sgl[0:64, 0:W], K.SGT[hh * 64:(hh + 1) * 64, g0 * 128:g0 * 128 + W], writes=[sgl])
            for cc in range(GC):
                c_ = g0 + cc
                csl = slice(c_ * 128, (c_ + 1) * 128)
                if c_ + 2 < NCH:
                    s_stage(c_ + 2)
                a = at[c_ % 3]
                osl = slice(cc * 128, (cc + 1) * 128)
                fw.op("pe", lambda e: e.matmul(po[0:64, osl], rv_h[:, c_, :], a[:, :], start=True, stop=False),
                      reads=[rv_h, a], writes=[po])
                fw.op("pe", lambda e: e.matmul(po[0:64, osl], ss[0][0:64, c_, hh * 64:(hh + 1) * 64], qx[0][0:64, csl],
                                               start=False, stop=False), reads=[ss[0], qx[0]], writes=[po])
                fw.op("pe", lambda e: e.matmul(po[0:64, osl], ss[1][0:64, c_, hh * 64:(hh + 1) * 64], qx[1][0:64, csl],
                                               start=False, stop=True), reads=[ss[1], qx[1]], writes=[po])
            fw.op("act", lambda e: e.activation(sq[0:64, 0, 0:W], po[0:64, 0:W], AF.Square), reads=[po], writes=[sq])
            rms_rstd(K, sq, 1, W, 64, rstd, P=64)
            fw.op("dve", lambda e: e.tensor_tensor(tmo[0:64, 0:W], po[0:64, 0:W], rstd[0:64, 0:W], ALU.mult), reads=[po, rstd], writes=[tmo])
            o = oo[gi % 2]
            fw.op("dve", lambda e: e.tensor_tensor(o[0:64, 0:W], tmo[0:64, 0:W], sgl[0:64, 0:W], ALU.mult), reads=[tmo, sgl], writes=[o])
            fw.dma("sp", K.AT[512 + hh * 64:512 + (hh + 1) * 64, g0 * 128:g0 * 128 + W], o[0:64, 0:W], reads=[o])
    fw.barrier()
    sb.release(m0)


_WEIGHT_NAMES = ("ffn_norm", "ffn_w_up", "ffn_conv_w", "ffn_conv_b", "ffn_w_down", "w_out_even", "w_out_odd",
                 "mix_norm_odd", "w_in_odd", "gla_w_gate_fwd", "gla_b_gate_fwd", "gla_w_gate_bwd", "gla_b_gate_bwd",
                 "gla_out_norm", "mix_norm_even", "w_in_even", "mla_q_norm", "mla_kv_norm", "mla_w_uq", "mla_w_ukv",
                 "mla_q_head_norm", "mla_k_head_norm", "ret_theta_fwd", "ret_theta_bwd", "ret_out_norm")


def kernel(**inputs):
    x = np.asarray(inputs["x"], dtype=np.float32)
    pos = np.asarray(inputs["positions"], dtype=np.int32)
    B, S, _ = x.shape
    T = S // 2
    ncores = 2 * B
    groups = [[2 * b, 2 * b + 1] for b in range(B)]
    nc, K = build_program(T, groups, n_layers=4)
    wts = {nm: np.ascontiguousarray(np.asarray(inputs[nm], dtype=np.float32)) for nm in _WEIGHT_NAMES}
    in_maps = []
    for core in range(ncores):
        b, r = core // 2, core % 2
        m = {"xT": np.ascontiguousarray(x[b, r * T:(r + 1) * T, :].T),
             "pos_loc": np.ascontiguousarray(pos[b, r * T:(r + 1) * T][None, :]),
             "pos_all": np.ascontiguousarray(pos[b][None, :]),
             "flags": np.tile(np.array([[r, 1 - r, 0, 0]], np.float32), (128, 1)),
             "consts": CONST_ARR}
        m.update(wts)
        in_maps.append(m)
    res = run_bass_kernel_spmd(nc, in_maps, core_ids=list(range(ncores)))
    out = np.empty((B, S, D), np.float32)
    for core in range(ncores):
        b, r = core // 2, core % 2
        out[b, r * T:(r + 1) * T, :] = np.asarray(res.results[core]["yT"]).T
    return out
```
